# Optimizing a Trainium2 kernel written in Bass

```python
import math
import jax, jax.numpy as jnp
from jax import lax
import numpy as np

D_MODEL = 1024
BATCH = 8
SEQ = 2048
DEPTH = 2
DEC_BATCH = 128
DEC_SEQ = 1
PAST_LEN = 2048
PAGE_SIZE = 128

HEAD_DIM = 64
MIX_WIDTH = D_MODEL
A_HEADS = (MIX_WIDTH // 2) // HEAD_DIM
A_WIDTH = A_HEADS * HEAD_DIM
B_WIDTH = MIX_WIDTH - A_WIDTH
B_GROUPS = 4
B_GROUP_DIM = B_WIDTH // B_GROUPS
CHUNK = 128
MOBA_BLOCK = 256
MOBA_TOPK = 3
ROPE_THETA = 500000.0
ROPE_DIMS = HEAD_DIM // 4
C_HEADS = (MIX_WIDTH // 2) // HEAD_DIM
C_WIDTH = C_HEADS * HEAD_DIM
D_WIDTH = MIX_WIDTH - C_WIDTH
POOL_WINDOWS = (2, 4, 8, 16)
POOL_GROUPS = len(POOL_WINDOWS)
POOL_GROUP_DIM = D_WIDTH // POOL_GROUPS
POOL_STATE = max(POOL_WINDOWS) - 1
Q_BLOCK = 128
D_FF = ((8 * D_MODEL // 3 + 255) // 256) * 256
CONV_W = 3
EPS = 1e-6

kernel_name = "moba_gmlp_stickbreaking_pool_convffn_step"


def rmsnorm(x, g):
    xf = x.astype(jnp.float32)
    y = xf * lax.rsqrt(jnp.mean(xf * xf, axis=-1, keepdims=True) + EPS)
    return (y * g).astype(x.dtype)


def layer_norm(x, g):
    xf = x.astype(jnp.float32)
    mu = jnp.mean(xf, axis=-1, keepdims=True)
    xc = xf - mu
    y = xc * lax.rsqrt(jnp.mean(xc * xc, axis=-1, keepdims=True) + EPS)
    return (y * g).astype(x.dtype)


def rope_partial(x, pos):
    half = ROPE_DIMS // 2
    inv_freq = ROPE_THETA ** (-(jnp.arange(half, dtype=jnp.float32) * 2.0 / ROPE_DIMS))
    ang = pos.astype(jnp.float32)[:, None] * inv_freq[None, :]
    cos = jnp.cos(ang)[None, :, None, :]
    sin = jnp.sin(ang)[None, :, None, :]
    xf = x.astype(jnp.float32)
    x1, x2, rest = xf[..., :half], xf[..., half:ROPE_DIMS], xf[..., ROPE_DIMS:]
    out = jnp.concatenate([x1 * cos - x2 * sin, x2 * cos + x1 * sin, rest], axis=-1)
    return out.astype(x.dtype)


def gather_pages(cache, page_table):
    g = cache[page_table]
    db, npg, pg = g.shape[0], g.shape[1], g.shape[2]
    return g.reshape(db, npg * pg, *g.shape[3:])


def moba_attention(q, k, v, q_pos):
    B, Tq, H, dh = q.shape
    L = k.shape[1]
    nb = -(-L // MOBA_BLOCK)
    pad = nb * MOBA_BLOCK - L
    kb = jnp.pad(k, ((0, 0), (0, pad), (0, 0), (0, 0))).reshape(B, nb, MOBA_BLOCK, H, dh)
    vb = jnp.pad(v, ((0, 0), (0, pad), (0, 0), (0, 0))).reshape(B, nb, MOBA_BLOCK, H, dh)
    kmean = jnp.mean(kb.astype(jnp.float32), axis=2)
    topk = min(MOBA_TOPK, nb)
    qb = Q_BLOCK if Tq % Q_BLOCK == 0 else Tq
    nqb = Tq // qb
    qs = q.reshape(B, nqb, qb, H, dh).transpose(0, 1, 3, 2, 4).reshape(B * nqb, H, qb, dh)
    ps = q_pos.reshape(nqb, qb)
    bidx = jnp.repeat(jnp.arange(B, dtype=jnp.int32), nqb)
    jidx = jnp.tile(jnp.arange(nqb, dtype=jnp.int32), B)
    scale = HEAD_DIM ** -0.5
    blk_off = jnp.arange(MOBA_BLOCK, dtype=jnp.int32)

    def one(args):
        qh, bi, ji = args
        pos = ps[ji]
        kbh = kb[bi].transpose(2, 0, 1, 3)
        vbh = vb[bi].transpose(2, 0, 1, 3)
        own = pos // MOBA_BLOCK
        gate = jnp.einsum('hqd,nhd->hqn', qh.astype(jnp.float32), kmean[bi])
        past = jnp.arange(nb)[None, None, :] < own[None, :, None]
        gate = jnp.where(past, gate, -jnp.inf)
        _, top = lax.top_k(gate, topk)
        top_ok = jnp.arange(topk)[None, None, :] < own[None, :, None]
        sel = jnp.concatenate([top, jnp.broadcast_to(own[None, :, None], (H, qb, 1))], axis=-1)
        ok = jnp.concatenate([jnp.broadcast_to(top_ok, (H, qb, topk)), jnp.ones((H, qb, 1), bool)], axis=-1)
        ks = jax.vmap(lambda kh, ih: kh[ih])(kbh, sel)
        vs = jax.vmap(lambda vh, ih: vh[ih])(vbh, sel)
        kpos = sel[..., None] * MOBA_BLOCK + blk_off
        mask = ok[..., None] & (kpos <= pos[None, :, None, None])
        s = jnp.einsum('hqd,hqjsd->hqjs', qh, ks).astype(jnp.float32) * scale
        s = jnp.where(mask, s, -jnp.inf)
        p = jax.nn.softmax(s.reshape(H, qb, -1), axis=-1).reshape(s.shape)
        return jnp.einsum('hqjs,hqjsd->qhd', p.astype(vs.dtype), vs)

    out = lax.map(one, (qs, bidx, jidx))
    return out.reshape(B, Tq, H, dh)


def stick_breaking_attention(q, k, v, q_pos):
    B, Tq, H, dh = q.shape
    L = k.shape[1]
    qb = Q_BLOCK if Tq % Q_BLOCK == 0 else Tq
    nqb = Tq // qb
    qs = q.reshape(B, nqb, qb, H, dh).transpose(1, 0, 2, 3, 4)
    ps = q_pos.reshape(nqb, qb)
    kpos = jnp.arange(L, dtype=jnp.int32)
    scale = HEAD_DIM ** -0.5

    def one(args):
        qh, pos = args
        z = jnp.einsum('bqhd,bkhd->bhqk', qh, k).astype(jnp.float32) * scale
        before = kpos[None, :] < pos[:, None]
        log_keep = jnp.where(before, jax.nn.log_sigmoid(-z), 0.0)
        later = lax.cumsum(log_keep, axis=3, reverse=True) - log_keep
        w = jnp.where(before, jnp.exp(jax.nn.log_sigmoid(z) + later), 0.0)
        return jnp.einsum('bhqk,bkhd->bqhd', w.astype(v.dtype), v)

    out = lax.map(one, (qs, ps))
    return out.transpose(1, 0, 2, 3, 4).reshape(B, Tq, H, dh)


def spatial_gating(u, v, sgu_w, sgu_b):
    B, T, _ = v.shape
    c = min(T, CHUNK)
    nc = T // c
    w = jnp.tril(sgu_w[:, :c, :c])
    vg = v.reshape(B, nc, c, B_GROUPS, B_GROUP_DIM)
    f = jnp.einsum('gts,bnsgd->bntgd', w, vg) + sgu_b[:, :c].T[None, None, :, :, None]
    return u * f.reshape(B, T, B_WIDTH)


def pool_mixer(u_prev, u_new, pos, pool_w, pool_scale):
    B, T, _ = u_new.shape
    P = u_prev.shape[1]
    u_all = jnp.concatenate([u_prev, u_new], axis=1).astype(jnp.float32)
    cs = jnp.pad(jnp.cumsum(u_all, axis=1), ((0, 0), (1, 0), (0, 0)))
    end = cs[:, P + 1:P + 1 + T]
    parts = []
    for g, w in enumerate(POOL_WINDOWS):
        sl = slice(g * POOL_GROUP_DIM, (g + 1) * POOL_GROUP_DIM)
        start = cs[:, P + 1 - w:P + 1 - w + T, sl]
        cnt = jnp.minimum(pos + 1, w).astype(jnp.float32)[None, :, None]
        parts.append((end[..., sl] - start) / cnt)
    pooled = jnp.concatenate(parts, axis=-1) - u_new.astype(jnp.float32)
    pooled = pooled.astype(u_new.dtype).reshape(B, T, POOL_GROUPS, POOL_GROUP_DIM)
    mixed = jnp.einsum('btgc,gcd->btgd', pooled, pool_w).reshape(B, T, D_WIDTH)
    return mixed * pool_scale


def moba_gmlp_mixer(x, pos, past_kv, norm_g, w_in, sgu_gain, sgu_w, sgu_b, w_out):
    B, T, _ = x.shape
    h = rmsnorm(x, norm_g)
    z = h @ w_in
    q, k, v, bu, bv = jnp.split(z, [A_WIDTH, 2 * A_WIDTH, 3 * A_WIDTH, 3 * A_WIDTH + B_WIDTH], axis=-1)
    q = rope_partial(q.reshape(B, T, A_HEADS, HEAD_DIM), pos)
    k = rope_partial(k.reshape(B, T, A_HEADS, HEAD_DIM), pos)
    v = v.reshape(B, T, A_HEADS, HEAD_DIM)
    new_kv = jnp.stack([k, v], axis=2)
    if past_kv is None:
        k_all, v_all = k, v
    else:
        k_all = jnp.concatenate([past_kv[:, :, 0], k], axis=1)
        v_all = jnp.concatenate([past_kv[:, :, 1], v], axis=1)
    a = moba_attention(q, k_all, v_all, pos).reshape(B, T, A_WIDTH)
    bvn = layer_norm(jax.nn.gelu(bv), sgu_gain)
    b_out = spatial_gating(jax.nn.gelu(bu), bvn, sgu_w, sgu_b)
    y = jnp.concatenate([a, b_out], axis=-1) @ w_out
    return x + y, new_kv, bvn


def sb_pool_mixer(x, pos, past_kv, pool_prev, norm_g, w_in, pool_w, pool_scale, w_out):
    B, T, _ = x.shape
    h = rmsnorm(x, norm_g)
    z = h @ w_in
    q, k, v, u = jnp.split(z, [C_WIDTH, 2 * C_WIDTH, 3 * C_WIDTH], axis=-1)
    q = q.reshape(B, T, C_HEADS, HEAD_DIM)
    k = k.reshape(B, T, C_HEADS, HEAD_DIM)
    v = v.reshape(B, T, C_HEADS, HEAD_DIM)
    new_kv = jnp.stack([k, v], axis=2)
    if past_kv is None:
        k_all, v_all = k, v
    else:
        k_all = jnp.concatenate([past_kv[:, :, 0], k], axis=1)
        v_all = jnp.concatenate([past_kv[:, :, 1], v], axis=1)
    c_out = stick_breaking_attention(q, k_all, v_all, pos).reshape(B, T, C_WIDTH)
    d_out = pool_mixer(pool_prev, u, pos, pool_w, pool_scale)
    new_pool = jnp.concatenate([pool_prev, u], axis=1)[:, -POOL_STATE:]
    y = jnp.concatenate([c_out, d_out], axis=-1) @ w_out
    return x + y, new_kv, new_pool


def conv_ffn(x, prev, norm_g, w_gate, w_up, conv_w, conv_b, w_down):
    T = x.shape[1]
    h = rmsnorm(x, norm_g)
    g = h @ w_gate
    g_all = jnp.concatenate([prev, g], axis=1)
    gc = sum(conv_w[i] * g_all[:, i:i + T] for i in range(CONV_W)) + conv_b
    a = jax.nn.gelu(gc) * (h @ w_up)
    return x + a @ w_down, g_all[:, -(CONV_W - 1):]


def setup_inputs(seed: int = 0) -> dict:
    key = jax.random.key(seed)
    ks = jax.random.split(key, 32)
    f32 = jnp.float32
    nrm = lambda k, shape, s: jax.random.normal(k, shape, f32) * s
    n_pages = PAST_LEN // PAGE_SIZE
    n_used = DEC_BATCH * n_pages
    n_phys = n_used + n_used // 4
    perm = jax.random.permutation(ks[0], n_phys)
    page_table = perm[:n_used].reshape(DEC_BATCH, n_pages).astype(jnp.int32)
    in0 = 3 * A_WIDTH + 2 * B_WIDTH
    in1 = 3 * C_WIDTH + D_WIDTH
    return {
        "x_prompt": nrm(ks[1], (BATCH, SEQ, D_MODEL), 1.0),
        "x_sample": nrm(ks[2], (DEC_BATCH, DEC_SEQ, D_MODEL), 1.0),
        "cache_l0_kv": nrm(ks[3], (n_phys, PAGE_SIZE, 2, A_HEADS, HEAD_DIM), 1.0),
        "cache_l1_kv": nrm(ks[4], (n_phys, PAGE_SIZE, 2, C_HEADS, HEAD_DIM), 1.0),
        "page_table": page_table,
        "state_l1_pool": nrm(ks[5], (DEC_BATCH, POOL_STATE, D_WIDTH), 1.0),
        "state_ffn_conv": nrm(ks[6], (DEPTH, DEC_BATCH, CONV_W - 1, D_FF), 1.0),
        "l0_norm": 1.0 + nrm(ks[7], (D_MODEL,), 0.02),
        "l0_w_in": nrm(ks[8], (D_MODEL, in0), D_MODEL ** -0.5),
        "l0_sgu_gain": 1.0 + nrm(ks[9], (B_WIDTH,), 0.02),
        "l0_sgu_w": nrm(ks[10], (B_GROUPS, CHUNK, CHUNK), CHUNK ** -0.5),
        "l0_sgu_b": 1.0 + nrm(ks[11], (B_GROUPS, CHUNK), 0.1),
        "l0_w_out": nrm(ks[12], (MIX_WIDTH, D_MODEL), MIX_WIDTH ** -0.5),
        "l1_norm": 1.0 + nrm(ks[13], (D_MODEL,), 0.02),
        "l1_w_in": nrm(ks[14], (D_MODEL, in1), D_MODEL ** -0.5),
        "l1_pool_w": nrm(ks[15], (POOL_GROUPS, POOL_GROUP_DIM, POOL_GROUP_DIM), POOL_GROUP_DIM ** -0.5),
        "l1_pool_scale": 1.0 + nrm(ks[16], (D_WIDTH,), 0.1),
        "l1_w_out": nrm(ks[17], (MIX_WIDTH, D_MODEL), MIX_WIDTH ** -0.5),
        "ffn_norm": 1.0 + nrm(ks[18], (DEPTH, D_MODEL), 0.02),
        "ffn_w_gate": nrm(ks[19], (DEPTH, D_MODEL, D_FF), D_MODEL ** -0.5),
        "ffn_w_up": nrm(ks[20], (DEPTH, D_MODEL, D_FF), D_MODEL ** -0.5),
        "ffn_conv_w": nrm(ks[21], (DEPTH, CONV_W, D_FF), CONV_W ** -0.5),
        "ffn_conv_b": nrm(ks[22], (DEPTH, D_FF), 0.02),
        "ffn_w_down": nrm(ks[23], (DEPTH, D_FF, D_MODEL), D_FF ** -0.5),
        "final_norm": 1.0 + nrm(ks[24], (D_MODEL,), 0.02),
    }


def reference(x_prompt, x_sample, cache_l0_kv, cache_l1_kv, page_table, state_l1_pool, state_ffn_conv,
              l0_norm, l0_w_in, l0_sgu_gain, l0_sgu_w, l0_sgu_b, l0_w_out,
              l1_norm, l1_w_in, l1_pool_w, l1_pool_scale, l1_w_out,
              ffn_norm, ffn_w_gate, ffn_w_up, ffn_conv_w, ffn_conv_b, ffn_w_down, final_norm):
    pos_p = jnp.arange(SEQ, dtype=jnp.int32)
    pos_s = PAST_LEN + jnp.arange(DEC_SEQ, dtype=jnp.int32)
    past0 = gather_pages(cache_l0_kv, page_table)
    past1 = gather_pages(cache_l1_kv, page_table)
    bp = x_prompt.shape[0]
    pool_zero = jnp.zeros((bp, POOL_STATE, D_WIDTH), x_prompt.dtype)
    conv_zero = jnp.zeros((bp, CONV_W - 1, D_FF), x_prompt.dtype)
    xp, xs = x_prompt, x_sample
    conv_p, conv_s = [], []
    for layer in range(DEPTH):
        if layer % 2 == 0:
            xp, kv0_p, _ = moba_gmlp_mixer(xp, pos_p, None, l0_norm, l0_w_in, l0_sgu_gain, l0_sgu_w, l0_sgu_b, l0_w_out)
            xs, kv0_s, sgu_v_s = moba_gmlp_mixer(xs, pos_s, past0, l0_norm, l0_w_in, l0_sgu_gain, l0_sgu_w, l0_sgu_b, l0_w_out)
        else:
            xp, kv1_p, pool_p = sb_pool_mixer(xp, pos_p, None, pool_zero, l1_norm, l1_w_in, l1_pool_w, l1_pool_scale, l1_w_out)
            xs, kv1_s, pool_s = sb_pool_mixer(xs, pos_s, past1, state_l1_pool, l1_norm, l1_w_in, l1_pool_w, l1_pool_scale, l1_w_out)
        xp, cp = conv_ffn(xp, conv_zero, ffn_norm[layer], ffn_w_gate[layer], ffn_w_up[layer], ffn_conv_w[layer], ffn_conv_b[layer], ffn_w_down[layer])
        xs, cs = conv_ffn(xs, state_ffn_conv[layer], ffn_norm[layer], ffn_w_gate[layer], ffn_w_up[layer], ffn_conv_w[layer], ffn_conv_b[layer], ffn_w_down[layer])
        conv_p.append(cp)
        conv_s.append(cs)
    y_prompt = rmsnorm(xp, final_norm)
    y_sample = rmsnorm(xs, final_norm)
    ffn_conv_p = jnp.stack(conv_p, axis=0)
    ffn_conv_s = jnp.stack(conv_s, axis=0)
    return (y_prompt, y_sample, kv0_p, kv0_s, sgu_v_s, kv1_p, kv1_s, pool_p, pool_s, ffn_conv_p, ffn_conv_s)
```

```python
import numpy as np
from contextlib import ExitStack
import concourse.bass as bass
import concourse.mybir as mybir
from concourse.bass_utils import run_bass_kernel_spmd

F32 = mybir.dt.float32
BF16 = mybir.dt.bfloat16
I32 = mybir.dt.int32
AF = mybir.ActivationFunctionType
ALU = mybir.AluOpType
AX = mybir.AxisListType

T = 2048
NS = 16
TT = T + NS
D = 1024
DFF = 2816
NFC = 22
EPS = 1e-6
NPHYS = 2560
NEG = -30000.0
STAGE = 99
SAMPLE = True


class Prog:
    def __init__(self, nc, es, ndma=20):
        self.nc = nc
        self.E = {"pe": nc.tensor, "act": nc.scalar, "dve": nc.vector, "pool": nc.gpsimd, "sp": nc.sync}
        self.sem = {k: es.enter_context(nc.semaphore("s_" + k)) for k in self.E}
        self.dsem = [es.enter_context(nc.semaphore("d%d" % i)) for i in range(ndma)]
        self.cnt = {k: 0 for k in self.E}
        self.seen = {k: {} for k in self.E}
        self.dval = [0] * ndma
        self.drr = 0
        self.lastw = {}
        self.readers = {}
        self.nins = 0
        self.log = []
        self._cur = None

    def _waits(self, eng, reads, writes, extra=()):
        need = {}

        def add(ev):
            k = ev[:2]
            if ev[2] > need.get(k, 0):
                need[k] = ev[2]

        for k in reads:
            if k in self.lastw:
                add(self.lastw[k])
        for k in writes:
            if k in self.lastw:
                add(self.lastw[k])
            for ev in self.readers.get(k, ()):
                add(ev)
        for ev in extra:
            add(ev)
        e = self.E[eng]
        for k, v in need.items():
            if k[0] == "e" and k[1] == eng and eng == "pe":
                continue
            if self.seen[eng].get(k, 0) >= v:
                continue
            self.seen[eng][k] = v
            s = self.sem[k[1]] if k[0] == "e" else self.dsem[k[1]]
            e.wait_ge(s, v)
            self.log.append((eng, 'wait', k, v))

    def _record(self, ev, reads, writes):
        for k in reads:
            self.readers.setdefault(k, []).append(ev)
        for k in writes:
            self.lastw[k] = ev
            self.readers[k] = []

    def op(self, eng, name, r=(), w=(), inc=True, **kw):
        self._waits(eng, r, w)
        ins = getattr(self.E[eng], name)(**kw)
        if inc:
            self.cnt[eng] += 1
            ins.then_inc(self.sem[eng], 1)
            self.log.append((eng, 'inc', ('e', eng), 1, name))
            ev = ("e", eng, self.cnt[eng])
        else:
            ev = ("e", eng, self.cnt[eng] + 1)
        self._record(ev, r, w)
        self.nins += 1
        return ins

    def dma(self, out, in_, r=(), w=(), eng="sp", **kw):
        i = self.drr
        self.drr = (self.drr + 1) % len(self.dsem)
        extra = [("d", i, self.dval[i])] if self.dval[i] else []
        self._waits(eng, r, w, extra)
        self.dval[i] += 16
        self.E[eng].dma_start(out=out, in_=in_, **kw).then_inc(self.dsem[i], 16)
        self.log.append((eng, 'inc', ('d', i), 16, 'dma'))
        self._record(("d", i, self.dval[i]), r, w)
        self.nins += 1

    def idma(self, out, in_, idx, r=(), w=()):
        eng = "pool"
        i = self.drr
        self.drr = (self.drr + 1) % len(self.dsem)
        extra = [("d", i, self.dval[i])] if self.dval[i] else []
        self._waits(eng, r, w, extra)
        self.dval[i] += 16
        self.E[eng].indirect_dma_start(out=out, out_offset=None, in_=in_,
                                       in_offset=bass.IndirectOffsetOnAxis(ap=idx, axis=0)).then_inc(self.dsem[i], 16)
        self.log.append((eng, 'inc', ('d', i), 16, 'idma'))
        self._record(("d", i, self.dval[i]), r, w)
        self.nins += 1

    def barrier(self):
        for eng, e in self.E.items():
            for o in self.E:
                if o != eng and self.cnt[o] > self.seen[eng].get(("e", o), 0):
                    self.seen[eng][("e", o)] = self.cnt[o]
                    e.wait_ge(self.sem[o], self.cnt[o])
                    self.log.append((eng, 'wait', ('e', o), self.cnt[o]))
            for i, v in enumerate(self.dval):
                if v > self.seen[eng].get(("d", i), 0):
                    self.seen[eng][("d", i)] = v
                    e.wait_ge(self.dsem[i], v)
                    self.log.append((eng, 'wait', ('d', i), v))
        self.lastw = {}
        self.readers = {}

    def finish(self):
        e = self.E["sp"]
        for i, v in enumerate(self.dval):
            if v:
                e.wait_ge(self.dsem[i], v)
        for o in self.E:
            if o != "sp" and self.cnt[o]:
                e.wait_ge(self.sem[o], self.cnt[o])


def build(NPHYS=NPHYS):
    nc = bass.Bass("TRN2", target_bir_lowering=False)

    def din(name, shape, dt=F32):
        return nc.dram_tensor(name, list(shape), dt, kind="ExternalInput").ap()

    def dout(name, shape):
        return nc.dram_tensor(name, list(shape), F32, kind="ExternalOutput").ap()

    x_p = din("x_p", [T, D])
    x_s = din("x_s", [NS, D])
    cache0 = din("cache0", [NPHYS, 128, 1024])
    cache1 = din("cache1", [NPHYS, 128, 1024])
    ptab = din("ptab", [NS, 16], I32)
    pool_st = din("pool_st", [NS, 15 * 512])
    conv_st = din("conv_st", [2, NS, 2 * DFF])
    w_in0 = din("w_in0", [D, 2560])
    w_out0 = din("w_out0", [D, D])
    w_in1 = din("w_in1", [D, 2048])
    w_out1 = din("w_out1", [D, D])
    w_gate = din("w_gate", [2, D, DFF])
    w_up = din("w_up", [2, D, DFF])
    w_down = din("w_down", [2, DFF, D])
    sgu_wT = din("sgu_wT", [128, 4, 128])
    pool_w = din("pool_w", [128, 4, 128])
    gcols = din("gcols", [128, 40])
    convc = din("convc", [128, 2 * 4 * NFC])
    sgu_gain_bc = din("sgu_gain_bc", [128, 512])
    sgu_bcol = din("sgu_bcol", [128, 4])
    sgu_s_w = din("sgu_s_w", [NS, 512])
    sgu_s_b = din("sgu_s_b", [NS, 512])
    pool_scol = din("pool_scol", [128, 4])
    pool_fix = din("pool_fix", [128, 4 * 16])
    ropec = din("ropec", [128, 17, 8])
    ropes = din("ropes", [128, 17, 8])

    y_p = dout("y_p", [T, D])
    y_s = dout("y_s", [NS, D])
    kv0_p = dout("kv0_p", [T, 1024])
    kv0_s = dout("kv0_s", [NS, 1024])
    sgu_s = dout("sgu_s", [NS, 512])
    kv1_p = dout("kv1_p", [T, 1024])
    kv1_s = dout("kv1_s", [NS, 1024])
    pool_p = dout("pool_p", [15, 512])
    pool_s = dout("pool_s", [NS, 15 * 512])
    conv_p = dout("conv_p", [2, 2, DFF])
    conv_s = dout("conv_s", [2, NS, 2 * DFF])
    dscr = nc.dram_tensor("dscr", [128, 4, TT], BF16, kind="Internal").ap()
    qscr0 = nc.dram_tensor("qscr0", [NS, 512], F32, kind="Internal").ap()
    qscr1 = nc.dram_tensor("qscr1", [NS, 512], F32, kind="Internal").ap()

    es = ExitStack()
    with es:
        P = Prog(nc, es)

        def sb(name, shape, dt=F32, stack=es):
            return stack.enter_context(nc.sbuf_tensor(name, list(shape), dt))

        banks = [es.enter_context(nc.psum_tensor("ps%d" % i, [128, 512], F32)) for i in range(8)]
        banksb = [b.bitcast(BF16) for b in banks]

        def bkey(i):
            return ("ps", i)

        xT = sb("xT", [128, 8, TT])
        hT = sb("hT", [128, 8, TT], BF16)
        ident_b = sb("ident_b", [128, 128], BF16)
        ident_f = sb("ident_f", [128, 128])
        ones_b = sb("ones_b", [128, 128], BF16)
        tri_b = sb("tri_b", [128, 128], BF16)
        tris_b = sb("tris_b", [128, 128], BF16)
        tris_f = sb("tris_f", [128, 128])
        gcol_t = sb("gcol_t", [128, 40])
        convc_t = sb("convc_t", [128, 2, 4, NFC])
        ropec_t = sb("ropec_t", [128, 17, 8])
        ropes_t = sb("ropes_t", [128, 17, 8])
        rs_t = sb("rs_t", [128, 2, 512])
        sq_t = sb("sq_t", [128, 4, 512], BF16)
        eps_t = sb("eps_t", [128, 1])

        def tkey(t):
            return ("hT", t)

        def gtiles(g):
            return list(range(4 * g, 4 * g + 4)) if g < 4 else [16]

        def gcolsl(g):
            return slice(512 * g, 512 * g + 512) if g < 4 else slice(T, TT)

        def tcols(t):
            return slice(128 * t, 128 * t + 128) if t < 16 else slice(T, TT)

        def trows(t):
            return 128 if t < 16 else NS

        def hkeys(g):
            return [tkey(t) for t in gtiles(g)]

        for tl in (ident_b, ident_f):
            P.op("pool", "memset", w=[tl.name], ap=tl[:], constant=1.0)
            P.op("pool", "affine_select", r=[tl.name], w=[tl.name], out=tl[:], in_=tl[:], pattern=[[-1, 128]],
                 compare_op=ALU.is_equal, fill=0.0, base=0, channel_multiplier=1)
        P.op("pool", "memset", w=["ones_b"], ap=ones_b[:], constant=1.0)
        P.op("pool", "memset", w=["tri_b"], ap=tri_b[:], constant=1.0)
        P.op("pool", "affine_select", r=["tri_b"], w=["tri_b"], out=tri_b[:], in_=tri_b[:], pattern=[[-1, 128]],
             compare_op=ALU.is_ge, fill=0.0, base=0, channel_multiplier=1)
        for tl in (tris_b, tris_f):
            P.op("pool", "memset", w=[tl.name], ap=tl[:], constant=1.0)
            P.op("pool", "affine_select", r=[tl.name], w=[tl.name], out=tl[:], in_=tl[:], pattern=[[-1, 128]],
                 compare_op=ALU.is_gt, fill=0.0, base=0, channel_multiplier=1)
        P.op("pool", "memset", w=["eps"], ap=eps_t[:], constant=EPS)
        P.dma(gcol_t[:], gcols[:, :], w=["gcol"])
        P.dma(convc_t[:].rearrange("p a b c -> p (a b c)"), convc[:, :], w=["convc"])
        P.dma(ropec_t[:], ropec[:, :, :], w=["rope"])
        P.dma(ropes_t[:], ropes[:, :, :], w=["rope"])

        rr = {"wst": 0, "wbf": 0, "sq": 0, "ev": 0, "bk": 0}
        W = {}

        def evac_eng():
            rr["ev"] ^= 1
            return "act" if rr["ev"] else "dve"

        def copy(eng, out, in_, r, w):
            if eng == "act":
                P.op("act", "activation", r=r, w=w, out=out, in_=in_, func=AF.Copy)
            else:
                P.op(eng, "tensor_copy", r=r, w=w, out=out, in_=in_)

        def nextbank(lo, hi):
            b = lo + rr["bk"] % (hi - lo)
            rr["bk"] += 1
            return b

        def walloc(stack, nb):
            W["wst"] = sb("wst%d" % P.nins, [128, 2, 512], F32, stack)
            W["wbf"] = sb("wbf%d" % P.nins, [128, nb, 8 * 512], BF16, stack)
            W["nb"] = nb
            rr["wbf"] = 0

        def load_w(src, nk=8, ncols=512):
            b = rr["wbf"]
            rr["wbf"] = (b + 1) % W["nb"]
            key = ("wbf", b)
            view = W["wbf"][:, b, 0:nk * ncols].rearrange("p (k c) -> p k c", k=nk)
            for k in range(nk):
                for c0 in range(0, ncols, 512):
                    n = min(512, ncols - c0)
                    s = rr["wst"]
                    rr["wst"] ^= 1
                    P.dma(W["wst"][:, s, 0:n], src[k * 128:(k + 1) * 128, c0:c0 + n], w=[("wst", s)])
                    P.op("pool", "tensor_copy", r=[("wst", s)], w=[key], out=view[:, k, c0:c0 + n], in_=W["wst"][:, s, 0:n])
            return view, key

        def rmsnorm_stats(g):
            cs = gcolsl(g)
            n = cs.stop - cs.start
            bk = 6 + (g % 2)
            for c in range(8):
                s = rr["sq"]
                rr["sq"] = (s + 1) % 4
                P.op("act", "activation", r=[("xT", g)], w=[("sq", s)], out=sq_t[:, s, 0:n], in_=xT[:, c, cs], func=AF.Square)
                P.op("pe", "matmul", r=[("sq", s), "ones_b"], w=[bkey(bk)], out=banks[bk][:, 0:n],
                     lhsT=ones_b[:], rhs=sq_t[:, s, 0:n], start=(c == 0), stop=(c == 7))
            ri = g % 2
            P.op("act", "activation", r=[bkey(bk), "eps"], w=[("rs", ri)], out=rs_t[:, ri, 0:n], in_=banks[bk][:, 0:n],
                 func=AF.Sqrt, scale=1.0 / D, bias=eps_t[:, 0:1])
            P.op("dve", "reciprocal", r=[("rs", ri)], w=[("rs", ri)], out=rs_t[:, ri, 0:n], in_=rs_t[:, ri, 0:n])
            return ri, n, cs

        def rmsnorm(nidx):
            for g in range(5):
                ri, n, cs = rmsnorm_stats(g)
                for c in range(8):
                    P.op("dve", "scalar_tensor_tensor", r=[("xT", g), ("rs", ri), "gcol"], w=hkeys(g),
                         out=hT[:, c, cs], in0=xT[:, c, cs], scalar=gcol_t[:, nidx * 8 + c:nidx * 8 + c + 1],
                         in1=rs_t[:, ri, 0:n], op0=ALU.mult, op1=ALU.mult)

        def transpose_to_hT(src_tile, src_key, rows, c0, t, eng=None):
            bk = nextbank(4, 6)
            for j in range(4):
                P.op("pe", "transpose", r=[src_key, "ident_b"], w=[bkey(bk)], out=banksb[bk][:, j * 128:j * 128 + rows],
                     in_=src_tile[:, j * 128:(j + 1) * 128], identity=ident_b[0:rows, 0:rows])
            copy(eng or evac_eng(), hT[:, c0:c0 + 4, tcols(t)],
                 banksb[bk][:, 0:512].rearrange("p (j q) -> p j q", j=4)[:, :, 0:rows], r=[bkey(bk)], w=[tkey(t)])

        def out_proj(wsrc):
            for dcg in range(2):
                wv, wk = load_w(wsrc[:, dcg * 512:(dcg + 1) * 512])
                for j in range(4):
                    dc = dcg * 4 + j
                    for g in range(5):
                        cs = gcolsl(g)
                        n = cs.stop - cs.start
                        bk = nextbank(0, 4)
                        for kc in range(8):
                            P.op("pe", "matmul", r=hkeys(g) + [wk], w=[bkey(bk)], out=banks[bk][:, 0:n],
                                 lhsT=wv[:, kc, j * 128:(j + 1) * 128], rhs=hT[:, kc, cs], start=(kc == 0), stop=(kc == 7))
                        P.op("dve", "tensor_tensor", r=[bkey(bk), ("xT", g)], w=[("xT", g)], out=xT[:, dc, cs],
                             in0=xT[:, dc, cs], in1=banks[bk][:, 0:n], op=ALU.add)

        with ExitStack() as ph:
            xin = sb("xin", [128, 2, D], stack=ph)
            for t in range(17):
                rows = trows(t)
                s = t % 2
                src = x_p[t * 128:(t + 1) * 128, :] if t < 16 else x_s[:, :]
                P.dma(xin[0:rows, s, :], src, w=[("xin", s)])
                for half in range(2):
                    bk = (2 * t + half) % 4
                    for j in range(4):
                        c = half * 4 + j
                        P.op("pe", "transpose", r=[("xin", s), "ident_f"], w=[bkey(bk)],
                             out=banks[bk][:, j * 128:j * 128 + rows], in_=xin[0:rows, s, c * 128:(c + 1) * 128],
                             identity=ident_f[0:rows, 0:rows])
                    copy(evac_eng(), xT[:, half * 4:half * 4 + 4, tcols(t)],
                         banks[bk][:].rearrange("p (j q) -> p j q", j=4)[:, :, 0:rows], r=[bkey(bk)], w=[("xT", min(t // 4, 4))])
            P.barrier()

        def qkv_proj(ph, wsrc, layer, kvp, kvs, QT, KT, Vt, qs_tok, ks_tok, vs_tok):
            stg = sb("stg%d" % layer, [128, 2, 512], F32, ph)
            bfs = sb("bfs%d" % layer, [128, 2, 512], BF16, ph)
            rt = sb("rt%d" % layer, [128, 4, 64], F32, ph)
            for cg in range(3):
                wv, wk = load_w(wsrc[:, cg * 512:(cg + 1) * 512])
                for t in range(17):
                    rows = trows(t)
                    bk = nextbank(0, 4)
                    s = t % 2
                    for kc in range(8):
                        P.op("pe", "matmul", r=[tkey(t), wk], w=[bkey(bk)], out=banks[bk][0:rows, :],
                             lhsT=hT[:, kc, tcols(t)], rhs=wv[:, kc, :], start=(kc == 0), stop=(kc == 7))
                    sk = ("stg", s)
                    P.op("act", "activation", r=[bkey(bk)], w=[sk], out=stg[0:rows, s, :], in_=banks[bk][0:rows, :], func=AF.Copy)
                    if layer == 0 and cg < 2:
                        X = stg[0:rows, s, :].rearrange("p (h d) -> p h d", h=8)
                        x1 = X[:, :, 0:8]
                        x2 = X[:, :, 8:16]
                        ca_ = ropec_t[0:rows, t, :]
                        sa_ = ropes_t[0:rows, t, :]
                        cb = bass.AP(ropec_t, ca_.offset, [list(ca_.ap[0]), [0, 8], [1, 8]])
                        sbb = bass.AP(ropes_t, sa_.offset, [list(sa_.ap[0]), [0, 8], [1, 8]])
                        tv = [rt[0:rows, i, :].rearrange("p (h d) -> p h d", h=8) for i in range(4)]
                        P.op("dve", "tensor_tensor", r=[sk, "rope"], w=["rt0"], out=tv[0], in0=x1, in1=cb, op=ALU.mult)
                        P.op("dve", "tensor_tensor", r=[sk, "rope"], w=["rt1"], out=tv[1], in0=x2, in1=sbb, op=ALU.mult)
                        P.op("dve", "tensor_tensor", r=[sk, "rope"], w=["rt2"], out=tv[2], in0=x2, in1=cb, op=ALU.mult)
                        P.op("dve", "tensor_tensor", r=[sk, "rope"], w=["rt3"], out=tv[3], in0=x1, in1=sbb, op=ALU.mult)
                        P.op("dve", "tensor_tensor", r=["rt0", "rt1"], w=[sk], out=x1, in0=tv[0], in1=tv[1], op=ALU.subtract)
                        P.op("dve", "tensor_tensor", r=["rt2", "rt3"], w=[sk], out=x2, in0=tv[2], in1=tv[3], op=ALU.add)
                    if cg >= 1:
                        dst = (kvp[t * 128:(t + 1) * 128, (cg - 1) * 512:cg * 512] if t < 16 else kvs[:, (cg - 1) * 512:cg * 512])
                        P.dma(dst, stg[0:rows, s, :], r=[sk])
                    if t == 16:
                        tok = (qs_tok, ks_tok, vs_tok)[cg]
                        P.op("dve", "tensor_copy", r=[sk], w=[tok.name], out=tok[:], in_=stg[0:rows, s, :])
                        continue
                    if cg == 2:
                        P.op("pool", "tensor_copy", r=[sk], w=[("Vt", t)], out=Vt[:, t, :], in_=stg[:, s, :])
                    else:
                        bkk = ("bfs", s)
                        P.op("pool", "tensor_copy", r=[sk], w=[bkk], out=bfs[:, s, :], in_=stg[:, s, :])
                        dstT = QT if cg == 0 else KT
                        bk2 = nextbank(4, 6)
                        for j in range(4):
                            P.op("pe", "transpose", r=[bkk, "ident_b"], w=[bkey(bk2)], out=banksb[bk2][:, j * 128:(j + 1) * 128],
                                 in_=bfs[:, s, j * 128:(j + 1) * 128], identity=ident_b[:])
                        copy("dve", dstT[:, :, tcols(t)], banksb[bk2][:, 0:512].rearrange("p (j q) -> p j q", j=4),
                             r=[bkey(bk2)], w=[("QT" if cg == 0 else "KT", t)])

        def attention(ph, layer, QT, KT, Vt):
            pexp = sb("pexp%d" % layer, [128, 2, T], BF16, ph)
            PTs = sb("PTs%d" % layer, [128, 4, 512], BF16, ph)
            atok = sb("atok%d" % layer, [128, 2, 512], BF16, ph)
            rsum = sb("rsum%d" % layer, [128, 2, 16], F32, ph)
            rtot = sb("rtot%d" % layer, [128, 2, 2], F32, ph)
            dtmp = sb("dtmp%d" % layer, [128, 2, 128], BF16, ph)
            if layer == 0:
                kms = sb("kms", [128, 4, 8], F32, ph)
                kmT = sb("kmT", [128, 4, 8], BF16, ph)
                gsb = sb("gsb", [128, 8, 8], F32, ph)
                m8 = sb("m8", [128, 8, 8], F32, ph)
                bias_t = sb("bias_t", [128, 2, 64], F32, ph)
                for c in range(4):
                    P.op("dve", "tensor_reduce", r=[("KT", t) for t in range(16)], w=["kms"], out=kms[:, c, :],
                         in_=KT[:, c, :].rearrange("p (n s) -> p n s", s=256), axis=AX.X, op=ALU.add)
                P.op("act", "activation", r=["kms"], w=["kmT"], out=kmT[:], in_=kms[:], func=AF.Copy, scale=1.0 / 256)
                KMb = sb("KMb", [128, 4, 64], BF16, ph)
                P.op("pool", "memset", w=["KMb"], ap=KMb[:], constant=0.0)
                for c in range(4):
                    P.op("dve", "tensor_copy", r=["kmT"], w=["KMb"], out=KMb[0:64, c, (2 * c) * 8:(2 * c) * 8 + 8], in_=kmT[0:64, c, :])
                    P.op("dve", "tensor_copy", r=["kmT"], w=["KMb"], out=KMb[64:128, c, (2 * c + 1) * 8:(2 * c + 1) * 8 + 8], in_=kmT[64:128, c, :])
            else:
                sprow = sb("sprow", [128, 1 + T], F32, ph)
                csx = sb("csx", [128, T], F32, ph)
                etmp = sb("etmp", [128, 2, 512], F32, ph)
                negT = sb("negT", [128, 2], F32, ph)
                carry = sb("carry", [128, 8], F32, ph)
                P.op("pool", "memset", w=["sprow"], ap=sprow[:, 0:1], constant=0.0)
            pi = 0
            for i in range(16):
                G = i // 2
                qk = [("QT", i)]
                W_ = 128 * (i + 1)
                if layer == 0:
                    bsl = i % 2
                    bkg = 6
                    for c in range(4):
                        P.op("pe", "matmul", r=qk + ["KMb"], w=[bkey(bkg)], out=banks[bkg][:, 0:64],
                             lhsT=QT[:, c, tcols(i)], rhs=KMb[:, c, :], start=(c == 0), stop=(c == 3))
                    if G >= 4:
                        P.op("act", "activation", r=[bkey(bkg)], w=["gsb"], out=gsb[:].rearrange("p h n -> p (h n)"),
                             in_=banks[bkg][:, 0:64], func=AF.Copy)
                        if G < 8:
                            P.op("pool", "memset", r=[], w=["gsb"], ap=gsb[:, :, G:8], constant=-1e30)
                        for h in range(8):
                            P.op("dve", "max", r=["gsb"], w=["m8"], out=m8[:, h, :], in_=gsb[:, h, :])
                        for h in range(8):
                            P.op("dve", "tensor_scalar", r=["gsb", "m8"], w=[("bias", bsl)], out=bias_t[:, bsl, h * 8:(h + 1) * 8],
                                 in0=gsb[:, h, :], scalar1=m8[:, h, 2:3], scalar2=NEG, op0=ALU.is_lt, op1=ALU.mult)
                    else:
                        P.op("pool", "memset", w=[("bias", bsl)], ap=bias_t[:, bsl, :], constant=0.0)
                for h in range(8):
                    c, po = h // 2, (h % 2) * 64
                    ps_ = pi % 2
                    pi += 1
                    pk = ("pexp", ps_)
                    kall = [("KT", t) for t in range(i + 1)]
                    nsl = 0
                    if layer == 0:
                        for kc in range(i + 1):
                            bk = nextbank(0, 4)
                            P.op("pe", "matmul", r=qk + kall, w=[bkey(bk)], out=banks[bk][:, 0:128],
                                 lhsT=QT[po:po + 64, c, tcols(i)], rhs=KT[po:po + 64, c, kc * 128:(kc + 1) * 128], start=True, stop=True)
                            if kc == i:
                                ds_ = ps_
                                P.op("act", "activation", r=[bkey(bk)], w=[("dtmp", ds_)], out=dtmp[:, ds_, :],
                                     in_=banks[bk][:, 0:128], func=AF.Exp, scale=0.125)
                                P.op("dve", "tensor_tensor", r=[("dtmp", ds_), "tri_b"], w=[pk], out=pexp[:, ps_, W_ - 128:W_],
                                     in0=dtmp[:, ds_, :], in1=tri_b[:], op=ALU.mult)
                            elif kc // 2 < G:
                                nb_ = kc // 2
                                P.op("act", "activation", r=[bkey(bk), ("bias", bsl)], w=[pk],
                                     out=pexp[:, ps_, kc * 128:(kc + 1) * 128], in_=banks[bk][:, 0:128], func=AF.Exp,
                                     scale=0.125, bias=bias_t[:, bsl, h * 8 + nb_:h * 8 + nb_ + 1])
                            else:
                                P.op("act", "activation", r=[bkey(bk)], w=[pk], out=pexp[:, ps_, kc * 128:(kc + 1) * 128],
                                     in_=banks[bk][:, 0:128], func=AF.Exp, scale=0.125)
                        P.op("dve", "reduce_sum", r=[pk], w=[("rtot", ps_)], out=rtot[:, ps_, 0:1],
                             in_=pexp[:, ps_, 0:W_], axis=AX.X)
                        P.op("dve", "reciprocal", r=[("rtot", ps_)], w=[("rtot", ps_)], out=rtot[:, ps_, 1:2], in_=rtot[:, ps_, 0:1])
                    else:
                        for s0 in range(0, W_, 512):
                            n = min(512, W_ - s0)
                            bk = nextbank(0, 4)
                            es_ = (s0 // 512) % 2
                            P.op("pe", "matmul", r=qk + kall, w=[bkey(bk)], out=banks[bk][:, 0:n],
                                 lhsT=QT[po:po + 64, c, tcols(i)], rhs=KT[po:po + 64, c, s0:s0 + n], start=True, stop=True)
                            P.op("act", "activation", r=[bkey(bk)], w=[("etmp", es_)], out=etmp[:, es_, 0:n], in_=banks[bk][:, 0:n],
                                 func=AF.Exp, scale=0.125)
                            P.op("act", "activation", r=[("etmp", es_)], w=["sprow"], out=sprow[:, 1 + s0:1 + s0 + n],
                                 in_=etmp[:, es_, 0:n], func=AF.Ln, bias=1.0)
                            if s0 + n == W_:
                                P.op("pool", "tensor_tensor", r=["sprow", "tris_f"], w=["sprow"], out=sprow[:, 1 + W_ - 128:1 + W_],
                                     in0=sprow[:, 1 + W_ - 128:1 + W_], in1=tris_f[:], op=ALU.mult)
                            si_ = s0 // 512
                            init = 0.0 if s0 == 0 else carry[:, si_ - 1:si_]
                            P.op("dve", "tensor_tensor_scan", r=["sprow", "csx", "carry"], w=["csx"], out=csx[:, s0:s0 + n],
                                 data0=sprow[:, s0:s0 + n], data1=sprow[:, s0:s0 + n], initial=init, op0=ALU.add, op1=ALU.bypass)
                            P.op("dve", "tensor_copy", r=["csx"], w=["carry"], out=carry[:, si_:si_ + 1], in_=csx[:, s0 + n - 1:s0 + n])
                            if s0 + n == W_:
                                P.op("dve", "tensor_scalar", r=["csx"], w=["negT"], out=negT[:, 0:1], in0=csx[:, W_ - 1:W_],
                                     scalar1=-1.0, scalar2=None, op0=ALU.mult)
                            P.op("dve", "scalar_tensor_tensor", r=[bkey(bk), "csx"], w=["csx"], out=csx[:, s0:s0 + n],
                                 in0=banks[bk][:, 0:n], scalar=0.125, in1=csx[:, s0:s0 + n], op0=ALU.mult, op1=ALU.add)
                        for s0 in range(0, W_, 512):
                            n = min(512, W_ - s0)
                            P.op("act", "activation", r=["csx", "negT"], w=[pk], out=pexp[:, ps_, s0:s0 + n], in_=csx[:, s0:s0 + n],
                                 func=AF.Exp, bias=negT[:, 0:1])
                        P.op("pool", "tensor_tensor", r=[pk, "tris_b"], w=[pk], out=pexp[:, ps_, W_ - 128:W_],
                             in0=pexp[:, ps_, W_ - 128:W_], in1=tris_b[:], op=ALU.mult)
                    bo = 6 + (pi % 2) if layer == 1 else 7
                    for k0 in range(0, i + 1, 4):
                        nk_ = min(4, i + 1 - k0)
                        bk = nextbank(4, 6)
                        pts = (k0 // 4) % 4
                        for j in range(nk_):
                            P.op("pe", "transpose", r=[pk, "ident_b"], w=[bkey(bk)], out=banksb[bk][:, j * 128:(j + 1) * 128],
                                 in_=pexp[:, ps_, (k0 + j) * 128:(k0 + j + 1) * 128], identity=ident_b[:])
                        copy(evac_eng(), PTs[:, pts, 0:nk_ * 128], banksb[bk][:, 0:nk_ * 128], r=[bkey(bk)], w=[("PTs", pts)])
                        for j in range(nk_):
                            kc = k0 + j
                            P.op("pe", "matmul", r=[("PTs", pts), ("Vt", kc)], w=[bkey(bo)], out=banks[bo][:, h * 64:(h + 1) * 64],
                                 lhsT=PTs[:, pts, j * 128:(j + 1) * 128], rhs=Vt[:, kc, h * 64:(h + 1) * 64],
                                 start=(kc == 0), stop=(kc == i))
                    asl = i % 2
                    if layer == 0:
                        P.op("act", "activation", r=[bkey(bo), ("rtot", ps_)], w=[("atok", asl)], out=atok[:, asl, h * 64:(h + 1) * 64],
                             in_=banks[bo][:, h * 64:(h + 1) * 64], func=AF.Copy, scale=rtot[:, ps_, 1:2])
                    else:
                        P.op("act", "activation", r=[bkey(bo)], w=[("atok", asl)], out=atok[:, asl, h * 64:(h + 1) * 64],
                             in_=banks[bo][:, h * 64:(h + 1) * 64], func=AF.Copy)
                transpose_to_hT(atok[:, i % 2, :], ("atok", i % 2), 128, 0, i)

        def sample_attention(ph, layer, cache, qtok, kvs_dram, qscr):
            L = "s%d" % layer
            NV = 26
            NK = 6
            vbuf = sb("vbuf" + L, [128, NV, 512], F32, ph)
            kbuf = sb("kbuf" + L, [128, NK, 512], F32, ph)
            ids_i = sb("ids_i" + L, [128, 16], I32, ph)
            idf = sb("idf" + L, [128, 16], F32, ph)
            idx = sb("idx" + L, [128, 2, 2, 16], I32, ph)
            idf2 = sb("idf2" + L, [128, 2, 16], F32, ph)
            pcol = sb("pcol" + L, [128, 1], F32, ph)
            pci = sb("pci" + L, [128, 1], I32, ph)
            rows = cache.rearrange("n t (two c) -> (n t two) c", two=2)
            P.op("pool", "iota", w=["pci"], out=pci[:], pattern=[[0, 1]], base=0, channel_multiplier=1)
            P.op("pool", "tensor_copy", r=["pci"], w=["pcol"], out=pcol[:], in_=pci[:])
            qb = sb("qb" + L, [128, 2, 512], F32, ph)
            prod = sb("prod" + L, [128, 2, 512], F32, ph)
            S_all = sb("S_all" + L, [128, 2, 17, 8], F32, ph)
            Gs = sb("Gs" + L, [128, 16, 8], F32, ph)
            gate = sb("gate" + L, [128, 8, 8], F32, ph)
            m8s = sb("m8s" + L, [128, 8, 8], F32, ph)
            biasb = sb("biasb" + L, [128, 8, 8], F32, ph)
            arg = sb("arg" + L, [128, 17, 8], F32, ph)
            carry_ = sb("carry_" + L, [128, 16, 8], F32, ph)
            Zp = sb("Zp" + L, [128, 1, 17, 128], F32, ph)
            Opad = sb("Opad" + L, [128, 512], F32, ph)
            kself = sb("kself" + L, [128, 512], F32, ph)
            vself = sb("vself" + L, [128, 512], F32, ph)
            rden = sb("rden" + L, [128, 2], F32, ph)
            ones_f = sb("ones_f" + L, [128, 128], F32, ph)
            tri_f = sb("tri_f" + L, [128, 128], F32, ph)
            P.op("pool", "memset", w=["Zp"], ap=Zp[:], constant=0.0)
            P.op("pool", "memset", w=["Opad"], ap=Opad[:], constant=0.0)
            P.op("pool", "memset", w=["kself"], ap=kself[:], constant=0.0)
            P.op("pool", "memset", w=["vself"], ap=vself[:], constant=0.0)
            P.op("pool", "memset", w=["ones_f"], ap=ones_f[:], constant=1.0)
            P.op("pool", "memset", w=["tri_f"], ap=tri_f[:], constant=1.0)
            P.op("pool", "affine_select", r=["tri_f"], w=["tri_f"], out=tri_f[:], in_=tri_f[:], pattern=[[-1, 128]],
                 compare_op=ALU.is_ge, fill=0.0, base=0, channel_multiplier=1)
            P.op("pool", "memset", w=["carry_"], ap=carry_[:], constant=0.0)
            P.dma(qscr[:, :], qtok[:], r=[qtok.name], w=["qscr"])
            npg = 17 if layer == 0 else 16
            cnt = {"k": 0, "v": 0, "r": 0}
            for s in range(NS):
                z = s % 2
                P.dma(qb[:, z, :], bass.AP(qscr.tensor, s * 512, [[0, 128], [1, 512]]), r=["qscr"], w=[("qb", z)])
                if layer == 0:
                    P.dma(kself[0:1, :], kvs_dram[s:s + 1, 0:512], w=["kself"])
                    P.dma(vself[0:1, :], kvs_dram[s:s + 1, 512:1024], w=["vself"])
                P.dma(ids_i[:], bass.AP(ptab.tensor, s * 16, [[0, 128], [1, 16]]), w=["ids_i"])
                P.op("pool", "tensor_copy", r=["ids_i"], w=["idf"], out=idf[:], in_=ids_i[:])
                P.op("pool", "tensor_scalar", r=["idf", "pcol"], w=["idf"], out=idf[:], in0=idf[:], scalar1=128.0, scalar2=pcol[:, 0:1],
                     op0=ALU.mult, op1=ALU.add)
                P.op("pool", "tensor_scalar", r=["idf"], w=["idf2"], out=idf2[:, 0, :], in0=idf[:], scalar1=2.0, scalar2=None, op0=ALU.mult)
                P.op("pool", "tensor_scalar", r=["idf"], w=["idf2"], out=idf2[:, 1, :], in0=idf[:], scalar1=2.0, scalar2=1.0, op0=ALU.mult, op1=ALU.add)
                P.op("pool", "tensor_copy", r=["idf2"], w=[("idx", z)], out=idx[:, z, :, :], in_=idf2[:])
                vsl = []
                for j in range(npg):
                    if j < 16:
                        vs_ = cnt["v"] % NV
                        cnt["v"] += 1
                        ks_ = cnt["k"] % NK
                        cnt["k"] += 1
                        P.idma(kbuf[:, ks_, :], rows[:, :], idx[:, z, 0, j:j + 1], r=[("idx", z)], w=[("kb", ks_)])
                        P.idma(vbuf[:, vs_, :], rows[:, :], idx[:, z, 1, j:j + 1], r=[("idx", z)], w=[("vb", vs_)])
                        ksrc, kkey = kbuf[:, ks_, :], ("kb", ks_)
                        vsl.append((vbuf[:, vs_, :], ("vb", vs_)))
                    else:
                        ksrc, kkey = kself[:], "kself"
                        vsl.append((vself[:], "vself"))
                    pr = j % 2
                    P.op("dve", "tensor_tensor", r=[kkey, ("qb", z)], w=[("prod", pr)], out=prod[:, pr, :], in0=ksrc, in1=qb[:, z, :], op=ALU.mult)
                    P.op("dve", "tensor_reduce", r=[("prod", pr)], w=[("S_all", z)], out=S_all[:, z, j, :],
                         in_=prod[:, pr, :].rearrange("p (h d) -> p h d", h=8), axis=AX.X, op=ALU.add)
                Sf = S_all[:, z, 0:16, :].rearrange("p a h -> p (a h)")
                sk = ("S_all", z)
                zk = ("Zp", 0)
                if layer == 0:
                    P.op("pe", "matmul", r=[sk, "ones_f"], w=[bkey(0)], out=banks[0][:, 0:128], lhsT=ones_f[:], rhs=Sf, start=True, stop=True)
                    P.op("act", "activation", r=[bkey(0)], w=["Gs"], out=Gs[:].rearrange("p a h -> p (a h)"), in_=banks[0][:, 0:128], func=AF.Copy)
                    Gv = Gs[:].rearrange("p (n i) h -> p n i h", i=2)
                    P.op("dve", "tensor_tensor", r=["Gs"], w=["gate"], out=gate[:], in0=Gv[:, :, 0, :], in1=Gv[:, :, 1, :], op=ALU.add)
                    for h in range(8):
                        P.op("dve", "max", r=["gate"], w=["m8s"], out=m8s[:, h, :], in_=gate[:, :, h])
                    for h in range(8):
                        P.op("dve", "tensor_scalar", r=["gate", "m8s"], w=["biasb"], out=biasb[:, :, h], in0=gate[:, :, h],
                             scalar1=m8s[:, h, 2:3], scalar2=NEG, op0=ALU.is_lt, op1=ALU.mult)
                    Sv = S_all[:, z, 0:16, :].rearrange("p (n i) h -> p n i h", i=2)
                    Av = arg[:, 0:16, :].rearrange("p (n i) h -> p n i h", i=2)
                    for i_ in range(2):
                        P.op("dve", "scalar_tensor_tensor", r=[sk, "biasb"], w=["arg"], out=Av[:, :, i_, :], in0=Sv[:, :, i_, :], scalar=0.125,
                             in1=biasb[:], op0=ALU.mult, op1=ALU.add)
                    P.op("dve", "tensor_scalar", r=[sk], w=["arg"], out=arg[:, 16, :], in0=S_all[:, z, 16, :], scalar1=0.125, scalar2=None, op0=ALU.mult)
                    P.op("act", "activation", r=["arg"], w=[zk], out=Zp[:, 0, :, 0:8], in_=arg[:], func=AF.Exp)
                    P.op("dve", "tensor_scalar", r=[zk, "ident_f"], w=[zk], out=Zp[:, 0, 16, 0:8], in0=Zp[:, 0, 16, 0:8], scalar1=ident_f[:, 0:1],
                         scalar2=None, op0=ALU.mult)
                else:
                    af = arg[:, 0:16, :].rearrange("p a h -> p (a h)")
                    gf = Gs[:].rearrange("p a h -> p (a h)")
                    P.op("act", "activation", r=[sk], w=["arg"], out=af, in_=Sf, func=AF.Exp, scale=0.125)
                    P.op("act", "activation", r=["arg"], w=["Gs"], out=gf, in_=af, func=AF.Ln, bias=1.0)
                    P.op("pe", "matmul", r=["Gs", "tri_f"], w=[bkey(0)], out=banks[0][:, 0:128], lhsT=tri_f[:], rhs=gf, start=True, stop=True)
                    P.op("pe", "matmul", r=["Gs", "ones_f"], w=[bkey(1)], out=banks[1][:, 0:128], lhsT=ones_f[:], rhs=gf, start=True, stop=True)
                    P.op("act", "activation", r=[bkey(1)], w=["gate"], out=prod[:, 0, 0:128], in_=banks[1][:, 0:128], func=AF.Copy)
                    Tv = prod[:, 0, 0:128].rearrange("p (a h) -> p a h", h=8)
                    for pgi in range(14, -1, -1):
                        P.op("dve", "tensor_tensor", r=["gate", "carry_"], w=["carry_"], out=carry_[:, pgi, :], in0=carry_[:, pgi + 1, :],
                             in1=Tv[:, pgi + 1, :], op=ALU.add)
                    P.op("dve", "scalar_tensor_tensor", r=[sk, bkey(0)], w=["arg"], out=af, in0=Sf, scalar=0.125, in1=banks[0][:, 0:128],
                         op0=ALU.mult, op1=ALU.subtract)
                    P.op("dve", "tensor_tensor", r=["arg", "carry_"], w=["arg"], out=af, in0=af, in1=carry_[:].rearrange("p a h -> p (a h)"), op=ALU.subtract)
                    P.op("act", "activation", r=["arg"], w=[zk], out=Zp[:, 0, 0:16, 0:8], in_=arg[:, 0:16, :], func=AF.Exp)
                bn_, bd_ = 2 + (s % 2), 4
                for j in range(npg):
                    vap, vk = vsl[j]
                    P.op("pe", "matmul", r=[zk, vk], w=[bkey(bn_)], out=banks[bn_][:, :], lhsT=Zp[:, 0, j, :], rhs=vap,
                         start=(j == 0), stop=(j == npg - 1))
                if layer == 0:
                    for j in range(npg):
                        P.op("pe", "matmul", r=[zk, "ones_f"], w=[bkey(bd_)], out=banks[bd_][:, 0:64], lhsT=Zp[:, 0, j, :], rhs=ones_f[:, 0:64],
                             start=(j == 0), stop=(j == npg - 1))
                    P.op("dve", "reciprocal", r=[bkey(bd_)], w=["rden"], out=rden[0:8, 0:1], in_=banks[bd_][0:8, 0:1])
                    P.op("dve", "tensor_scalar", r=[bkey(bn_), "rden"], w=["Opad"], out=Opad[0:8, :], in0=banks[bn_][0:8, :], scalar1=rden[0:8, 0:1],
                         scalar2=None, op0=ALU.mult)
                else:
                    P.op("act", "activation", r=[bkey(bn_)], w=["Opad"], out=Opad[0:8, :], in_=banks[bn_][0:8, :], func=AF.Copy)
                bt_ = 5
                for c in range(4):
                    P.op("pe", "transpose", r=["Opad", "ident_f"], w=[bkey(bt_)], out=banks[bt_][:, c * 128:(c + 1) * 128],
                         in_=Opad[:, c * 128:(c + 1) * 128], identity=ident_f[:])
                for c in range(4):
                    P.op("dve", "tensor_copy", r=[bkey(bt_)], w=[tkey(16)], out=hT[0:64, c, T + s:T + s + 1],
                         in_=banks[bt_][0:64, c * 128 + 2 * c:c * 128 + 2 * c + 1])
                    P.op("act", "activation", r=[bkey(bt_)], w=[tkey(16)], out=hT[64:128, c, T + s:T + s + 1],
                         in_=banks[bt_][64:128, c * 128 + 2 * c + 1:c * 128 + 2 * c + 2], func=AF.Copy)

        def ffn(l):
            with ExitStack() as ph:
                walloc(ph, 3)
                aT = sb("aT%d" % l, [128, 4, TT], BF16, ph)
                gbuf = sb("gbuf%d" % l, [128, 2 + T], F32, ph)
                gss = sb("gss%d" % l, [128, 3, NS], F32, ph)
                ctmp = sb("ctmp%d" % l, [128, 2, 512], F32, ph)
                gel = sb("gel%d" % l, [128, 2, 512], BF16, ph)
                cst = sb("cst%d" % l, [NS, 2, 512], F32, ph)
                cvs = sb("cvs%d" % l, [NS, 2, 512], F32, ph)
                cvp = sb("cvp%d" % l, [2, 512], F32, ph)
                P.op("pool", "memset", w=["gbuf"], ap=gbuf[:, 0:2], constant=0.0)
                cst_v = conv_st[l].rearrange("s (r f) -> s r f", r=2)
                cso_v = conv_s[l].rearrange("s (r f) -> s r f", r=2)
                ei = 0
                for fg in range(6):
                    f0 = fg * 512
                    ncols = min(512, DFF - f0)
                    nf = ncols // 128
                    wg, wgk = load_w(w_gate[l][:, f0:f0 + ncols], ncols=ncols)
                    wu, wuk = load_w(w_up[l][:, f0:f0 + ncols], ncols=ncols)
                    P.dma(cst[:, :, 0:ncols], cst_v[:, :, f0:f0 + ncols], w=["cst"])
                    P.op("pool", "tensor_copy", r=["cst"], w=["cvs"], out=cvs[:, 0, 0:ncols], in_=cst[:, 1, 0:ncols])
                    for j in range(nf):
                        fc = fg * 4 + j
                        cw = [convc_t[:, l, i_, fc:fc + 1] for i_ in range(4)]
                        bks = nextbank(4, 6)
                        for r_ in range(2):
                            P.op("pe", "transpose", r=["cst", "ident_f"], w=[bkey(bks)], out=banks[bks][:, r_ * NS:(r_ + 1) * NS],
                                 in_=cst[:, r_, j * 128:(j + 1) * 128], identity=ident_f[0:NS, 0:NS])
                        P.op("dve", "tensor_copy", r=[bkey(bks)], w=["gss"], out=gss[:, 0:2, :].rearrange("p r s -> p (r s)"),
                             in_=banks[bks][:, 0:2 * NS])
                        for g in range(5):
                            cs = gcolsl(g)
                            n = cs.stop - cs.start
                            ba = nextbank(0, 4)
                            for kc in range(8):
                                P.op("pe", "matmul", r=hkeys(g) + [wgk], w=[bkey(ba)], out=banks[ba][:, 0:n],
                                     lhsT=wg[:, kc, j * 128:(j + 1) * 128], rhs=hT[:, kc, cs], start=(kc == 0), stop=(kc == 7))
                            bb = nextbank(0, 4)
                            for kc in range(8):
                                P.op("pe", "matmul", r=hkeys(g) + [wuk], w=[bkey(bb)], out=banks[bb][:, 0:n],
                                     lhsT=wu[:, kc, j * 128:(j + 1) * 128], rhs=hT[:, kc, cs], start=(kc == 0), stop=(kc == 7))
                            e_ = ei % 2
                            ei += 1
                            ck, gk = ("ctmp", e_), ("gel", e_)
                            if g < 4:
                                o = 2 + 512 * g
                                P.op("act", "activation", r=[bkey(ba)], w=["gbuf"], out=gbuf[:, o:o + 512], in_=banks[ba][:, 0:512], func=AF.Copy)
                                srcs = [gbuf[:, o - 2:o + 510], gbuf[:, o - 1:o + 511], gbuf[:, o:o + 512]]
                                sk_ = "gbuf"
                            else:
                                P.op("act", "activation", r=[bkey(ba)], w=["gss"], out=gss[:, 2, :], in_=banks[ba][:, 0:NS], func=AF.Copy)
                                srcs = [gss[:, 0, :], gss[:, 1, :], gss[:, 2, :]]
                                sk_ = "gss"
                            ct = ctmp[:, e_, 0:n]
                            P.op("dve", "tensor_scalar", r=[sk_, "convc"], w=[ck], out=ct, in0=srcs[2], scalar1=cw[2], scalar2=cw[3],
                                 op0=ALU.mult, op1=ALU.add)
                            P.op("dve", "scalar_tensor_tensor", r=[sk_, "convc", ck], w=[ck], out=ct, in0=srcs[1], scalar=cw[1], in1=ct,
                                 op0=ALU.mult, op1=ALU.add)
                            P.op("dve", "scalar_tensor_tensor", r=[sk_, "convc", ck], w=[ck], out=ct, in0=srcs[0], scalar=cw[0], in1=ct,
                                 op0=ALU.mult, op1=ALU.add)
                            P.op("act", "activation", r=[ck], w=[gk], out=gel[:, e_, 0:n], in_=ct, func=AF.Gelu_apprx_tanh)
                            P.op("dve", "tensor_tensor", r=[gk, bkey(bb)], w=[("aT", g)], out=aT[:, j, cs], in0=gel[:, e_, 0:n],
                                 in1=banks[bb][:, 0:n], op=ALU.mult)
                        bko = nextbank(4, 6)
                        P.op("pe", "transpose", r=["gbuf", "ident_f"], w=[bkey(bko)], out=banks[bko][0:2, 0:128],
                             in_=gbuf[:, T:T + 2], identity=ident_f[:])
                        P.op("pe", "transpose", r=["gss", "ident_f"], w=[bkey(bko)], out=banks[bko][0:NS, 128:256],
                             in_=gss[:, 2, :], identity=ident_f[:])
                        P.op("dve", "tensor_copy", r=[bkey(bko)], w=["cvp"], out=cvp[:, j * 128:(j + 1) * 128], in_=banks[bko][0:2, 0:128])
                        P.op("dve", "tensor_copy", r=[bkey(bko)], w=["cvs"], out=cvs[:, 1, j * 128:(j + 1) * 128], in_=banks[bko][0:NS, 128:256])
                    P.dma(conv_p[l][:, f0:f0 + ncols], cvp[:, 0:ncols], r=["cvp"])
                    P.dma(cso_v[:, :, f0:f0 + ncols], cvs[:, :, 0:ncols], r=["cvs"])
                    wd, wdk = load_w(w_down[l][f0:f0 + ncols, :], nk=nf, ncols=1024)
                    for dc in range(8):
                        for g in range(5):
                            cs = gcolsl(g)
                            n = cs.stop - cs.start
                            bk = nextbank(0, 4)
                            for j in range(nf):
                                P.op("pe", "matmul", r=[("aT", g), wdk], w=[bkey(bk)], out=banks[bk][:, 0:n],
                                     lhsT=wd[:, j, dc * 128:(dc + 1) * 128], rhs=aT[:, j, cs], start=(j == 0), stop=(j == nf - 1))
                            P.op("dve", "tensor_tensor", r=[bkey(bk), ("xT", g)], w=[("xT", g)], out=xT[:, dc, cs],
                                 in0=xT[:, dc, cs], in1=banks[bk][:, 0:n], op=ALU.add)
                P.barrier()

        rmsnorm(0)
        if STAGE >= 1:
            with ExitStack() as ph0:
                qs_tok = sb("qs_tok0", [NS, 512], F32, ph0)
                ks_tok = sb("ks_tok0", [NS, 512], F32, ph0)
                vs_tok = sb("vs_tok0", [NS, 512], F32, ph0)
                phA = ExitStack()
                QT = sb("QT0", [128, 4, T], BF16, phA)
                KT = sb("KT0", [128, 4, T], BF16, phA)
                Vt = sb("Vt0", [128, 16, 512], BF16, phA)
                with ExitStack() as ph:
                    walloc(ph, 2)
                    with ExitStack() as phq:
                        qkv_proj(phq, w_in0, 0, kv0_p, kv0_s, QT, KT, Vt, qs_tok, ks_tok, vs_tok)
                        P.barrier()
                    if STAGE >= 2:
                        sgw_f = sb("sgw_f", [128, 4, 128], F32, ph)
                        sgw_b = sb("sgw_b", [128, 4, 128], BF16, ph)
                        gain_t = sb("gain_t", [128, 512], F32, ph)
                        bcol_t = sb("bcol_t", [128, 4], F32, ph)
                        ssw = sb("ssw", [NS, 512], F32, ph)
                        ssb = sb("ssb", [NS, 512], F32, ph)
                        gst = sb("gst", [128, 2, 512], F32, ph)
                        gbf = sb("gbf", [128, 2, 512], BF16, ph)
                        ubf = sb("ubf", [128, 2, 512], BF16, ph)
                        fbt = sb("fbt", [128, 2, 512], BF16, ph)
                        bst = sb("bst", [128, 2, 8], F32, ph)
                        P.dma(sgw_f[:], sgu_wT[:, :, :], w=["sgw_f"])
                        P.op("pool", "affine_select", r=["sgw_f"], w=["sgw_b"], out=sgw_b[:], in_=sgw_f[:], pattern=[[0, 4], [1, 128]],
                             compare_op=ALU.is_ge, fill=0.0, base=0, channel_multiplier=-1)
                        P.dma(gain_t[:], sgu_gain_bc[:, :], w=["gain"])
                        P.dma(bcol_t[:], sgu_bcol[:, :], w=["bcol"])
                        P.dma(ssw[:], sgu_s_w[:, :], w=["ssw"])
                        P.dma(ssb[:], sgu_s_b[:, :], w=["ssw"])
                        wv_, wvk = load_w(w_in0[:, 2048:2560])
                        wu_, wuk = load_w(w_in0[:, 1536:2048])
                        for t in range(17):
                            rows = trows(t)
                            s = t % 2
                            bk = nextbank(0, 4)
                            for kc in range(8):
                                P.op("pe", "matmul", r=[tkey(t), wvk], w=[bkey(bk)], out=banks[bk][0:rows, :],
                                     lhsT=hT[:, kc, tcols(t)], rhs=wv_[:, kc, :], start=(kc == 0), stop=(kc == 7))
                            bu_ = nextbank(0, 4)
                            for kc in range(8):
                                P.op("pe", "matmul", r=[tkey(t), wuk], w=[bkey(bu_)], out=banks[bu_][0:rows, :],
                                     lhsT=hT[:, kc, tcols(t)], rhs=wu_[:, kc, :], start=(kc == 0), stop=(kc == 7))
                            gk = ("gst", s)
                            P.op("act", "activation", r=[bkey(bk)], w=[gk], out=gst[0:rows, s, :], in_=banks[bk][0:rows, :], func=AF.Gelu_apprx_tanh)
                            P.op("act", "activation", r=[bkey(bu_)], w=[("ubf", s)], out=ubf[0:rows, s, :], in_=banks[bu_][0:rows, :], func=AF.Gelu_apprx_tanh)
                            P.op("dve", "bn_stats", r=[gk], w=[("bst", s)], out=bst[0:rows, s, 0:6], in_=gst[0:rows, s, :])
                            P.op("dve", "bn_aggr", r=[("bst", s)], w=[("bst", s)], out=bst[0:rows, s, 6:8], in_=bst[0:rows, s, 0:6])
                            P.op("act", "activation", r=[("bst", s), "eps"], w=[("bst", s)], out=bst[0:rows, s, 7:8], in_=bst[0:rows, s, 7:8],
                                 func=AF.Sqrt, bias=eps_t[0:rows, 0:1])
                            P.op("dve", "reciprocal", r=[("bst", s)], w=[("bst", s)], out=bst[0:rows, s, 7:8], in_=bst[0:rows, s, 7:8])
                            P.op("dve", "tensor_scalar", r=[gk, ("bst", s)], w=[gk], out=gst[0:rows, s, :], in0=gst[0:rows, s, :],
                                 scalar1=bst[0:rows, s, 6:7], scalar2=bst[0:rows, s, 7:8], op0=ALU.subtract, op1=ALU.mult)
                            P.op("dve", "tensor_tensor", r=[gk, "gain"], w=[gk], out=gst[0:rows, s, :], in0=gst[0:rows, s, :],
                                 in1=gain_t[0:rows, :], op=ALU.mult)
                            fk = ("fbt", s)
                            if t == 16:
                                P.dma(sgu_s[:, :], gst[0:rows, s, :], r=[gk])
                                P.op("dve", "tensor_tensor", r=[gk, "ssw"], w=[gk], out=gst[0:rows, s, :], in0=gst[0:rows, s, :], in1=ssw[:], op=ALU.mult)
                                P.op("dve", "tensor_tensor", r=[gk, "ssw"], w=[gk], out=gst[0:rows, s, :], in0=gst[0:rows, s, :], in1=ssb[:], op=ALU.add)
                                P.op("dve", "tensor_tensor", r=[gk, ("ubf", s)], w=[fk], out=fbt[0:rows, s, :], in0=gst[0:rows, s, :],
                                     in1=ubf[0:rows, s, :], op=ALU.mult)
                            else:
                                P.op("pool", "tensor_copy", r=[gk], w=[("gbf", s)], out=gbf[:, s, :], in_=gst[:, s, :])
                                bf_ = nextbank(4, 6)
                                for g4 in range(4):
                                    P.op("pe", "matmul", r=[("gbf", s), "sgw_b"], w=[bkey(bf_)], out=banks[bf_][:, g4 * 128:(g4 + 1) * 128],
                                         lhsT=sgw_b[:, g4, :], rhs=gbf[:, s, g4 * 128:(g4 + 1) * 128], start=True, stop=True)
                                for g4 in range(4):
                                    P.op("dve", "scalar_tensor_tensor", r=[bkey(bf_), "bcol", ("ubf", s)], w=[fk],
                                         out=fbt[:, s, g4 * 128:(g4 + 1) * 128], in0=banks[bf_][:, g4 * 128:(g4 + 1) * 128],
                                         scalar=bcol_t[:, g4:g4 + 1], in1=ubf[:, s, g4 * 128:(g4 + 1) * 128], op0=ALU.add, op1=ALU.mult)
                            transpose_to_hT(fbt[0:rows, s, :], fk, rows, 4, t)
                    P.barrier()
                if STAGE >= 3:
                    with ExitStack() as ph:
                        attention(ph, 0, QT, KT, Vt)
                        P.barrier()
                P.barrier()
                phA.close()
                if STAGE >= 3 and SAMPLE:
                    with ExitStack() as ph:
                        sample_attention(ph, 0, cache0, qs_tok, kv0_s, qscr0)
                        P.barrier()
            if STAGE >= 4:
                with ExitStack() as ph:
                    walloc(ph, 2)
                    out_proj(w_out0)
                    P.barrier()
        if STAGE >= 5:
            rmsnorm(1)
            ffn(0)
        if STAGE >= 6:
            rmsnorm(2)
            with ExitStack() as ph:
                walloc(ph, 2)
                uT = sb("uT", [128, 4, 15 + T], F32, ph)
                pa = sb("pa", [128, 15 + T], F32, ph)
                pb = sb("pb", [128, 15 + T], F32, ph)
                pT_ = sb("pT_", [128, T], BF16, ph)
                dst_ = sb("dst_", [128, TT], BF16, ph)
                plw_f = sb("plw_f", [128, 4, 128], F32, ph)
                plw_b = sb("plw_b", [128, 4, 128], BF16, ph)
                pscol = sb("pscol", [128, 4], F32, ph)
                pfix = sb("pfix", [128, 64], F32, ph)
                us_tok = sb("us_tok", [NS, 512], F32, ph)
                pst = sb("pst", [NS, 15, 128], F32, ph)
                pls = sb("pls", [NS, 128], F32, ph)
                plsb = sb("plsb", [128, NS], BF16, ph)
                ppo = sb("ppo", [15, 512], F32, ph)
                P.dma(plw_f[:], pool_w[:, :, :], w=["plw_f"])
                P.op("pool", "tensor_copy", r=["plw_f"], w=["plw_b"], out=plw_b[:], in_=plw_f[:])
                P.dma(pscol[:], pool_scol[:, :], w=["pscol"])
                P.dma(pfix[:], pool_fix[:, :], w=["pfix"])
                P.op("pool", "memset", w=["uT"], ap=uT[:, :, 0:15], constant=0.0)
                P.op("pool", "memset", w=["pa"], ap=pa[:, 0:15], constant=0.0)
                P.op("pool", "memset", w=["pb"], ap=pb[:, 0:15], constant=0.0)
                wv_, wvk = load_w(w_in1[:, 1536:2048])
                bk = nextbank(0, 4)
                for kc in range(8):
                    P.op("pe", "matmul", r=[tkey(16), wvk], w=[bkey(bk)], out=banks[bk][0:NS, :], lhsT=hT[:, kc, T:TT], rhs=wv_[:, kc, :],
                         start=(kc == 0), stop=(kc == 7))
                P.op("act", "activation", r=[bkey(bk)], w=["us_tok"], out=us_tok[:], in_=banks[bk][0:NS, :], func=AF.Copy)
                P.dma(pool_s.rearrange("s (r c) -> s r c", r=15)[:, 14, :], us_tok[:], r=["us_tok"])
                pst_v = pool_st.rearrange("s (r c) -> s r c", r=15)
                pso_v = pool_s.rearrange("s (r c) -> s r c", r=15)
                for j in range(4):
                    w_ = (2, 4, 8, 16)[j]
                    for g in range(4):
                        bk = nextbank(0, 4)
                        for kc in range(8):
                            P.op("pe", "matmul", r=hkeys(g) + [wvk], w=[bkey(bk)], out=banks[bk][:, :], lhsT=wv_[:, kc, j * 128:(j + 1) * 128],
                                 rhs=hT[:, kc, gcolsl(g)], start=(kc == 0), stop=(kc == 7))
                        copy(evac_eng(), uT[:, j, 15 + 512 * g:15 + 512 * g + 512], banks[bk][:, :], r=[bkey(bk)], w=["uT"])
                    bko = nextbank(4, 6)
                    P.op("pe", "transpose", r=["uT", "ident_f"], w=[bkey(bko)], out=banks[bko][0:15, 0:128], in_=uT[:, j, T:T + 15],
                         identity=ident_f[:])
                    P.op("dve", "tensor_copy", r=[bkey(bko)], w=["ppo"], out=ppo[:, j * 128:(j + 1) * 128], in_=banks[bko][0:15, 0:128])
                    cur, ck_ = uT[:, j, :], "uT"
                    bufs = [(pa, "pa"), (pb, "pb")]
                    for st_ in range(j + 1):
                        sh = 1 << st_
                        nx, nk2 = bufs[st_ % 2]
                        P.op("dve", "tensor_tensor", r=[ck_], w=[nk2], out=nx[:, 15:15 + T], in0=cur[:, 15:15 + T], in1=cur[:, 15 - sh:15 + T - sh], op=ALU.add)
                        cur, ck_ = nx, nk2
                    oth, ok_ = bufs[(j + 1) % 2]
                    P.op("dve", "scalar_tensor_tensor", r=[ck_, "uT"], w=[ok_], out=oth[:, 15:15 + T], in0=cur[:, 15:15 + T], scalar=1.0 / w_,
                         in1=uT[:, j, 15:15 + T], op0=ALU.mult, op1=ALU.subtract)
                    P.op("dve", "tensor_tensor", r=[ck_, "pfix"], w=[ck_], out=cur[:, 15:31], in0=cur[:, 15:31], in1=pfix[:, j * 16:(j + 1) * 16], op=ALU.mult)
                    P.op("dve", "tensor_tensor", r=[ck_, "uT"], w=[ok_], out=oth[:, 15:31], in0=cur[:, 15:31], in1=uT[:, j, 15:31], op=ALU.subtract)
                    P.op("act", "activation", r=[ok_], w=["pT_"], out=pT_[:], in_=oth[:, 15:15 + T], func=AF.Copy)
                    for g in range(4):
                        bk = nextbank(0, 4)
                        P.op("pe", "matmul", r=["pT_", "plw_b"], w=[bkey(bk)], out=banks[bk][:, :], lhsT=plw_b[:, j, :], rhs=pT_[:, gcolsl(g)],
                             start=True, stop=True)
                        P.op("act", "activation", r=[bkey(bk), "pscol"], w=["dst_"], out=dst_[:, gcolsl(g)], in_=banks[bk][:, :], func=AF.Copy,
                             scale=pscol[:, j:j + 1])
                    P.dma(pst[:], pst_v[:, :, j * 128:(j + 1) * 128], w=["pst"])
                    P.dma(pso_v[:, 0:14, j * 128:(j + 1) * 128], pst[:, 1:15, :], r=["pst"])
                    P.op("dve", "tensor_reduce", r=["pst"], w=["pls"], out=pls[:], in_=pst[:, 16 - w_:15, :].rearrange("p r c -> p c r"),
                         axis=AX.X, op=ALU.add)
                    P.op("dve", "tensor_tensor", r=["pls", "us_tok"], w=["pls"], out=pls[:], in0=pls[:], in1=us_tok[:, j * 128:(j + 1) * 128], op=ALU.add)
                    P.op("dve", "scalar_tensor_tensor", r=["pls", "us_tok"], w=["pls"], out=pls[:], in0=pls[:], scalar=1.0 / w_,
                         in1=us_tok[:, j * 128:(j + 1) * 128], op0=ALU.mult, op1=ALU.subtract)
                    bko = nextbank(4, 6)
                    P.op("pe", "transpose", r=["pls", "ident_f"], w=[bkey(bko)], out=banks[bko][:, 0:NS], in_=pls[:], identity=ident_f[0:NS, 0:NS])
                    P.op("act", "activation", r=[bkey(bko)], w=["plsb"], out=plsb[:], in_=banks[bko][:, 0:NS], func=AF.Copy)
                    bk = nextbank(0, 4)
                    P.op("pe", "matmul", r=["plsb", "plw_b"], w=[bkey(bk)], out=banks[bk][:, 0:NS], lhsT=plw_b[:, j, :], rhs=plsb[:], start=True, stop=True)
                    P.op("act", "activation", r=[bkey(bk), "pscol"], w=["dst_"], out=dst_[:, T:TT], in_=banks[bk][:, 0:NS], func=AF.Copy,
                         scale=pscol[:, j:j + 1])
                    P.dma(dscr[:, j, :], dst_[:], r=["dst_"], w=["dscr"])
                P.dma(pool_p[:, :], ppo[:], r=["ppo"])
                P.barrier()
            with ExitStack() as ph1:
                qs1 = sb("qs_tok1", [NS, 512], F32, ph1)
                ks1 = sb("ks_tok1", [NS, 512], F32, ph1)
                vs1 = sb("vs_tok1", [NS, 512], F32, ph1)
                phB = ExitStack()
                QT = sb("QT1", [128, 4, T], BF16, phB)
                KT = sb("KT1", [128, 4, T], BF16, phB)
                Vt = sb("Vt1", [128, 16, 512], BF16, phB)
                with ExitStack() as ph:
                    walloc(ph, 2)
                    qkv_proj(ph, w_in1, 1, kv1_p, kv1_s, QT, KT, Vt, qs1, ks1, vs1)
                    P.barrier()
                P.dma(hT[:, 4:8, :], dscr[:, :, :], r=["dscr"], w=[tkey(t) for t in range(17)])
                if STAGE >= 7:
                    with ExitStack() as ph:
                        attention(ph, 1, QT, KT, Vt)
                        P.barrier()
                P.barrier()
                phB.close()
                if STAGE >= 7 and SAMPLE:
                    with ExitStack() as ph:
                        sample_attention(ph, 1, cache1, qs1, kv1_s, qscr1)
                        P.barrier()
            if STAGE >= 8:
                with ExitStack() as ph:
                    walloc(ph, 2)
                    out_proj(w_out1)
                    P.barrier()
        if STAGE >= 9:
            rmsnorm(3)
            ffn(1)
        if STAGE >= 10:
            with ExitStack() as ph:
                yT = sb("yT", [128, 8, 512], F32, ph)
                ytok = sb("ytok", [128, 2, D], F32, ph)
                for g in range(5):
                    ri, n, cs = rmsnorm_stats(g)
                    for c in range(8):
                        P.op("dve", "scalar_tensor_tensor", r=[("xT", g), ("rs", ri), "gcol"], w=["yT"],
                             out=yT[:, c, 0:n], in0=xT[:, c, cs], scalar=gcol_t[:, 4 * 8 + c:4 * 8 + c + 1],
                             in1=rs_t[:, ri, 0:n], op0=ALU.mult, op1=ALU.mult)
                    for tt, t in enumerate(gtiles(g)):
                        rows = trows(t)
                        s = t % 2
                        for half in range(2):
                            bk = nextbank(0, 4)
                            for j in range(4):
                                c = half * 4 + j
                                P.op("pe", "transpose", r=["yT", "ident_f"], w=[bkey(bk)], out=banks[bk][0:rows, j * 128:(j + 1) * 128],
                                     in_=yT[:, c, tt * 128:tt * 128 + rows], identity=ident_f[:])
                            copy(evac_eng(), ytok[0:rows, s, half * 512:(half + 1) * 512], banks[bk][0:rows, :], r=[bkey(bk)], w=[("ytok", s)])
                        dst = y_p[t * 128:(t + 1) * 128, :] if t < 16 else y_s[:, :]
                        P.dma(dst, ytok[0:rows, s, :], r=[("ytok", s)])
                P.barrier()

        P.finish()
    print("instructions:", P.nins, {k: v for k, v in P.cnt.items()})
    return nc


def _prep_inputs(inputs):
    f32 = np.float32
    g = lambda k: np.asarray(inputs[k])
    common = {}
    common["cache0"] = np.ascontiguousarray(g("cache_l0_kv"), dtype=f32).reshape(-1, 128, 1024)
    common["cache1"] = np.ascontiguousarray(g("cache_l1_kv"), dtype=f32).reshape(-1, 128, 1024)
    common["w_in0"] = np.ascontiguousarray(g("l0_w_in"), dtype=f32)
    common["w_out0"] = np.ascontiguousarray(g("l0_w_out"), dtype=f32)
    common["w_in1"] = np.ascontiguousarray(g("l1_w_in"), dtype=f32)
    common["w_out1"] = np.ascontiguousarray(g("l1_w_out"), dtype=f32)
    common["w_gate"] = np.ascontiguousarray(g("ffn_w_gate"), dtype=f32)
    common["w_up"] = np.ascontiguousarray(g("ffn_w_up"), dtype=f32)
    common["w_down"] = np.ascontiguousarray(g("ffn_w_down"), dtype=f32)
    common["sgu_wT"] = np.ascontiguousarray(g("l0_sgu_w").transpose(2, 0, 1), dtype=f32)
    common["pool_w"] = np.ascontiguousarray(g("l1_pool_w").transpose(1, 0, 2), dtype=f32)
    norms = np.stack([g("l0_norm"), g("ffn_norm")[0], g("l1_norm"), g("ffn_norm")[1], g("final_norm")], 0)
    common["gcols"] = np.ascontiguousarray(norms.reshape(5, 8, 128).transpose(2, 0, 1).reshape(128, 40), dtype=f32)
    cw = g("ffn_conv_w")
    cb = g("ffn_conv_b")
    cc = np.concatenate([cw, cb[:, None, :]], axis=1)
    common["convc"] = np.ascontiguousarray(cc.reshape(2, 4, NFC, 128).transpose(3, 0, 1, 2).reshape(128, 2 * 4 * NFC), dtype=f32)
    common["sgu_gain_bc"] = np.ascontiguousarray(np.broadcast_to(g("l0_sgu_gain")[None, :], (128, 512)), dtype=f32)
    common["sgu_bcol"] = np.ascontiguousarray(g("l0_sgu_b").T, dtype=f32)
    common["sgu_s_w"] = np.ascontiguousarray(np.broadcast_to(np.repeat(g("l0_sgu_w")[:, 0, 0], 128)[None, :], (NS, 512)), dtype=f32)
    common["sgu_s_b"] = np.ascontiguousarray(np.broadcast_to(np.repeat(g("l0_sgu_b")[:, 0], 128)[None, :], (NS, 512)), dtype=f32)
    common["pool_scol"] = np.ascontiguousarray(g("l1_pool_scale").reshape(4, 128).T, dtype=f32)
    fix = np.ones((4, 16), f32)
    for gi, w in enumerate((2, 4, 8, 16)):
        for t in range(16):
            fix[gi, t] = 1.0 / min(t + 1, w)
    common["pool_fix"] = np.ascontiguousarray(np.broadcast_to(fix.reshape(1, 64), (128, 64)), dtype=f32)
    half = 8
    inv_freq = (np.float32(500000.0) ** (-(np.arange(half, dtype=f32) * np.float32(2.0) / np.float32(16)))).astype(f32)
    pos = np.concatenate([np.arange(T), np.full(128, T)]).astype(f32)
    ang = (pos[:, None] * inv_freq[None, :]).astype(f32)
    common["ropec"] = np.ascontiguousarray(np.cos(ang).astype(f32).reshape(17, 128, 8).transpose(1, 0, 2))
    common["ropes"] = np.ascontiguousarray(np.sin(ang).astype(f32).reshape(17, 128, 8).transpose(1, 0, 2))
    maps = []
    xp = g("x_prompt")
    xs = g("x_sample")
    pt = g("page_table")
    ps_ = g("state_l1_pool")
    cs_ = g("state_ffn_conv")
    for c in range(8):
        m = dict(common)
        sl = slice(NS * c, NS * (c + 1))
        m["x_p"] = np.ascontiguousarray(xp[c], dtype=f32)
        m["x_s"] = np.ascontiguousarray(xs[sl, 0, :], dtype=f32)
        m["ptab"] = np.ascontiguousarray(pt[sl], dtype=np.int32)
        m["pool_st"] = np.ascontiguousarray(ps_[sl].reshape(NS, 15 * 512), dtype=f32)
        m["conv_st"] = np.ascontiguousarray(cs_[:, sl].reshape(2, NS, 2 * DFF), dtype=f32)
        maps.append(m)
    return maps


_NC = None


def kernel(**inputs):
    global _NC
    maps = _prep_inputs(inputs)
    if _NC is None:
        _NC = build(maps[0]["cache0"].shape[0])
    res = run_bass_kernel_spmd(_NC, maps, core_ids=list(range(8)))
    R = res.results
    f32 = np.float32
    cat = lambda k: np.stack([np.asarray(r[k], dtype=f32) for r in R], 0)
    y_p = cat("y_p")
    y_s = cat("y_s").reshape(128, 1, D)
    kv0_p = cat("kv0_p").reshape(8, T, 2, 8, 64)
    kv0_s = cat("kv0_s").reshape(128, 1, 2, 8, 64)
    sgu_s = cat("sgu_s").reshape(128, 1, 512)
    kv1_p = cat("kv1_p").reshape(8, T, 2, 8, 64)
    kv1_s = cat("kv1_s").reshape(128, 1, 2, 8, 64)
    pool_p = cat("pool_p")
    pool_s = cat("pool_s").reshape(128, 15, 512)
    conv_p = cat("conv_p").transpose(1, 0, 2, 3)
    conv_s = cat("conv_s").reshape(8, 2, NS, 2, DFF).transpose(1, 0, 2, 3, 4).reshape(2, 128, 2, DFF)
    return (y_p, y_s, kv0_p, kv0_s, sgu_s, kv1_p, kv1_s, pool_p, pool_s,
            np.ascontiguousarray(conv_p), np.ascontiguousarray(conv_s))
```

```python
import numpy as np
from contextlib import ExitStack
import concourse.bass as bass
import concourse.mybir as mybir
from concourse.bass_utils import run_bass_kernel_spmd

F32 = mybir.dt.float32
BF16 = mybir.dt.bfloat16
I32 = mybir.dt.int32
AF = mybir.ActivationFunctionType
ALU = mybir.AluOpType
AX = mybir.AxisListType

T = 2048
NS = 16
TT = T + NS
D = 1024
DFF = 2816
NFC = 22
EPS = 1e-6
NPHYS = 2560
NEG = -30000.0
STAGE = 99
SAMPLE = True


class Prog:
    def __init__(self, nc, es, ndma=20):
        self.nc = nc
        self.E = {"pe": nc.tensor, "act": nc.scalar, "dve": nc.vector, "pool": nc.gpsimd, "sp": nc.sync}
        self.sem = {k: es.enter_context(nc.semaphore("s_" + k)) for k in self.E}
        self.dsem = [es.enter_context(nc.semaphore("d%d" % i)) for i in range(ndma)]
        self.cnt = {k: 0 for k in self.E}
        self.seen = {k: {} for k in self.E}
        self.dval = [0] * ndma
        self.drr = 0
        self.lastw = {}
        self.readers = {}
        self.nins = 0
        self.log = []
        self._cur = None

    def _waits(self, eng, reads, writes, extra=()):
        need = {}

        def add(ev):
            k = ev[:2]
            if ev[2] > need.get(k, 0):
                need[k] = ev[2]

        for k in reads:
            if k in self.lastw:
                add(self.lastw[k])
        for k in writes:
            if k in self.lastw:
                add(self.lastw[k])
            for ev in self.readers.get(k, ()):
                add(ev)
        for ev in extra:
            add(ev)
        e = self.E[eng]
        for k, v in need.items():
            if k[0] == "e" and k[1] == eng and eng == "pe":
                continue
            if self.seen[eng].get(k, 0) >= v:
                continue
            self.seen[eng][k] = v
            s = self.sem[k[1]] if k[0] == "e" else self.dsem[k[1]]
            e.wait_ge(s, v)
            self.log.append((eng, 'wait', k, v))

    def _record(self, ev, reads, writes):
        for k in reads:
            self.readers.setdefault(k, []).append(ev)
        for k in writes:
            self.lastw[k] = ev
            self.readers[k] = []

    def op(self, eng, name, r=(), w=(), inc=True, **kw):
        self._waits(eng, r, w)
        ins = getattr(self.E[eng], name)(**kw)
        if inc:
            self.cnt[eng] += 1
            ins.then_inc(self.sem[eng], 1)
            self.log.append((eng, 'inc', ('e', eng), 1, name))
            ev = ("e", eng, self.cnt[eng])
        else:
            ev = ("e", eng, self.cnt[eng] + 1)
        self._record(ev, r, w)
        self.nins += 1
        return ins

    def dma(self, out, in_, r=(), w=(), eng="sp", **kw):
        i = self.drr
        self.drr = (self.drr + 1) % len(self.dsem)
        extra = [("d", i, self.dval[i])] if self.dval[i] else []
        self._waits(eng, r, w, extra)
        self.dval[i] += 16
        self.E[eng].dma_start(out=out, in_=in_, **kw).then_inc(self.dsem[i], 16)
        self.log.append((eng, 'inc', ('d', i), 16, 'dma'))
        self._record(("d", i, self.dval[i]), r, w)
        self.nins += 1

    def idma(self, out, in_, idx, r=(), w=()):
        eng = "pool"
        i = self.drr
        self.drr = (self.drr + 1) % len(self.dsem)
        extra = [("d", i, self.dval[i])] if self.dval[i] else []
        self._waits(eng, r, w, extra)
        self.dval[i] += 16
        self.E[eng].indirect_dma_start(out=out, out_offset=None, in_=in_,
                                       in_offset=bass.IndirectOffsetOnAxis(ap=idx, axis=0)).then_inc(self.dsem[i], 16)
        self.log.append((eng, 'inc', ('d', i), 16, 'idma'))
        self._record(("d", i, self.dval[i]), r, w)
        self.nins += 1

    def barrier(self):
        for eng, e in self.E.items():
            for o in self.E:
                if o != eng and self.cnt[o] > self.seen[eng].get(("e", o), 0):
                    self.seen[eng][("e", o)] = self.cnt[o]
                    e.wait_ge(self.sem[o], self.cnt[o])
                    self.log.append((eng, 'wait', ('e', o), self.cnt[o]))
            for i, v in enumerate(self.dval):
                if v > self.seen[eng].get(("d", i), 0):
                    self.seen[eng][("d", i)] = v
                    e.wait_ge(self.dsem[i], v)
                    self.log.append((eng, 'wait', ('d', i), v))
        self.lastw = {}
        self.readers = {}

    def finish(self):
        e = self.E["sp"]
        for i, v in enumerate(self.dval):
            if v:
                e.wait_ge(self.dsem[i], v)
        for o in self.E:
            if o != "sp" and self.cnt[o]:
                e.wait_ge(self.sem[o], self.cnt[o])


def build(NPHYS=NPHYS):
    nc = bass.Bass("TRN2", target_bir_lowering=False)

    def din(name, shape, dt=F32):
        return nc.dram_tensor(name, list(shape), dt, kind="ExternalInput").ap()

    def dout(name, shape):
        return nc.dram_tensor(name, list(shape), F32, kind="ExternalOutput").ap()

    x_p = din("x_p", [T, D])
    x_s = din("x_s", [NS, D])
    cache0 = din("cache0", [NPHYS, 128, 1024])
    cache1 = din("cache1", [NPHYS, 128, 1024])
    ptab = din("ptab", [NS, 16], I32)
    pool_st = din("pool_st", [NS, 15 * 512])
    conv_st = din("conv_st", [2, NS, 2 * DFF])
    w_in0 = din("w_in0", [D, 2560])
    w_out0 = din("w_out0", [D, D])
    w_in1 = din("w_in1", [D, 2048])
    w_out1 = din("w_out1", [D, D])
    w_gate = din("w_gate", [2, D, DFF])
    w_up = din("w_up", [2, D, DFF])
    w_down = din("w_down", [2, DFF, D])
    sgu_wT = din("sgu_wT", [128, 4, 128])
    pool_w = din("pool_w", [128, 4, 128])
    gcols = din("gcols", [128, 40])
    convc = din("convc", [128, 2 * 4 * NFC])
    sgu_gain_bc = din("sgu_gain_bc", [128, 512])
    sgu_bcol = din("sgu_bcol", [128, 4])
    sgu_s_w = din("sgu_s_w", [NS, 512])
    sgu_s_b = din("sgu_s_b", [NS, 512])
    pool_scol = din("pool_scol", [128, 4])
    pool_fix = din("pool_fix", [128, 4 * 16])
    ropec = din("ropec", [128, 17, 8])
    ropes = din("ropes", [128, 17, 8])

    y_p = dout("y_p", [T, D])
    y_s = dout("y_s", [NS, D])
    kv0_p = dout("kv0_p", [T, 1024])
    kv0_s = dout("kv0_s", [NS, 1024])
    sgu_s = dout("sgu_s", [NS, 512])
    kv1_p = dout("kv1_p", [T, 1024])
    kv1_s = dout("kv1_s", [NS, 1024])
    pool_p = dout("pool_p", [15, 512])
    pool_s = dout("pool_s", [NS, 15 * 512])
    conv_p = dout("conv_p", [2, 2, DFF])
    conv_s = dout("conv_s", [2, NS, 2 * DFF])
    dscr = nc.dram_tensor("dscr", [128, 4, TT], BF16, kind="Internal").ap()
    qscr0 = nc.dram_tensor("qscr0", [NS, 512], F32, kind="Internal").ap()
    qscr1 = nc.dram_tensor("qscr1", [NS, 512], F32, kind="Internal").ap()

    es = ExitStack()
    with es:
        P = Prog(nc, es)

        def sb(name, shape, dt=F32, stack=es):
            return stack.enter_context(nc.sbuf_tensor(name, list(shape), dt))

        banks = [es.enter_context(nc.psum_tensor("ps%d" % i, [128, 512], F32)) for i in range(8)]
        banksb = [b.bitcast(BF16) for b in banks]

        def bkey(i):
            return ("ps", i)

        xT = sb("xT", [128, 8, TT])
        hT = sb("hT", [128, 8, TT], BF16)
        ident_b = sb("ident_b", [128, 128], BF16)
        ident_f = sb("ident_f", [128, 128])
        ones_b = sb("ones_b", [128, 128], BF16)
        tri_b = sb("tri_b", [128, 128], BF16)
        tris_b = sb("tris_b", [128, 128], BF16)
        tris_f = sb("tris_f", [128, 128])
        gcol_t = sb("gcol_t", [128, 40])
        convc_t = sb("convc_t", [128, 2, 4, NFC])
        ropec_t = sb("ropec_t", [128, 17, 8])
        ropes_t = sb("ropes_t", [128, 17, 8])
        rs_t = sb("rs_t", [128, 2, 512])
        sq_t = sb("sq_t", [128, 4, 512], BF16)
        eps_t = sb("eps_t", [128, 1])

        def tkey(t):
            return ("hT", t)

        def gtiles(g):
            return list(range(4 * g, 4 * g + 4)) if g < 4 else [16]

        def gcolsl(g):
            return slice(512 * g, 512 * g + 512) if g < 4 else slice(T, TT)

        def tcols(t):
            return slice(128 * t, 128 * t + 128) if t < 16 else slice(T, TT)

        def trows(t):
            return 128 if t < 16 else NS

        def hkeys(g):
            return [tkey(t) for t in gtiles(g)]

        for tl in (ident_b, ident_f):
            P.op("pool", "memset", w=[tl.name], ap=tl[:], constant=1.0)
            P.op("pool", "affine_select", r=[tl.name], w=[tl.name], out=tl[:], in_=tl[:], pattern=[[-1, 128]],
                 compare_op=ALU.is_equal, fill=0.0, base=0, channel_multiplier=1)
        P.op("pool", "memset", w=["ones_b"], ap=ones_b[:], constant=1.0)
        P.op("pool", "memset", w=["tri_b"], ap=tri_b[:], constant=1.0)
        P.op("pool", "affine_select", r=["tri_b"], w=["tri_b"], out=tri_b[:], in_=tri_b[:], pattern=[[-1, 128]],
             compare_op=ALU.is_ge, fill=0.0, base=0, channel_multiplier=1)
        for tl in (tris_b, tris_f):
            P.op("pool", "memset", w=[tl.name], ap=tl[:], constant=1.0)
            P.op("pool", "affine_select", r=[tl.name], w=[tl.name], out=tl[:], in_=tl[:], pattern=[[-1, 128]],
                 compare_op=ALU.is_gt, fill=0.0, base=0, channel_multiplier=1)
        P.op("pool", "memset", w=["eps"], ap=eps_t[:], constant=EPS)
        P.dma(gcol_t[:], gcols[:, :], w=["gcol"])
        P.dma(convc_t[:].rearrange("p a b c -> p (a b c)"), convc[:, :], w=["convc"])
        P.dma(ropec_t[:], ropec[:, :, :], w=["rope"])
        P.dma(ropes_t[:], ropes[:, :, :], w=["rope"])

        rr = {"wst": 0, "wbf": 0, "sq": 0, "ev": 0, "bk": 0}
        W = {}

        def evac_eng():
            rr["ev"] ^= 1
            return "act" if rr["ev"] else "dve"

        def copy(eng, out, in_, r, w):
            if eng == "act":
                P.op("act", "activation", r=r, w=w, out=out, in_=in_, func=AF.Copy)
            else:
                P.op(eng, "tensor_copy", r=r, w=w, out=out, in_=in_)

        def nextbank(lo, hi):
            b = lo + rr["bk"] % (hi - lo)
            rr["bk"] += 1
            return b

        def walloc(stack, nb):
            W["wst"] = sb("wst%d" % P.nins, [128, 2, 512], F32, stack)
            W["wbf"] = sb("wbf%d" % P.nins, [128, nb, 8 * 512], BF16, stack)
            W["nb"] = nb
            rr["wbf"] = 0

        def load_w(src, nk=8, ncols=512):
            b = rr["wbf"]
            rr["wbf"] = (b + 1) % W["nb"]
            key = ("wbf", b)
            view = W["wbf"][:, b, 0:nk * ncols].rearrange("p (k c) -> p k c", k=nk)
            for k in range(nk):
                for c0 in range(0, ncols, 512):
                    n = min(512, ncols - c0)
                    s = rr["wst"]
                    rr["wst"] ^= 1
                    P.dma(W["wst"][:, s, 0:n], src[k * 128:(k + 1) * 128, c0:c0 + n], w=[("wst", s)])
                    P.op("pool", "tensor_copy", r=[("wst", s)], w=[key], out=view[:, k, c0:c0 + n], in_=W["wst"][:, s, 0:n])
            return view, key

        def rmsnorm_stats(g):
            cs = gcolsl(g)
            n = cs.stop - cs.start
            bk = 6 + (g % 2)
            for c in range(8):
                s = rr["sq"]
                rr["sq"] = (s + 1) % 4
                P.op("act", "activation", r=[("xT", g)], w=[("sq", s)], out=sq_t[:, s, 0:n], in_=xT[:, c, cs], func=AF.Square)
                P.op("pe", "matmul", r=[("sq", s), "ones_b"], w=[bkey(bk)], out=banks[bk][:, 0:n],
                     lhsT=ones_b[:], rhs=sq_t[:, s, 0:n], start=(c == 0), stop=(c == 7))
            ri = g % 2
            P.op("act", "activation", r=[bkey(bk), "eps"], w=[("rs", ri)], out=rs_t[:, ri, 0:n], in_=banks[bk][:, 0:n],
                 func=AF.Sqrt, scale=1.0 / D, bias=eps_t[:, 0:1])
            P.op("dve", "reciprocal", r=[("rs", ri)], w=[("rs", ri)], out=rs_t[:, ri, 0:n], in_=rs_t[:, ri, 0:n])
            return ri, n, cs

        def rmsnorm(nidx):
            for g in range(5):
                ri, n, cs = rmsnorm_stats(g)
                for c in range(8):
                    P.op("dve", "scalar_tensor_tensor", r=[("xT", g), ("rs", ri), "gcol"], w=hkeys(g),
                         out=hT[:, c, cs], in0=xT[:, c, cs], scalar=gcol_t[:, nidx * 8 + c:nidx * 8 + c + 1],
                         in1=rs_t[:, ri, 0:n], op0=ALU.mult, op1=ALU.mult)

        def transpose_to_hT(src_tile, src_key, rows, c0, t, eng=None):
            bk = nextbank(4, 6)
            for j in range(4):
                P.op("pe", "transpose", r=[src_key, "ident_b"], w=[bkey(bk)], out=banksb[bk][:, j * 128:j * 128 + rows],
                     in_=src_tile[:, j * 128:(j + 1) * 128], identity=ident_b[0:rows, 0:rows])
            copy(eng or evac_eng(), hT[:, c0:c0 + 4, tcols(t)],
                 banksb[bk][:, 0:512].rearrange("p (j q) -> p j q", j=4)[:, :, 0:rows], r=[bkey(bk)], w=[tkey(t)])

        def out_proj(wsrc):
            for dcg in range(2):
                wv, wk = load_w(wsrc[:, dcg * 512:(dcg + 1) * 512])
                for j in range(4):
                    dc = dcg * 4 + j
                    for g in range(5):
                        cs = gcolsl(g)
                        n = cs.stop - cs.start
                        bk = nextbank(0, 4)
                        for kc in range(8):
                            P.op("pe", "matmul", r=hkeys(g) + [wk], w=[bkey(bk)], out=banks[bk][:, 0:n],
                                 lhsT=wv[:, kc, j * 128:(j + 1) * 128], rhs=hT[:, kc, cs], start=(kc == 0), stop=(kc == 7))
                        P.op("dve", "tensor_tensor", r=[bkey(bk), ("xT", g)], w=[("xT", g)], out=xT[:, dc, cs],
                             in0=xT[:, dc, cs], in1=banks[bk][:, 0:n], op=ALU.add)

        with ExitStack() as ph:
            xin = sb("xin", [128, 2, D], stack=ph)
            for t in range(17):
                rows = trows(t)
                s = t % 2
                src = x_p[t * 128:(t + 1) * 128, :] if t < 16 else x_s[:, :]
                P.dma(xin[0:rows, s, :], src, w=[("xin", s)])
                for half in range(2):
                    bk = (2 * t + half) % 4
                    for j in range(4):
                        c = half * 4 + j
                        P.op("pe", "transpose", r=[("xin", s), "ident_f"], w=[bkey(bk)],
                             out=banks[bk][:, j * 128:j * 128 + rows], in_=xin[0:rows, s, c * 128:(c + 1) * 128],
                             identity=ident_f[0:rows, 0:rows])
                    copy(evac_eng(), xT[:, half * 4:half * 4 + 4, tcols(t)],
                         banks[bk][:].rearrange("p (j q) -> p j q", j=4)[:, :, 0:rows], r=[bkey(bk)], w=[("xT", min(t // 4, 4))])
            P.barrier()

        def qkv_proj(ph, wsrc, layer, kvp, kvs, QT, KT, Vt, qs_tok, ks_tok, vs_tok):
            stg = sb("stg%d" % layer, [128, 2, 512], F32, ph)
            bfs = sb("bfs%d" % layer, [128, 2, 512], BF16, ph)
            rt = sb("rt%d" % layer, [128, 4, 64], F32, ph)
            for cg in range(3):
                wv, wk = load_w(wsrc[:, cg * 512:(cg + 1) * 512])
                for t in range(17):
                    rows = trows(t)
                    bk = nextbank(0, 4)
                    s = t % 2
                    for kc in range(8):
                        P.op("pe", "matmul", r=[tkey(t), wk], w=[bkey(bk)], out=banks[bk][0:rows, :],
                             lhsT=hT[:, kc, tcols(t)], rhs=wv[:, kc, :], start=(kc == 0), stop=(kc == 7))
                    sk = ("stg", s)
                    P.op("act", "activation", r=[bkey(bk)], w=[sk], out=stg[0:rows, s, :], in_=banks[bk][0:rows, :], func=AF.Copy)
                    if layer == 0 and cg < 2:
                        X = stg[0:rows, s, :].rearrange("p (h d) -> p h d", h=8)
                        x1 = X[:, :, 0:8]
                        x2 = X[:, :, 8:16]
                        ca_ = ropec_t[0:rows, t, :]
                        sa_ = ropes_t[0:rows, t, :]
                        cb = bass.AP(ropec_t, ca_.offset, [list(ca_.ap[0]), [0, 8], [1, 8]])
                        sbb = bass.AP(ropes_t, sa_.offset, [list(sa_.ap[0]), [0, 8], [1, 8]])
                        tv = [rt[0:rows, i, :].rearrange("p (h d) -> p h d", h=8) for i in range(4)]
                        P.op("dve", "tensor_tensor", r=[sk, "rope"], w=["rt0"], out=tv[0], in0=x1, in1=cb, op=ALU.mult)
                        P.op("dve", "tensor_tensor", r=[sk, "rope"], w=["rt1"], out=tv[1], in0=x2, in1=sbb, op=ALU.mult)
                        P.op("dve", "tensor_tensor", r=[sk, "rope"], w=["rt2"], out=tv[2], in0=x2, in1=cb, op=ALU.mult)
                        P.op("dve", "tensor_tensor", r=[sk, "rope"], w=["rt3"], out=tv[3], in0=x1, in1=sbb, op=ALU.mult)
                        P.op("dve", "tensor_tensor", r=["rt0", "rt1"], w=[sk], out=x1, in0=tv[0], in1=tv[1], op=ALU.subtract)
                        P.op("dve", "tensor_tensor", r=["rt2", "rt3"], w=[sk], out=x2, in0=tv[2], in1=tv[3], op=ALU.add)
                    if cg >= 1:
                        dst = (kvp[t * 128:(t + 1) * 128, (cg - 1) * 512:cg * 512] if t < 16 else kvs[:, (cg - 1) * 512:cg * 512])
                        P.dma(dst, stg[0:rows, s, :], r=[sk])
                    if t == 16:
                        tok = (qs_tok, ks_tok, vs_tok)[cg]
                        P.op("dve", "tensor_copy", r=[sk], w=[tok.name], out=tok[:], in_=stg[0:rows, s, :])
                        continue
                    if cg == 2:
                        P.op("pool", "tensor_copy", r=[sk], w=[("Vt", t)], out=Vt[:, t, :], in_=stg[:, s, :])
                    else:
                        bkk = ("bfs", s)
                        P.op("pool", "tensor_copy", r=[sk], w=[bkk], out=bfs[:, s, :], in_=stg[:, s, :])
                        dstT = QT if cg == 0 else KT
                        bk2 = nextbank(4, 6)
                        for j in range(4):
                            P.op("pe", "transpose", r=[bkk, "ident_b"], w=[bkey(bk2)], out=banksb[bk2][:, j * 128:(j + 1) * 128],
                                 in_=bfs[:, s, j * 128:(j + 1) * 128], identity=ident_b[:])
                        copy("dve", dstT[:, :, tcols(t)], banksb[bk2][:, 0:512].rearrange("p (j q) -> p j q", j=4),
                             r=[bkey(bk2)], w=[("QT" if cg == 0 else "KT", t)])

        def attention(ph, layer, QT, KT, Vt):
            pexp = sb("pexp%d" % layer, [128, 2, T], BF16, ph)
            PTs = sb("PTs%d" % layer, [128, 4, 512], BF16, ph)
            atok = sb("atok%d" % layer, [128, 2, 512], BF16, ph)
            rsum = sb("rsum%d" % layer, [128, 2, 16], F32, ph)
            rtot = sb("rtot%d" % layer, [128, 2, 2], F32, ph)
            dtmp = sb("dtmp%d" % layer, [128, 2, 128], BF16, ph)
            if layer == 0:
                kms = sb("kms", [128, 4, 8], F32, ph)
                kmT = sb("kmT", [128, 4, 8], BF16, ph)
                gsb = sb("gsb", [128, 8, 8], F32, ph)
                m8 = sb("m8", [128, 8, 8], F32, ph)
                bias_t = sb("bias_t", [128, 2, 64], F32, ph)
                for c in range(4):
                    P.op("dve", "tensor_reduce", r=[("KT", t) for t in range(16)], w=["kms"], out=kms[:, c, :],
                         in_=KT[:, c, :].rearrange("p (n s) -> p n s", s=256), axis=AX.X, op=ALU.add)
                P.op("act", "activation", r=["kms"], w=["kmT"], out=kmT[:], in_=kms[:], func=AF.Copy, scale=1.0 / 256)
                KMb = sb("KMb", [128, 4, 64], BF16, ph)
                P.op("pool", "memset", w=["KMb"], ap=KMb[:], constant=0.0)
                for c in range(4):
                    P.op("dve", "tensor_copy", r=["kmT"], w=["KMb"], out=KMb[0:64, c, (2 * c) * 8:(2 * c) * 8 + 8], in_=kmT[0:64, c, :])
                    P.op("dve", "tensor_copy", r=["kmT"], w=["KMb"], out=KMb[64:128, c, (2 * c + 1) * 8:(2 * c + 1) * 8 + 8], in_=kmT[64:128, c, :])
            else:
                sprow = sb("sprow", [128, 1 + T], F32, ph)
                csx = sb("csx", [128, T], F32, ph)
                etmp = sb("etmp", [128, 2, 512], F32, ph)
                negT = sb("negT", [128, 2], F32, ph)
                carry = sb("carry", [128, 8], F32, ph)
                P.op("pool", "memset", w=["sprow"], ap=sprow[:, 0:1], constant=0.0)
            pi = 0
            for i in range(16):
                G = i // 2
                qk = [("QT", i)]
                W_ = 128 * (i + 1)
                if layer == 0:
                    bsl = i % 2
                    bkg = 6
                    for c in range(4):
                        P.op("pe", "matmul", r=qk + ["KMb"], w=[bkey(bkg)], out=banks[bkg][:, 0:64],
                             lhsT=QT[:, c, tcols(i)], rhs=KMb[:, c, :], start=(c == 0), stop=(c == 3))
                    if G >= 4:
                        P.op("act", "activation", r=[bkey(bkg)], w=["gsb"], out=gsb[:].rearrange("p h n -> p (h n)"),
                             in_=banks[bkg][:, 0:64], func=AF.Copy)
                        if G < 8:
                            P.op("pool", "memset", r=[], w=["gsb"], ap=gsb[:, :, G:8], constant=-1e30)
                        for h in range(8):
                            P.op("dve", "max", r=["gsb"], w=["m8"], out=m8[:, h, :], in_=gsb[:, h, :])
                        for h in range(8):
                            P.op("dve", "tensor_scalar", r=["gsb", "m8"], w=[("bias", bsl)], out=bias_t[:, bsl, h * 8:(h + 1) * 8],
                                 in0=gsb[:, h, :], scalar1=m8[:, h, 2:3], scalar2=NEG, op0=ALU.is_lt, op1=ALU.mult)
                    else:
                        P.op("pool", "memset", w=[("bias", bsl)], ap=bias_t[:, bsl, :], constant=0.0)
                for h in range(8):
                    c, po = h // 2, (h % 2) * 64
                    ps_ = pi % 2
                    pi += 1
                    pk = ("pexp", ps_)
                    kall = [("KT", t) for t in range(i + 1)]
                    nsl = 0
                    if layer == 0:
                        for kc in range(i + 1):
                            bk = nextbank(0, 4)
                            P.op("pe", "matmul", r=qk + kall, w=[bkey(bk)], out=banks[bk][:, 0:128],
                                 lhsT=QT[po:po + 64, c, tcols(i)], rhs=KT[po:po + 64, c, kc * 128:(kc + 1) * 128], start=True, stop=True)
                            if kc == i:
                                ds_ = ps_
                                P.op("act", "activation", r=[bkey(bk)], w=[("dtmp", ds_)], out=dtmp[:, ds_, :],
                                     in_=banks[bk][:, 0:128], func=AF.Exp, scale=0.125)
                                P.op("dve", "tensor_tensor", r=[("dtmp", ds_), "tri_b"], w=[pk], out=pexp[:, ps_, W_ - 128:W_],
                                     in0=dtmp[:, ds_, :], in1=tri_b[:], op=ALU.mult)
                            elif kc // 2 < G:
                                nb_ = kc // 2
                                P.op("act", "activation", r=[bkey(bk), ("bias", bsl)], w=[pk],
                                     out=pexp[:, ps_, kc * 128:(kc + 1) * 128], in_=banks[bk][:, 0:128], func=AF.Exp,
                                     scale=0.125, bias=bias_t[:, bsl, h * 8 + nb_:h * 8 + nb_ + 1])
                            else:
                                P.op("act", "activation", r=[bkey(bk)], w=[pk], out=pexp[:, ps_, kc * 128:(kc + 1) * 128],
                                     in_=banks[bk][:, 0:128], func=AF.Exp, scale=0.125)
                        P.op("dve", "reduce_sum", r=[pk], w=[("rtot", ps_)], out=rtot[:, ps_, 0:1],
                             in_=pexp[:, ps_, 0:W_], axis=AX.X)
                        P.op("dve", "reciprocal", r=[("rtot", ps_)], w=[("rtot", ps_)], out=rtot[:, ps_, 1:2], in_=rtot[:, ps_, 0:1])
                    else:
                        for s0 in range(0, W_, 512):
                            n = min(512, W_ - s0)
                            bk = nextbank(0, 4)
                            es_ = (s0 // 512) % 2
                            P.op("pe", "matmul", r=qk + kall, w=[bkey(bk)], out=banks[bk][:, 0:n],
                                 lhsT=QT[po:po + 64, c, tcols(i)], rhs=KT[po:po + 64, c, s0:s0 + n], start=True, stop=True)
                            P.op("act", "activation", r=[bkey(bk)], w=[("etmp", es_)], out=etmp[:, es_, 0:n], in_=banks[bk][:, 0:n],
                                 func=AF.Exp, scale=0.125)
                            P.op("act", "activation", r=[("etmp", es_)], w=["sprow"], out=sprow[:, 1 + s0:1 + s0 + n],
                                 in_=etmp[:, es_, 0:n], func=AF.Ln, bias=1.0)
                            if s0 + n == W_:
                                P.op("pool", "tensor_tensor", r=["sprow", "tris_f"], w=["sprow"], out=sprow[:, 1 + W_ - 128:1 + W_],
                                     in0=sprow[:, 1 + W_ - 128:1 + W_], in1=tris_f[:], op=ALU.mult)
                            si_ = s0 // 512
                            init = 0.0 if s0 == 0 else carry[:, si_ - 1:si_]
                            P.op("dve", "tensor_tensor_scan", r=["sprow", "csx", "carry"], w=["csx"], out=csx[:, s0:s0 + n],
                                 data0=sprow[:, s0:s0 + n], data1=sprow[:, s0:s0 + n], initial=init, op0=ALU.add, op1=ALU.bypass)
                            P.op("dve", "tensor_copy", r=["csx"], w=["carry"], out=carry[:, si_:si_ + 1], in_=csx[:, s0 + n - 1:s0 + n])
                            if s0 + n == W_:
                                P.op("dve", "tensor_scalar", r=["csx"], w=["negT"], out=negT[:, 0:1], in0=csx[:, W_ - 1:W_],
                                     scalar1=-1.0, scalar2=None, op0=ALU.mult)
                            P.op("dve", "scalar_tensor_tensor", r=[bkey(bk), "csx"], w=["csx"], out=csx[:, s0:s0 + n],
                                 in0=banks[bk][:, 0:n], scalar=0.125, in1=csx[:, s0:s0 + n], op0=ALU.mult, op1=ALU.add)
                        for s0 in range(0, W_, 512):
                            n = min(512, W_ - s0)
                            P.op("act", "activation", r=["csx", "negT"], w=[pk], out=pexp[:, ps_, s0:s0 + n], in_=csx[:, s0:s0 + n],
                                 func=AF.Exp, bias=negT[:, 0:1])
                        P.op("pool", "tensor_tensor", r=[pk, "tris_b"], w=[pk], out=pexp[:, ps_, W_ - 128:W_],
                             in0=pexp[:, ps_, W_ - 128:W_], in1=tris_b[:], op=ALU.mult)
                    bo = 6 + (pi % 2) if layer == 1 else 7
                    for k0 in range(0, i + 1, 4):
                        nk_ = min(4, i + 1 - k0)
                        bk = nextbank(4, 6)
                        pts = (k0 // 4) % 4
                        for j in range(nk_):
                            P.op("pe", "transpose", r=[pk, "ident_b"], w=[bkey(bk)], out=banksb[bk][:, j * 128:(j + 1) * 128],
                                 in_=pexp[:, ps_, (k0 + j) * 128:(k0 + j + 1) * 128], identity=ident_b[:])
                        copy(evac_eng(), PTs[:, pts, 0:nk_ * 128], banksb[bk][:, 0:nk_ * 128], r=[bkey(bk)], w=[("PTs", pts)])
                        for j in range(nk_):
                            kc = k0 + j
                            P.op("pe", "matmul", r=[("PTs", pts), ("Vt", kc)], w=[bkey(bo)], out=banks[bo][:, h * 64:(h + 1) * 64],
                                 lhsT=PTs[:, pts, j * 128:(j + 1) * 128], rhs=Vt[:, kc, h * 64:(h + 1) * 64],
                                 start=(kc == 0), stop=(kc == i))
                    asl = i % 2
                    if layer == 0:
                        P.op("act", "activation", r=[bkey(bo), ("rtot", ps_)], w=[("atok", asl)], out=atok[:, asl, h * 64:(h + 1) * 64],
                             in_=banks[bo][:, h * 64:(h + 1) * 64], func=AF.Copy, scale=rtot[:, ps_, 1:2])
                    else:
                        P.op("act", "activation", r=[bkey(bo)], w=[("atok", asl)], out=atok[:, asl, h * 64:(h + 1) * 64],
                             in_=banks[bo][:, h * 64:(h + 1) * 64], func=AF.Copy)
                transpose_to_hT(atok[:, i % 2, :], ("atok", i % 2), 128, 0, i)

        def sample_attention(ph, layer, cache, qtok, kvs_dram, qscr):
            L = "s%d" % layer
            NV = 26
            NK = 6
            vbuf = sb("vbuf" + L, [128, NV, 512], F32, ph)
            kbuf = sb("kbuf" + L, [128, NK, 512], F32, ph)
            ids_i = sb("ids_i" + L, [128, 16], I32, ph)
            idf = sb("idf" + L, [128, 16], F32, ph)
            idx = sb("idx" + L, [128, 2, 2, 16], I32, ph)
            idf2 = sb("idf2" + L, [128, 2, 16], F32, ph)
            pcol = sb("pcol" + L, [128, 1], F32, ph)
            pci = sb("pci" + L, [128, 1], I32, ph)
            rows = cache.rearrange("n t (two c) -> (n t two) c", two=2)
            P.op("pool", "iota", w=["pci"], out=pci[:], pattern=[[0, 1]], base=0, channel_multiplier=1)
            P.op("pool", "tensor_copy", r=["pci"], w=["pcol"], out=pcol[:], in_=pci[:])
            qb = sb("qb" + L, [128, 2, 512], F32, ph)
            prod = sb("prod" + L, [128, 2, 512], F32, ph)
            S_all = sb("S_all" + L, [128, 2, 17, 8], F32, ph)
            Gs = sb("Gs" + L, [128, 16, 8], F32, ph)
            gate = sb("gate" + L, [128, 8, 8], F32, ph)
            m8s = sb("m8s" + L, [128, 8, 8], F32, ph)
            biasb = sb("biasb" + L, [128, 8, 8], F32, ph)
            arg = sb("arg" + L, [128, 17, 8], F32, ph)
            carry_ = sb("carry_" + L, [128, 16, 8], F32, ph)
            Zp = sb("Zp" + L, [128, 1, 17, 128], F32, ph)
            Opad = sb("Opad" + L, [128, 512], F32, ph)
            kself = sb("kself" + L, [128, 512], F32, ph)
            vself = sb("vself" + L, [128, 512], F32, ph)
            rden = sb("rden" + L, [128, 2], F32, ph)
            ones_f = sb("ones_f" + L, [128, 128], F32, ph)
            tri_f = sb("tri_f" + L, [128, 128], F32, ph)
            P.op("pool", "memset", w=["Zp"], ap=Zp[:], constant=0.0)
            P.op("pool", "memset", w=["Opad"], ap=Opad[:], constant=0.0)
            P.op("pool", "memset", w=["kself"], ap=kself[:], constant=0.0)
            P.op("pool", "memset", w=["vself"], ap=vself[:], constant=0.0)
            P.op("pool", "memset", w=["ones_f"], ap=ones_f[:], constant=1.0)
            P.op("pool", "memset", w=["tri_f"], ap=tri_f[:], constant=1.0)
            P.op("pool", "affine_select", r=["tri_f"], w=["tri_f"], out=tri_f[:], in_=tri_f[:], pattern=[[-1, 128]],
                 compare_op=ALU.is_ge, fill=0.0, base=0, channel_multiplier=1)
            P.op("pool", "memset", w=["carry_"], ap=carry_[:], constant=0.0)
            P.dma(qscr[:, :], qtok[:], r=[qtok.name], w=["qscr"])
            npg = 17 if layer == 0 else 16
            cnt = {"k": 0, "v": 0, "r": 0}
            VS = {}
            def scores(s):
                z = s % 2
                P.dma(qb[:, z, :], bass.AP(qscr.tensor, s * 512, [[0, 128], [1, 512]]), r=["qscr"], w=[("qb", z)])
                P.dma(ids_i[:], bass.AP(ptab.tensor, s * 16, [[0, 128], [1, 16]]), w=["ids_i"])
                if layer == 0:
                    P.dma(kself[0:1, :], kvs_dram[s:s + 1, 0:512], w=["kself"])
                    P.dma(vself[0:1, :], kvs_dram[s:s + 1, 512:1024], w=["vself"])
                P.op("pool", "tensor_copy", r=["ids_i"], w=["idf"], out=idf[:], in_=ids_i[:])
                P.op("pool", "tensor_scalar", r=["idf", "pcol"], w=["idf"], out=idf[:], in0=idf[:], scalar1=128.0, scalar2=pcol[:, 0:1],
                     op0=ALU.mult, op1=ALU.add)
                P.op("pool", "tensor_scalar", r=["idf"], w=["idf2"], out=idf2[:, 0, :], in0=idf[:], scalar1=2.0, scalar2=None, op0=ALU.mult)
                P.op("pool", "tensor_scalar", r=["idf"], w=["idf2"], out=idf2[:, 1, :], in0=idf[:], scalar1=2.0, scalar2=1.0, op0=ALU.mult, op1=ALU.add)
                P.op("pool", "tensor_copy", r=["idf2"], w=[("idx", z)], out=idx[:, z, :, :], in_=idf2[:])
                vsl = []
                VS[s] = vsl
                for j in range(npg):
                    if j < 16:
                        vs_ = cnt["v"] % NV
                        cnt["v"] += 1
                        ks_ = cnt["k"] % NK
                        cnt["k"] += 1
                        P.idma(kbuf[:, ks_, :], rows[:, :], idx[:, z, 0, j:j + 1], r=[("idx", z)], w=[("kb", ks_)])
                        P.idma(vbuf[:, vs_, :], rows[:, :], idx[:, z, 1, j:j + 1], r=[("idx", z)], w=[("vb", vs_)])
                        ksrc, kkey = kbuf[:, ks_, :], ("kb", ks_)
                        vsl.append((vbuf[:, vs_, :], ("vb", vs_)))
                    else:
                        ksrc, kkey = kself[:], "kself"
                        vsl.append((vself[:], "vself"))
                    pr = j % 2
                    P.op("dve", "tensor_tensor", r=[kkey, ("qb", z)], w=[("prod", pr)], out=prod[:, pr, :], in0=ksrc, in1=qb[:, z, :], op=ALU.mult)
                    P.op("dve", "tensor_reduce", r=[("prod", pr)], w=[("S_all", z)], out=S_all[:, z, j, :],
                         in_=prod[:, pr, :].rearrange("p (h d) -> p h d", h=8), axis=AX.X, op=ALU.add)

            def mid(s):
                z = s % 2
                sk = ("S_all", z)
                zk = ("Zp", 0)
                bn_, bd_ = 2 + (s % 2), 4
                vsl = VS[s]
                Sf = S_all[:, z, 0:16, :].rearrange("p a h -> p (a h)")
                sk = ("S_all", z)
                zk = ("Zp", 0)
                if layer == 0:
                    P.op("pe", "matmul", r=[sk, "ones_f"], w=[bkey(0)], out=banks[0][:, 0:128], lhsT=ones_f[:], rhs=Sf, start=True, stop=True)
                    P.op("act", "activation", r=[bkey(0)], w=["Gs"], out=Gs[:].rearrange("p a h -> p (a h)"), in_=banks[0][:, 0:128], func=AF.Copy)
                    Gv = Gs[:].rearrange("p (n i) h -> p n i h", i=2)
                    P.op("dve", "tensor_tensor", r=["Gs"], w=["gate"], out=gate[:], in0=Gv[:, :, 0, :], in1=Gv[:, :, 1, :], op=ALU.add)
                    for h in range(8):
                        P.op("dve", "max", r=["gate"], w=["m8s"], out=m8s[:, h, :], in_=gate[:, :, h])
                    for h in range(8):
                        P.op("dve", "tensor_scalar", r=["gate", "m8s"], w=["biasb"], out=biasb[:, :, h], in0=gate[:, :, h],
                             scalar1=m8s[:, h, 2:3], scalar2=NEG, op0=ALU.is_lt, op1=ALU.mult)
                    Sv = S_all[:, z, 0:16, :].rearrange("p (n i) h -> p n i h", i=2)
                    Av = arg[:, 0:16, :].rearrange("p (n i) h -> p n i h", i=2)
                    for i_ in range(2):
                        P.op("dve", "scalar_tensor_tensor", r=[sk, "biasb"], w=["arg"], out=Av[:, :, i_, :], in0=Sv[:, :, i_, :], scalar=0.125,
                             in1=biasb[:], op0=ALU.mult, op1=ALU.add)
                    P.op("dve", "tensor_scalar", r=[sk], w=["arg"], out=arg[:, 16, :], in0=S_all[:, z, 16, :], scalar1=0.125, scalar2=None, op0=ALU.mult)
                    P.op("act", "activation", r=["arg"], w=[zk], out=Zp[:, 0, :, 0:8], in_=arg[:], func=AF.Exp)
                    P.op("dve", "tensor_scalar", r=[zk, "ident_f"], w=[zk], out=Zp[:, 0, 16, 0:8], in0=Zp[:, 0, 16, 0:8], scalar1=ident_f[:, 0:1],
                         scalar2=None, op0=ALU.mult)
                else:
                    af = arg[:, 0:16, :].rearrange("p a h -> p (a h)")
                    gf = Gs[:].rearrange("p a h -> p (a h)")
                    P.op("act", "activation", r=[sk], w=["arg"], out=af, in_=Sf, func=AF.Exp, scale=0.125)
                    P.op("act", "activation", r=["arg"], w=["Gs"], out=gf, in_=af, func=AF.Ln, bias=1.0)
                    P.op("pe", "matmul", r=["Gs", "tri_f"], w=[bkey(0)], out=banks[0][:, 0:128], lhsT=tri_f[:], rhs=gf, start=True, stop=True)
                    P.op("pe", "matmul", r=["Gs", "ones_f"], w=[bkey(1)], out=banks[1][:, 0:128], lhsT=ones_f[:], rhs=gf, start=True, stop=True)
                    P.op("act", "activation", r=[bkey(1)], w=[("prod", 0)], out=prod[:, 0, 0:128], in_=banks[1][:, 0:128], func=AF.Copy)
                    Tv = prod[:, 0, 0:128].rearrange("p (a h) -> p a h", h=8)
                    for pgi in range(14, -1, -1):
                        P.op("dve", "tensor_tensor", r=[("prod", 0), "carry_"], w=["carry_"], out=carry_[:, pgi, :], in0=carry_[:, pgi + 1, :],
                             in1=Tv[:, pgi + 1, :], op=ALU.add)
                    P.op("dve", "scalar_tensor_tensor", r=[sk, bkey(0)], w=["arg"], out=af, in0=Sf, scalar=0.125, in1=banks[0][:, 0:128],
                         op0=ALU.mult, op1=ALU.subtract)
                    P.op("dve", "tensor_tensor", r=["arg", "carry_"], w=["arg"], out=af, in0=af, in1=carry_[:].rearrange("p a h -> p (a h)"), op=ALU.subtract)
                    P.op("act", "activation", r=["arg"], w=[zk], out=Zp[:, 0, 0:16, 0:8], in_=arg[:, 0:16, :], func=AF.Exp)
                bn_, bd_ = 2 + (s % 2), 4
                for j in range(npg):
                    vap, vk = vsl[j]
                    P.op("pe", "matmul", r=[zk, vk], w=[bkey(bn_)], out=banks[bn_][:, :], lhsT=Zp[:, 0, j, :], rhs=vap,
                         start=(j == 0), stop=(j == npg - 1))
                if layer == 0:
                    for j in range(npg):
                        P.op("pe", "matmul", r=[zk, "ones_f"], w=[bkey(bd_)], out=banks[bd_][:, 0:64], lhsT=Zp[:, 0, j, :], rhs=ones_f[:, 0:64],
                             start=(j == 0), stop=(j == npg - 1))

            def finish(s):
                z = s % 2
                sk = ("S_all", z)
                zk = ("Zp", 0)
                bn_, bd_ = 2 + (s % 2), 4
                vsl = VS[s]
                if layer == 0:
                    P.op("dve", "reciprocal", r=[bkey(bd_)], w=["rden"], out=rden[0:8, 0:1], in_=banks[bd_][0:8, 0:1])
                    P.op("dve", "tensor_scalar", r=[bkey(bn_), "rden"], w=["Opad"], out=Opad[0:8, :], in0=banks[bn_][0:8, :], scalar1=rden[0:8, 0:1],
                         scalar2=None, op0=ALU.mult)
                else:
                    P.op("act", "activation", r=[bkey(bn_)], w=["Opad"], out=Opad[0:8, :], in_=banks[bn_][0:8, :], func=AF.Copy)
                bt_ = 5
                for c in range(4):
                    P.op("pe", "transpose", r=["Opad", "ident_f"], w=[bkey(bt_)], out=banks[bt_][:, c * 128:(c + 1) * 128],
                         in_=Opad[:, c * 128:(c + 1) * 128], identity=ident_f[:])
                for c in range(4):
                    P.op("dve", "tensor_copy", r=[bkey(bt_)], w=[tkey(16)], out=hT[0:64, c, T + s:T + s + 1],
                         in_=banks[bt_][0:64, c * 128 + 2 * c:c * 128 + 2 * c + 1])
                    P.op("act", "activation", r=[bkey(bt_)], w=[tkey(16)], out=hT[64:128, c, T + s:T + s + 1],
                         in_=banks[bt_][64:128, c * 128 + 2 * c + 1:c * 128 + 2 * c + 2], func=AF.Copy)


            scores(0)
            for s in range(NS):
                mid(s)
                if s + 1 < NS:
                    scores(s + 1)
                finish(s)

        def ffn(l):
            with ExitStack() as ph:
                walloc(ph, 3)
                aT = sb("aT%d" % l, [128, 4, TT], BF16, ph)
                gbuf = sb("gbuf%d" % l, [128, 2 + T], F32, ph)
                gss = sb("gss%d" % l, [128, 3, NS], F32, ph)
                ctmp = sb("ctmp%d" % l, [128, 2, 512], F32, ph)
                gel = sb("gel%d" % l, [128, 2, 512], BF16, ph)
                cst = sb("cst%d" % l, [NS, 2, 512], F32, ph)
                cvs = sb("cvs%d" % l, [NS, 2, 512], F32, ph)
                cvp = sb("cvp%d" % l, [2, 512], F32, ph)
                P.op("pool", "memset", w=["gbuf"], ap=gbuf[:, 0:2], constant=0.0)
                cst_v = conv_st[l].rearrange("s (r f) -> s r f", r=2)
                cso_v = conv_s[l].rearrange("s (r f) -> s r f", r=2)
                ei = 0
                for fg in range(6):
                    f0 = fg * 512
                    ncols = min(512, DFF - f0)
                    nf = ncols // 128
                    wg, wgk = load_w(w_gate[l][:, f0:f0 + ncols], ncols=ncols)
                    wu, wuk = load_w(w_up[l][:, f0:f0 + ncols], ncols=ncols)
                    P.dma(cst[:, :, 0:ncols], cst_v[:, :, f0:f0 + ncols], w=["cst"])
                    P.op("pool", "tensor_copy", r=["cst"], w=["cvs"], out=cvs[:, 0, 0:ncols], in_=cst[:, 1, 0:ncols])
                    for j in range(nf):
                        fc = fg * 4 + j
                        cw = [convc_t[:, l, i_, fc:fc + 1] for i_ in range(4)]
                        bks = nextbank(4, 6)
                        for r_ in range(2):
                            P.op("pe", "transpose", r=["cst", "ident_f"], w=[bkey(bks)], out=banks[bks][:, r_ * NS:(r_ + 1) * NS],
                                 in_=cst[:, r_, j * 128:(j + 1) * 128], identity=ident_f[0:NS, 0:NS])
                        P.op("dve", "tensor_copy", r=[bkey(bks)], w=["gss"], out=gss[:, 0:2, :].rearrange("p r s -> p (r s)"),
                             in_=banks[bks][:, 0:2 * NS])
                        for g in range(5):
                            cs = gcolsl(g)
                            n = cs.stop - cs.start
                            ba = nextbank(0, 4)
                            for kc in range(8):
                                P.op("pe", "matmul", r=hkeys(g) + [wgk], w=[bkey(ba)], out=banks[ba][:, 0:n],
                                     lhsT=wg[:, kc, j * 128:(j + 1) * 128], rhs=hT[:, kc, cs], start=(kc == 0), stop=(kc == 7))
                            bb = nextbank(0, 4)
                            for kc in range(8):
                                P.op("pe", "matmul", r=hkeys(g) + [wuk], w=[bkey(bb)], out=banks[bb][:, 0:n],
                                     lhsT=wu[:, kc, j * 128:(j + 1) * 128], rhs=hT[:, kc, cs], start=(kc == 0), stop=(kc == 7))
                            e_ = ei % 2
                            ei += 1
                            ck, gk = ("ctmp", e_), ("gel", e_)
                            if g < 4:
                                o = 2 + 512 * g
                                P.op("act", "activation", r=[bkey(ba)], w=["gbuf"], out=gbuf[:, o:o + 512], in_=banks[ba][:, 0:512], func=AF.Copy)
                                srcs = [gbuf[:, o - 2:o + 510], gbuf[:, o - 1:o + 511], gbuf[:, o:o + 512]]
                                sk_ = "gbuf"
                            else:
                                P.op("act", "activation", r=[bkey(ba)], w=["gss"], out=gss[:, 2, :], in_=banks[ba][:, 0:NS], func=AF.Copy)
                                srcs = [gss[:, 0, :], gss[:, 1, :], gss[:, 2, :]]
                                sk_ = "gss"
                            ct = ctmp[:, e_, 0:n]
                            P.op("dve", "tensor_scalar", r=[sk_, "convc"], w=[ck], out=ct, in0=srcs[2], scalar1=cw[2], scalar2=cw[3],
                                 op0=ALU.mult, op1=ALU.add)
                            P.op("dve", "scalar_tensor_tensor", r=[sk_, "convc", ck], w=[ck], out=ct, in0=srcs[1], scalar=cw[1], in1=ct,
                                 op0=ALU.mult, op1=ALU.add)
                            P.op("dve", "scalar_tensor_tensor", r=[sk_, "convc", ck], w=[ck], out=ct, in0=srcs[0], scalar=cw[0], in1=ct,
                                 op0=ALU.mult, op1=ALU.add)
                            P.op("act", "activation", r=[ck], w=[gk], out=gel[:, e_, 0:n], in_=ct, func=AF.Gelu_apprx_tanh)
                            P.op("dve", "tensor_tensor", r=[gk, bkey(bb)], w=[("aT", g)], out=aT[:, j, cs], in0=gel[:, e_, 0:n],
                                 in1=banks[bb][:, 0:n], op=ALU.mult)
                        bko = nextbank(4, 6)
                        P.op("pe", "transpose", r=["gbuf", "ident_f"], w=[bkey(bko)], out=banks[bko][0:2, 0:128],
                             in_=gbuf[:, T:T + 2], identity=ident_f[:])
                        P.op("pe", "transpose", r=["gss", "ident_f"], w=[bkey(bko)], out=banks[bko][0:NS, 128:256],
                             in_=gss[:, 2, :], identity=ident_f[:])
                        P.op("dve", "tensor_copy", r=[bkey(bko)], w=["cvp"], out=cvp[:, j * 128:(j + 1) * 128], in_=banks[bko][0:2, 0:128])
                        P.op("dve", "tensor_copy", r=[bkey(bko)], w=["cvs"], out=cvs[:, 1, j * 128:(j + 1) * 128], in_=banks[bko][0:NS, 128:256])
                    P.dma(conv_p[l][:, f0:f0 + ncols], cvp[:, 0:ncols], r=["cvp"])
                    P.dma(cso_v[:, :, f0:f0 + ncols], cvs[:, :, 0:ncols], r=["cvs"])
                    wd, wdk = load_w(w_down[l][f0:f0 + ncols, :], nk=nf, ncols=1024)
                    for dc in range(8):
                        for g in range(5):
                            cs = gcolsl(g)
                            n = cs.stop - cs.start
                            bk = nextbank(0, 4)
                            for j in range(nf):
                                P.op("pe", "matmul", r=[("aT", g), wdk], w=[bkey(bk)], out=banks[bk][:, 0:n],
                                     lhsT=wd[:, j, dc * 128:(dc + 1) * 128], rhs=aT[:, j, cs], start=(j == 0), stop=(j == nf - 1))
                            P.op("dve", "tensor_tensor", r=[bkey(bk), ("xT", g)], w=[("xT", g)], out=xT[:, dc, cs],
                                 in0=xT[:, dc, cs], in1=banks[bk][:, 0:n], op=ALU.add)
                P.barrier()

        rmsnorm(0)
        if STAGE >= 1:
            with ExitStack() as ph0:
                qs_tok = sb("qs_tok0", [NS, 512], F32, ph0)
                ks_tok = sb("ks_tok0", [NS, 512], F32, ph0)
                vs_tok = sb("vs_tok0", [NS, 512], F32, ph0)
                phA = ExitStack()
                QT = sb("QT0", [128, 4, T], BF16, phA)
                KT = sb("KT0", [128, 4, T], BF16, phA)
                Vt = sb("Vt0", [128, 16, 512], BF16, phA)
                with ExitStack() as ph:
                    walloc(ph, 2)
                    with ExitStack() as phq:
                        qkv_proj(phq, w_in0, 0, kv0_p, kv0_s, QT, KT, Vt, qs_tok, ks_tok, vs_tok)
                        P.barrier()
                    if STAGE >= 2:
                        sgw_f = sb("sgw_f", [128, 4, 128], F32, ph)
                        sgw_b = sb("sgw_b", [128, 4, 128], BF16, ph)
                        gain_t = sb("gain_t", [128, 512], F32, ph)
                        bcol_t = sb("bcol_t", [128, 4], F32, ph)
                        ssw = sb("ssw", [NS, 512], F32, ph)
                        ssb = sb("ssb", [NS, 512], F32, ph)
                        gst = sb("gst", [128, 2, 512], F32, ph)
                        gbf = sb("gbf", [128, 2, 512], BF16, ph)
                        ubf = sb("ubf", [128, 2, 512], BF16, ph)
                        fbt = sb("fbt", [128, 2, 512], BF16, ph)
                        bst = sb("bst", [128, 2, 8], F32, ph)
                        P.dma(sgw_f[:], sgu_wT[:, :, :], w=["sgw_f"])
                        P.op("pool", "affine_select", r=["sgw_f"], w=["sgw_b"], out=sgw_b[:], in_=sgw_f[:], pattern=[[0, 4], [1, 128]],
                             compare_op=ALU.is_ge, fill=0.0, base=0, channel_multiplier=-1)
                        P.dma(gain_t[:], sgu_gain_bc[:, :], w=["gain"])
                        P.dma(bcol_t[:], sgu_bcol[:, :], w=["bcol"])
                        P.dma(ssw[:], sgu_s_w[:, :], w=["ssw"])
                        P.dma(ssb[:], sgu_s_b[:, :], w=["ssw"])
                        wv_, wvk = load_w(w_in0[:, 2048:2560])
                        wu_, wuk = load_w(w_in0[:, 1536:2048])
                        for t in range(17):
                            rows = trows(t)
                            s = t % 2
                            bk = nextbank(0, 4)
                            for kc in range(8):
                                P.op("pe", "matmul", r=[tkey(t), wvk], w=[bkey(bk)], out=banks[bk][0:rows, :],
                                     lhsT=hT[:, kc, tcols(t)], rhs=wv_[:, kc, :], start=(kc == 0), stop=(kc == 7))
                            bu_ = nextbank(0, 4)
                            for kc in range(8):
                                P.op("pe", "matmul", r=[tkey(t), wuk], w=[bkey(bu_)], out=banks[bu_][0:rows, :],
                                     lhsT=hT[:, kc, tcols(t)], rhs=wu_[:, kc, :], start=(kc == 0), stop=(kc == 7))
                            gk = ("gst", s)
                            P.op("act", "activation", r=[bkey(bk)], w=[gk], out=gst[0:rows, s, :], in_=banks[bk][0:rows, :], func=AF.Gelu_apprx_tanh)
                            P.op("act", "activation", r=[bkey(bu_)], w=[("ubf", s)], out=ubf[0:rows, s, :], in_=banks[bu_][0:rows, :], func=AF.Gelu_apprx_tanh)
                            P.op("dve", "bn_stats", r=[gk], w=[("bst", s)], out=bst[0:rows, s, 0:6], in_=gst[0:rows, s, :])
                            P.op("dve", "bn_aggr", r=[("bst", s)], w=[("bst", s)], out=bst[0:rows, s, 6:8], in_=bst[0:rows, s, 0:6])
                            P.op("act", "activation", r=[("bst", s), "eps"], w=[("bst", s)], out=bst[0:rows, s, 7:8], in_=bst[0:rows, s, 7:8],
                                 func=AF.Sqrt, bias=eps_t[0:rows, 0:1])
                            P.op("dve", "reciprocal", r=[("bst", s)], w=[("bst", s)], out=bst[0:rows, s, 7:8], in_=bst[0:rows, s, 7:8])
                            P.op("dve", "tensor_scalar", r=[gk, ("bst", s)], w=[gk], out=gst[0:rows, s, :], in0=gst[0:rows, s, :],
                                 scalar1=bst[0:rows, s, 6:7], scalar2=bst[0:rows, s, 7:8], op0=ALU.subtract, op1=ALU.mult)
                            P.op("dve", "tensor_tensor", r=[gk, "gain"], w=[gk], out=gst[0:rows, s, :], in0=gst[0:rows, s, :],
                                 in1=gain_t[0:rows, :], op=ALU.mult)
                            fk = ("fbt", s)
                            if t == 16:
                                P.dma(sgu_s[:, :], gst[0:rows, s, :], r=[gk])
                                P.op("dve", "tensor_tensor", r=[gk, "ssw"], w=[gk], out=gst[0:rows, s, :], in0=gst[0:rows, s, :], in1=ssw[:], op=ALU.mult)
                                P.op("dve", "tensor_tensor", r=[gk, "ssw"], w=[gk], out=gst[0:rows, s, :], in0=gst[0:rows, s, :], in1=ssb[:], op=ALU.add)
                                P.op("dve", "tensor_tensor", r=[gk, ("ubf", s)], w=[fk], out=fbt[0:rows, s, :], in0=gst[0:rows, s, :],
                                     in1=ubf[0:rows, s, :], op=ALU.mult)
                            else:
                                P.op("pool", "tensor_copy", r=[gk], w=[("gbf", s)], out=gbf[:, s, :], in_=gst[:, s, :])
                                bf_ = nextbank(4, 6)
                                for g4 in range(4):
                                    P.op("pe", "matmul", r=[("gbf", s), "sgw_b"], w=[bkey(bf_)], out=banks[bf_][:, g4 * 128:(g4 + 1) * 128],
                                         lhsT=sgw_b[:, g4, :], rhs=gbf[:, s, g4 * 128:(g4 + 1) * 128], start=True, stop=True)
                                for g4 in range(4):
                                    P.op("dve", "scalar_tensor_tensor", r=[bkey(bf_), "bcol", ("ubf", s)], w=[fk],
                                         out=fbt[:, s, g4 * 128:(g4 + 1) * 128], in0=banks[bf_][:, g4 * 128:(g4 + 1) * 128],
                                         scalar=bcol_t[:, g4:g4 + 1], in1=ubf[:, s, g4 * 128:(g4 + 1) * 128], op0=ALU.add, op1=ALU.mult)
                            transpose_to_hT(fbt[0:rows, s, :], fk, rows, 4, t)
                    P.barrier()
                if STAGE >= 3:
                    with ExitStack() as ph:
                        attention(ph, 0, QT, KT, Vt)
                        P.barrier()
                P.barrier()
                phA.close()
                if STAGE >= 3 and SAMPLE:
                    with ExitStack() as ph:
                        sample_attention(ph, 0, cache0, qs_tok, kv0_s, qscr0)
                        P.barrier()
            if STAGE >= 4:
                with ExitStack() as ph:
                    walloc(ph, 2)
                    out_proj(w_out0)
                    P.barrier()
        if STAGE >= 5:
            rmsnorm(1)
            ffn(0)
        if STAGE >= 6:
            rmsnorm(2)
            with ExitStack() as ph:
                walloc(ph, 2)
                uT = sb("uT", [128, 4, 15 + T], F32, ph)
                pa = sb("pa", [128, 15 + T], F32, ph)
                pb = sb("pb", [128, 15 + T], F32, ph)
                pT_ = sb("pT_", [128, T], BF16, ph)
                dst_ = sb("dst_", [128, TT], BF16, ph)
                plw_f = sb("plw_f", [128, 4, 128], F32, ph)
                plw_b = sb("plw_b", [128, 4, 128], BF16, ph)
                pscol = sb("pscol", [128, 4], F32, ph)
                pfix = sb("pfix", [128, 64], F32, ph)
                us_tok = sb("us_tok", [NS, 512], F32, ph)
                pst = sb("pst", [NS, 15, 128], F32, ph)
                pls = sb("pls", [NS, 128], F32, ph)
                plsb = sb("plsb", [128, NS], BF16, ph)
                ppo = sb("ppo", [15, 512], F32, ph)
                P.dma(plw_f[:], pool_w[:, :, :], w=["plw_f"])
                P.op("pool", "tensor_copy", r=["plw_f"], w=["plw_b"], out=plw_b[:], in_=plw_f[:])
                P.dma(pscol[:], pool_scol[:, :], w=["pscol"])
                P.dma(pfix[:], pool_fix[:, :], w=["pfix"])
                P.op("pool", "memset", w=["uT"], ap=uT[:, :, 0:15], constant=0.0)
                P.op("pool", "memset", w=["pa"], ap=pa[:, 0:15], constant=0.0)
                P.op("pool", "memset", w=["pb"], ap=pb[:, 0:15], constant=0.0)
                wv_, wvk = load_w(w_in1[:, 1536:2048])
                bk = nextbank(0, 4)
                for kc in range(8):
                    P.op("pe", "matmul", r=[tkey(16), wvk], w=[bkey(bk)], out=banks[bk][0:NS, :], lhsT=hT[:, kc, T:TT], rhs=wv_[:, kc, :],
                         start=(kc == 0), stop=(kc == 7))
                P.op("act", "activation", r=[bkey(bk)], w=["us_tok"], out=us_tok[:], in_=banks[bk][0:NS, :], func=AF.Copy)
                P.dma(pool_s.rearrange("s (r c) -> s r c", r=15)[:, 14, :], us_tok[:], r=["us_tok"])
                pst_v = pool_st.rearrange("s (r c) -> s r c", r=15)
                pso_v = pool_s.rearrange("s (r c) -> s r c", r=15)
                for j in range(4):
                    w_ = (2, 4, 8, 16)[j]
                    for g in range(4):
                        bk = nextbank(0, 4)
                        for kc in range(8):
                            P.op("pe", "matmul", r=hkeys(g) + [wvk], w=[bkey(bk)], out=banks[bk][:, :], lhsT=wv_[:, kc, j * 128:(j + 1) * 128],
                                 rhs=hT[:, kc, gcolsl(g)], start=(kc == 0), stop=(kc == 7))
                        copy(evac_eng(), uT[:, j, 15 + 512 * g:15 + 512 * g + 512], banks[bk][:, :], r=[bkey(bk)], w=["uT"])
                    bko = nextbank(4, 6)
                    P.op("pe", "transpose", r=["uT", "ident_f"], w=[bkey(bko)], out=banks[bko][0:15, 0:128], in_=uT[:, j, T:T + 15],
                         identity=ident_f[:])
                    P.op("dve", "tensor_copy", r=[bkey(bko)], w=["ppo"], out=ppo[:, j * 128:(j + 1) * 128], in_=banks[bko][0:15, 0:128])
                    cur, ck_ = uT[:, j, :], "uT"
                    bufs = [(pa, "pa"), (pb, "pb")]
                    for st_ in range(j + 1):
                        sh = 1 << st_
                        nx, nk2 = bufs[st_ % 2]
                        P.op("dve", "tensor_tensor", r=[ck_], w=[nk2], out=nx[:, 15:15 + T], in0=cur[:, 15:15 + T], in1=cur[:, 15 - sh:15 + T - sh], op=ALU.add)
                        cur, ck_ = nx, nk2
                    oth, ok_ = bufs[(j + 1) % 2]
                    P.op("dve", "scalar_tensor_tensor", r=[ck_, "uT"], w=[ok_], out=oth[:, 15:15 + T], in0=cur[:, 15:15 + T], scalar=1.0 / w_,
                         in1=uT[:, j, 15:15 + T], op0=ALU.mult, op1=ALU.subtract)
                    P.op("dve", "tensor_tensor", r=[ck_, "pfix"], w=[ck_], out=cur[:, 15:31], in0=cur[:, 15:31], in1=pfix[:, j * 16:(j + 1) * 16], op=ALU.mult)
                    P.op("dve", "tensor_tensor", r=[ck_, "uT"], w=[ok_], out=oth[:, 15:31], in0=cur[:, 15:31], in1=uT[:, j, 15:31], op=ALU.subtract)
                    P.op("act", "activation", r=[ok_], w=["pT_"], out=pT_[:], in_=oth[:, 15:15 + T], func=AF.Copy)
                    for g in range(4):
                        bk = nextbank(0, 4)
                        P.op("pe", "matmul", r=["pT_", "plw_b"], w=[bkey(bk)], out=banks[bk][:, :], lhsT=plw_b[:, j, :], rhs=pT_[:, gcolsl(g)],
                             start=True, stop=True)
                        P.op("act", "activation", r=[bkey(bk), "pscol"], w=["dst_"], out=dst_[:, gcolsl(g)], in_=banks[bk][:, :], func=AF.Copy,
                             scale=pscol[:, j:j + 1])
                    P.dma(pst[:], pst_v[:, :, j * 128:(j + 1) * 128], w=["pst"])
                    P.dma(pso_v[:, 0:14, j * 128:(j + 1) * 128], pst[:, 1:15, :], r=["pst"])
                    P.op("dve", "tensor_reduce", r=["pst"], w=["pls"], out=pls[:], in_=pst[:, 16 - w_:15, :].rearrange("p r c -> p c r"),
                         axis=AX.X, op=ALU.add)
                    P.op("dve", "tensor_tensor", r=["pls", "us_tok"], w=["pls"], out=pls[:], in0=pls[:], in1=us_tok[:, j * 128:(j + 1) * 128], op=ALU.add)
                    P.op("dve", "scalar_tensor_tensor", r=["pls", "us_tok"], w=["pls"], out=pls[:], in0=pls[:], scalar=1.0 / w_,
                         in1=us_tok[:, j * 128:(j + 1) * 128], op0=ALU.mult, op1=ALU.subtract)
                    bko = nextbank(4, 6)
                    P.op("pe", "transpose", r=["pls", "ident_f"], w=[bkey(bko)], out=banks[bko][:, 0:NS], in_=pls[:], identity=ident_f[0:NS, 0:NS])
                    P.op("act", "activation", r=[bkey(bko)], w=["plsb"], out=plsb[:], in_=banks[bko][:, 0:NS], func=AF.Copy)
                    bk = nextbank(0, 4)
                    P.op("pe", "matmul", r=["plsb", "plw_b"], w=[bkey(bk)], out=banks[bk][:, 0:NS], lhsT=plw_b[:, j, :], rhs=plsb[:], start=True, stop=True)
                    P.op("act", "activation", r=[bkey(bk), "pscol"], w=["dst_"], out=dst_[:, T:TT], in_=banks[bk][:, 0:NS], func=AF.Copy,
                         scale=pscol[:, j:j + 1])
                    P.dma(dscr[:, j, :], dst_[:], r=["dst_"], w=["dscr"])
                P.dma(pool_p[:, :], ppo[:], r=["ppo"])
                P.barrier()
            with ExitStack() as ph1:
                qs1 = sb("qs_tok1", [NS, 512], F32, ph1)
                ks1 = sb("ks_tok1", [NS, 512], F32, ph1)
                vs1 = sb("vs_tok1", [NS, 512], F32, ph1)
                phB = ExitStack()
                QT = sb("QT1", [128, 4, T], BF16, phB)
                KT = sb("KT1", [128, 4, T], BF16, phB)
                Vt = sb("Vt1", [128, 16, 512], BF16, phB)
                with ExitStack() as ph:
                    walloc(ph, 2)
                    qkv_proj(ph, w_in1, 1, kv1_p, kv1_s, QT, KT, Vt, qs1, ks1, vs1)
                    P.barrier()
                P.dma(hT[:, 4:8, :], dscr[:, :, :], r=["dscr"], w=[tkey(t) for t in range(17)])
                if STAGE >= 7:
                    with ExitStack() as ph:
                        attention(ph, 1, QT, KT, Vt)
                        P.barrier()
                P.barrier()
                phB.close()
                if STAGE >= 7 and SAMPLE:
                    with ExitStack() as ph:
                        sample_attention(ph, 1, cache1, qs1, kv1_s, qscr1)
                        P.barrier()
            if STAGE >= 8:
                with ExitStack() as ph:
                    walloc(ph, 2)
                    out_proj(w_out1)
                    P.barrier()
        if STAGE >= 9:
            rmsnorm(3)
            ffn(1)
        if STAGE >= 10:
            with ExitStack() as ph:
                yT = sb("yT", [128, 8, 512], F32, ph)
                ytok = sb("ytok", [128, 2, D], F32, ph)
                for g in range(5):
                    ri, n, cs = rmsnorm_stats(g)
                    for c in range(8):
                        P.op("dve", "scalar_tensor_tensor", r=[("xT", g), ("rs", ri), "gcol"], w=["yT"],
                             out=yT[:, c, 0:n], in0=xT[:, c, cs], scalar=gcol_t[:, 4 * 8 + c:4 * 8 + c + 1],
                             in1=rs_t[:, ri, 0:n], op0=ALU.mult, op1=ALU.mult)
                    for tt, t in enumerate(gtiles(g)):
                        rows = trows(t)
                        s = t % 2
                        for half in range(2):
                            bk = nextbank(0, 4)
                            for j in range(4):
                                c = half * 4 + j
                                P.op("pe", "transpose", r=["yT", "ident_f"], w=[bkey(bk)], out=banks[bk][0:rows, j * 128:(j + 1) * 128],
                                     in_=yT[:, c, tt * 128:tt * 128 + rows], identity=ident_f[:])
                            copy(evac_eng(), ytok[0:rows, s, half * 512:(half + 1) * 512], banks[bk][0:rows, :], r=[bkey(bk)], w=[("ytok", s)])
                        dst = y_p[t * 128:(t + 1) * 128, :] if t < 16 else y_s[:, :]
                        P.dma(dst, ytok[0:rows, s, :], r=[("ytok", s)])
                P.barrier()

        P.finish()
    print("instructions:", P.nins, {k: v for k, v in P.cnt.items()})
    return nc


def _prep_inputs(inputs):
    f32 = np.float32
    g = lambda k: np.asarray(inputs[k])
    common = {}
    common["cache0"] = np.ascontiguousarray(g("cache_l0_kv"), dtype=f32).reshape(-1, 128, 1024)
    common["cache1"] = np.ascontiguousarray(g("cache_l1_kv"), dtype=f32).reshape(-1, 128, 1024)
    common["w_in0"] = np.ascontiguousarray(g("l0_w_in"), dtype=f32)
    common["w_out0"] = np.ascontiguousarray(g("l0_w_out"), dtype=f32)
    common["w_in1"] = np.ascontiguousarray(g("l1_w_in"), dtype=f32)
    common["w_out1"] = np.ascontiguousarray(g("l1_w_out"), dtype=f32)
    common["w_gate"] = np.ascontiguousarray(g("ffn_w_gate"), dtype=f32)
    common["w_up"] = np.ascontiguousarray(g("ffn_w_up"), dtype=f32)
    common["w_down"] = np.ascontiguousarray(g("ffn_w_down"), dtype=f32)
    common["sgu_wT"] = np.ascontiguousarray(g("l0_sgu_w").transpose(2, 0, 1), dtype=f32)
    common["pool_w"] = np.ascontiguousarray(g("l1_pool_w").transpose(1, 0, 2), dtype=f32)
    norms = np.stack([g("l0_norm"), g("ffn_norm")[0], g("l1_norm"), g("ffn_norm")[1], g("final_norm")], 0)
    common["gcols"] = np.ascontiguousarray(norms.reshape(5, 8, 128).transpose(2, 0, 1).reshape(128, 40), dtype=f32)
    cw = g("ffn_conv_w")
    cb = g("ffn_conv_b")
    cc = np.concatenate([cw, cb[:, None, :]], axis=1)
    common["convc"] = np.ascontiguousarray(cc.reshape(2, 4, NFC, 128).transpose(3, 0, 1, 2).reshape(128, 2 * 4 * NFC), dtype=f32)
    common["sgu_gain_bc"] = np.ascontiguousarray(np.broadcast_to(g("l0_sgu_gain")[None, :], (128, 512)), dtype=f32)
    common["sgu_bcol"] = np.ascontiguousarray(g("l0_sgu_b").T, dtype=f32)
    common["sgu_s_w"] = np.ascontiguousarray(np.broadcast_to(np.repeat(g("l0_sgu_w")[:, 0, 0], 128)[None, :], (NS, 512)), dtype=f32)
    common["sgu_s_b"] = np.ascontiguousarray(np.broadcast_to(np.repeat(g("l0_sgu_b")[:, 0], 128)[None, :], (NS, 512)), dtype=f32)
    common["pool_scol"] = np.ascontiguousarray(g("l1_pool_scale").reshape(4, 128).T, dtype=f32)
    fix = np.ones((4, 16), f32)
    for gi, w in enumerate((2, 4, 8, 16)):
        for t in range(16):
            fix[gi, t] = 1.0 / min(t + 1, w)
    common["pool_fix"] = np.ascontiguousarray(np.broadcast_to(fix.reshape(1, 64), (128, 64)), dtype=f32)
    half = 8
    inv_freq = (np.float32(500000.0) ** (-(np.arange(half, dtype=f32) * np.float32(2.0) / np.float32(16)))).astype(f32)
    pos = np.concatenate([np.arange(T), np.full(128, T)]).astype(f32)
    ang = (pos[:, None] * inv_freq[None, :]).astype(f32)
    common["ropec"] = np.ascontiguousarray(np.cos(ang).astype(f32).reshape(17, 128, 8).transpose(1, 0, 2))
    common["ropes"] = np.ascontiguousarray(np.sin(ang).astype(f32).reshape(17, 128, 8).transpose(1, 0, 2))
    maps = []
    xp = g("x_prompt")
    xs = g("x_sample")
    pt = g("page_table")
    ps_ = g("state_l1_pool")
    cs_ = g("state_ffn_conv")
    for c in range(8):
        m = dict(common)
        sl = slice(NS * c, NS * (c + 1))
        m["x_p"] = np.ascontiguousarray(xp[c], dtype=f32)
        m["x_s"] = np.ascontiguousarray(xs[sl, 0, :], dtype=f32)
        m["ptab"] = np.ascontiguousarray(pt[sl], dtype=np.int32)
        m["pool_st"] = np.ascontiguousarray(ps_[sl].reshape(NS, 15 * 512), dtype=f32)
        m["conv_st"] = np.ascontiguousarray(cs_[:, sl].reshape(2, NS, 2 * DFF), dtype=f32)
        maps.append(m)
    return maps


_NC = None


def kernel(**inputs):
    global _NC
    maps = _prep_inputs(inputs)
    if _NC is None:
        _NC = build(maps[0]["cache0"].shape[0])
    res = run_bass_kernel_spmd(_NC, maps, core_ids=list(range(8)))
    R = res.results
    f32 = np.float32
    cat = lambda k: np.stack([np.asarray(r[k], dtype=f32) for r in R], 0)
    y_p = cat("y_p")
    y_s = cat("y_s").reshape(128, 1, D)
    kv0_p = cat("kv0_p").reshape(8, T, 2, 8, 64)
    kv0_s = cat("kv0_s").reshape(128, 1, 2, 8, 64)
    sgu_s = cat("sgu_s").reshape(128, 1, 512)
    kv1_p = cat("kv1_p").reshape(8, T, 2, 8, 64)
    kv1_s = cat("kv1_s").reshape(128, 1, 2, 8, 64)
    pool_p = cat("pool_p")
    pool_s = cat("pool_s").reshape(128, 15, 512)
    conv_p = cat("conv_p").transpose(1, 0, 2, 3)
    conv_s = cat("conv_s").reshape(8, 2, NS, 2, DFF).transpose(1, 0, 2, 3, 4).reshape(2, 128, 2, DFF)
    return (y_p, y_s, kv0_p, kv0_s, sgu_s, kv1_p, kv1_s, pool_p, pool_s,
            np.ascontiguousarray(conv_p), np.ascontiguousarray(conv_s))
```

```python
import numpy as np
from contextlib import ExitStack
import concourse.bass as bass
import concourse.mybir as mybir
from concourse.bass_utils import run_bass_kernel_spmd

F32 = mybir.dt.float32
BF16 = mybir.dt.bfloat16
I32 = mybir.dt.int32
AF = mybir.ActivationFunctionType
ALU = mybir.AluOpType
AX = mybir.AxisListType

T = 2048
NS = 16
TT = T + NS
D = 1024
DFF = 2816
NFC = 22
EPS = 1e-6
NPHYS = 2560
NEG = -30000.0
STAGE = 99
SAMPLE = True


class Prog:
    def __init__(self, nc, es, ndma=20):
        self.nc = nc
        self.E = {"pe": nc.tensor, "act": nc.scalar, "dve": nc.vector, "pool": nc.gpsimd, "sp": nc.sync}
        self.sem = {k: es.enter_context(nc.semaphore("s_" + k)) for k in self.E}
        self.dsem = [es.enter_context(nc.semaphore("d%d" % i)) for i in range(ndma)]
        self.cnt = {k: 0 for k in self.E}
        self.seen = {k: {} for k in self.E}
        self.dval = [0] * ndma
        self.drr = 0
        self.lastw = {}
        self.readers = {}
        self.nins = 0
        self.log = []
        self._cur = None

    def _waits(self, eng, reads, writes, extra=()):
        need = {}

        def add(ev):
            k = ev[:2]
            if ev[2] > need.get(k, 0):
                need[k] = ev[2]

        for k in reads:
            if k in self.lastw:
                add(self.lastw[k])
        for k in writes:
            if k in self.lastw:
                add(self.lastw[k])
            for ev in self.readers.get(k, ()):
                add(ev)
        for ev in extra:
            add(ev)
        e = self.E[eng]
        for k, v in need.items():
            if k[0] == "e" and k[1] == eng and eng == "pe":
                continue
            if self.seen[eng].get(k, 0) >= v:
                continue
            self.seen[eng][k] = v
            s = self.sem[k[1]] if k[0] == "e" else self.dsem[k[1]]
            e.wait_ge(s, v)
            self.log.append((eng, 'wait', k, v))

    def _record(self, ev, reads, writes):
        for k in reads:
            self.readers.setdefault(k, []).append(ev)
        for k in writes:
            self.lastw[k] = ev
            self.readers[k] = []

    def op(self, eng, name, r=(), w=(), inc=True, **kw):
        self._waits(eng, r, w)
        ins = getattr(self.E[eng], name)(**kw)
        if inc:
            self.cnt[eng] += 1
            ins.then_inc(self.sem[eng], 1)
            self.log.append((eng, 'inc', ('e', eng), 1, name))
            ev = ("e", eng, self.cnt[eng])
        else:
            ev = ("e", eng, self.cnt[eng] + 1)
        self._record(ev, r, w)
        self.nins += 1
        return ins

    def dma(self, out, in_, r=(), w=(), eng="sp", **kw):
        i = self.drr
        self.drr = (self.drr + 1) % len(self.dsem)
        extra = [("d", i, self.dval[i])] if self.dval[i] else []
        self._waits(eng, r, w, extra)
        self.dval[i] += 16
        self.E[eng].dma_start(out=out, in_=in_, **kw).then_inc(self.dsem[i], 16)
        self.log.append((eng, 'inc', ('d', i), 16, 'dma'))
        self._record(("d", i, self.dval[i]), r, w)
        self.nins += 1

    def idma(self, out, in_, idx, r=(), w=()):
        eng = "pool"
        i = self.drr
        self.drr = (self.drr + 1) % len(self.dsem)
        extra = [("d", i, self.dval[i])] if self.dval[i] else []
        self._waits(eng, r, w, extra)
        self.dval[i] += 16
        self.E[eng].indirect_dma_start(out=out, out_offset=None, in_=in_,
                                       in_offset=bass.IndirectOffsetOnAxis(ap=idx, axis=0)).then_inc(self.dsem[i], 16)
        self.log.append((eng, 'inc', ('d', i), 16, 'idma'))
        self._record(("d", i, self.dval[i]), r, w)
        self.nins += 1

    def barrier(self):
        for eng, e in self.E.items():
            for o in self.E:
                if o != eng and self.cnt[o] > self.seen[eng].get(("e", o), 0):
                    self.seen[eng][("e", o)] = self.cnt[o]
                    e.wait_ge(self.sem[o], self.cnt[o])
                    self.log.append((eng, 'wait', ('e', o), self.cnt[o]))
            for i, v in enumerate(self.dval):
                if v > self.seen[eng].get(("d", i), 0):
                    self.seen[eng][("d", i)] = v
                    e.wait_ge(self.dsem[i], v)
                    self.log.append((eng, 'wait', ('d', i), v))
        self.lastw = {}
        self.readers = {}

    def finish(self):
        e = self.E["sp"]
        for i, v in enumerate(self.dval):
            if v:
                e.wait_ge(self.dsem[i], v)
        for o in self.E:
            if o != "sp" and self.cnt[o]:
                e.wait_ge(self.sem[o], self.cnt[o])


def build(NPHYS=NPHYS):
    nc = bass.Bass("TRN2", target_bir_lowering=False)

    def din(name, shape, dt=F32):
        return nc.dram_tensor(name, list(shape), dt, kind="ExternalInput").ap()

    def dout(name, shape):
        return nc.dram_tensor(name, list(shape), F32, kind="ExternalOutput").ap()

    x_p = din("x_p", [T, D])
    x_s = din("x_s", [NS, D])
    cache0 = din("cache0", [NPHYS, 128, 1024])
    cache1 = din("cache1", [NPHYS, 128, 1024])
    ptab = din("ptab", [NS, 16], I32)
    pool_st = din("pool_st", [NS, 15 * 512])
    conv_st = din("conv_st", [2, NS, 2 * DFF])
    w_in0 = din("w_in0", [D, 2560])
    w_out0 = din("w_out0", [D, D])
    w_in1 = din("w_in1", [D, 2048])
    w_out1 = din("w_out1", [D, D])
    w_gate = din("w_gate", [2, D, DFF])
    w_up = din("w_up", [2, D, DFF])
    w_down = din("w_down", [2, DFF, D])
    sgu_wT = din("sgu_wT", [128, 4, 128])
    pool_w = din("pool_w", [128, 4, 128])
    gcols = din("gcols", [128, 40])
    convc = din("convc", [128, 2 * 4 * NFC])
    sgu_gain_bc = din("sgu_gain_bc", [128, 512])
    sgu_bcol = din("sgu_bcol", [128, 4])
    sgu_s_w = din("sgu_s_w", [NS, 512])
    sgu_s_b = din("sgu_s_b", [NS, 512])
    pool_scol = din("pool_scol", [128, 4])
    pool_fix = din("pool_fix", [128, 4 * 16])
    ropec = din("ropec", [128, 17, 8])
    ropes = din("ropes", [128, 17, 8])

    y_p = dout("y_p", [T, D])
    y_s = dout("y_s", [NS, D])
    kv0_p = dout("kv0_p", [T, 1024])
    kv0_s = dout("kv0_s", [NS, 1024])
    sgu_s = dout("sgu_s", [NS, 512])
    kv1_p = dout("kv1_p", [T, 1024])
    kv1_s = dout("kv1_s", [NS, 1024])
    pool_p = dout("pool_p", [15, 512])
    pool_s = dout("pool_s", [NS, 15 * 512])
    conv_p = dout("conv_p", [2, 2, DFF])
    conv_s = dout("conv_s", [2, NS, 2 * DFF])
    dscr = nc.dram_tensor("dscr", [128, 4, TT], BF16, kind="Internal").ap()
    qscr0 = nc.dram_tensor("qscr0", [NS, 512], F32, kind="Internal").ap()
    qscr1 = nc.dram_tensor("qscr1", [NS, 512], F32, kind="Internal").ap()

    es = ExitStack()
    with es:
        P = Prog(nc, es)

        def sb(name, shape, dt=F32, stack=es):
            return stack.enter_context(nc.sbuf_tensor(name, list(shape), dt))

        banks = [es.enter_context(nc.psum_tensor("ps%d" % i, [128, 512], F32)) for i in range(8)]
        banksb = [b.bitcast(BF16) for b in banks]

        def bkey(i):
            return ("ps", i)

        xT = sb("xT", [128, 8, TT])
        hT = sb("hT", [128, 8, TT], BF16)
        ident_b = sb("ident_b", [128, 128], BF16)
        ident_f = sb("ident_f", [128, 128])
        ones_b = sb("ones_b", [128, 128], BF16)
        tri_b = sb("tri_b", [128, 128], BF16)
        tris_b = sb("tris_b", [128, 128], BF16)
        tris_f = sb("tris_f", [128, 128])
        gcol_t = sb("gcol_t", [128, 40])
        convc_t = sb("convc_t", [128, 2, 4, NFC])
        ropec_t = sb("ropec_t", [128, 17, 8])
        ropes_t = sb("ropes_t", [128, 17, 8])
        rs_t = sb("rs_t", [128, 2, 512])
        sq_t = sb("sq_t", [128, 4, 512], BF16)
        eps_t = sb("eps_t", [128, 1])

        def tkey(t):
            return ("hT", t)

        def gtiles(g):
            return list(range(4 * g, 4 * g + 4)) if g < 4 else [16]

        def gcolsl(g):
            return slice(512 * g, 512 * g + 512) if g < 4 else slice(T, TT)

        def tcols(t):
            return slice(128 * t, 128 * t + 128) if t < 16 else slice(T, TT)

        def trows(t):
            return 128 if t < 16 else NS

        def hkeys(g):
            return [tkey(t) for t in gtiles(g)]

        for tl in (ident_b, ident_f):
            P.op("pool", "memset", w=[tl.name], ap=tl[:], constant=1.0)
            P.op("pool", "affine_select", r=[tl.name], w=[tl.name], out=tl[:], in_=tl[:], pattern=[[-1, 128]],
                 compare_op=ALU.is_equal, fill=0.0, base=0, channel_multiplier=1)
        P.op("pool", "memset", w=["ones_b"], ap=ones_b[:], constant=1.0)
        P.op("pool", "memset", w=["tri_b"], ap=tri_b[:], constant=1.0)
        P.op("pool", "affine_select", r=["tri_b"], w=["tri_b"], out=tri_b[:], in_=tri_b[:], pattern=[[-1, 128]],
             compare_op=ALU.is_ge, fill=0.0, base=0, channel_multiplier=1)
        for tl in (tris_b, tris_f):
            P.op("pool", "memset", w=[tl.name], ap=tl[:], constant=1.0)
            P.op("pool", "affine_select", r=[tl.name], w=[tl.name], out=tl[:], in_=tl[:], pattern=[[-1, 128]],
                 compare_op=ALU.is_gt, fill=0.0, base=0, channel_multiplier=1)
        P.op("pool", "memset", w=["eps"], ap=eps_t[:], constant=EPS)
        P.dma(gcol_t[:], gcols[:, :], w=["gcol"])
        P.dma(convc_t[:].rearrange("p a b c -> p (a b c)"), convc[:, :], w=["convc"])
        P.dma(ropec_t[:], ropec[:, :, :], w=["rope"])
        P.dma(ropes_t[:], ropes[:, :, :], w=["rope"])

        rr = {"wst": 0, "wbf": 0, "sq": 0, "ev": 0, "bk": 0}
        W = {}

        def evac_eng():
            rr["ev"] ^= 1
            return "act" if rr["ev"] else "dve"

        def copy(eng, out, in_, r, w):
            if eng == "act":
                P.op("act", "activation", r=r, w=w, out=out, in_=in_, func=AF.Copy)
            else:
                P.op(eng, "tensor_copy", r=r, w=w, out=out, in_=in_)

        def nextbank(lo, hi):
            b = lo + rr["bk"] % (hi - lo)
            rr["bk"] += 1
            return b

        def walloc(stack, nb):
            W["wst"] = sb("wst%d" % P.nins, [128, 2, 512], F32, stack)
            W["wbf"] = sb("wbf%d" % P.nins, [128, nb, 8 * 512], BF16, stack)
            W["nb"] = nb
            rr["wbf"] = 0

        def load_w(src, nk=8, ncols=512):
            b = rr["wbf"]
            rr["wbf"] = (b + 1) % W["nb"]
            key = ("wbf", b)
            view = W["wbf"][:, b, 0:nk * ncols].rearrange("p (k c) -> p k c", k=nk)
            for k in range(nk):
                for c0 in range(0, ncols, 512):
                    n = min(512, ncols - c0)
                    s = rr["wst"]
                    rr["wst"] ^= 1
                    P.dma(W["wst"][:, s, 0:n], src[k * 128:(k + 1) * 128, c0:c0 + n], w=[("wst", s)])
                    P.op("pool", "tensor_copy", r=[("wst", s)], w=[key], out=view[:, k, c0:c0 + n], in_=W["wst"][:, s, 0:n])
            return view, key

        def rmsnorm_stats(g):
            cs = gcolsl(g)
            n = cs.stop - cs.start
            bk = 6 + (g % 2)
            for c in range(8):
                s = rr["sq"]
                rr["sq"] = (s + 1) % 4
                P.op("act", "activation", r=[("xT", g)], w=[("sq", s)], out=sq_t[:, s, 0:n], in_=xT[:, c, cs], func=AF.Square)
                P.op("pe", "matmul", r=[("sq", s), "ones_b"], w=[bkey(bk)], out=banks[bk][:, 0:n],
                     lhsT=ones_b[:], rhs=sq_t[:, s, 0:n], start=(c == 0), stop=(c == 7))
            ri = g % 2
            P.op("act", "activation", r=[bkey(bk), "eps"], w=[("rs", ri)], out=rs_t[:, ri, 0:n], in_=banks[bk][:, 0:n],
                 func=AF.Sqrt, scale=1.0 / D, bias=eps_t[:, 0:1])
            P.op("dve", "reciprocal", r=[("rs", ri)], w=[("rs", ri)], out=rs_t[:, ri, 0:n], in_=rs_t[:, ri, 0:n])
            return ri, n, cs

        def rmsnorm(nidx):
            for g in range(5):
                ri, n, cs = rmsnorm_stats(g)
                for c in range(8):
                    P.op("dve", "scalar_tensor_tensor", r=[("xT", g), ("rs", ri), "gcol"], w=hkeys(g),
                         out=hT[:, c, cs], in0=xT[:, c, cs], scalar=gcol_t[:, nidx * 8 + c:nidx * 8 + c + 1],
                         in1=rs_t[:, ri, 0:n], op0=ALU.mult, op1=ALU.mult)

        def transpose_to_hT(src_tile, src_key, rows, c0, t, eng=None):
            bk = nextbank(4, 6)
            for j in range(4):
                P.op("pe", "transpose", r=[src_key, "ident_b"], w=[bkey(bk)], out=banksb[bk][:, j * 128:j * 128 + rows],
                     in_=src_tile[:, j * 128:(j + 1) * 128], identity=ident_b[0:rows, 0:rows])
            copy(eng or evac_eng(), hT[:, c0:c0 + 4, tcols(t)],
                 banksb[bk][:, 0:512].rearrange("p (j q) -> p j q", j=4)[:, :, 0:rows], r=[bkey(bk)], w=[tkey(t)])

        def out_proj(wsrc):
            for dcg in range(2):
                wv, wk = load_w(wsrc[:, dcg * 512:(dcg + 1) * 512])
                for j in range(4):
                    dc = dcg * 4 + j
                    for g in range(5):
                        cs = gcolsl(g)
                        n = cs.stop - cs.start
                        bk = nextbank(0, 4)
                        for kc in range(8):
                            P.op("pe", "matmul", r=hkeys(g) + [wk], w=[bkey(bk)], out=banks[bk][:, 0:n],
                                 lhsT=wv[:, kc, j * 128:(j + 1) * 128], rhs=hT[:, kc, cs], start=(kc == 0), stop=(kc == 7))
                        P.op("dve", "tensor_tensor", r=[bkey(bk), ("xT", g)], w=[("xT", g)], out=xT[:, dc, cs],
                             in0=xT[:, dc, cs], in1=banks[bk][:, 0:n], op=ALU.add)

        with ExitStack() as ph:
            xin = sb("xin", [128, 2, D], stack=ph)
            for t in range(17):
                rows = trows(t)
                s = t % 2
                src = x_p[t * 128:(t + 1) * 128, :] if t < 16 else x_s[:, :]
                P.dma(xin[0:rows, s, :], src, w=[("xin", s)])
                for half in range(2):
                    bk = (2 * t + half) % 4
                    for j in range(4):
                        c = half * 4 + j
                        P.op("pe", "transpose", r=[("xin", s), "ident_f"], w=[bkey(bk)],
                             out=banks[bk][:, j * 128:j * 128 + rows], in_=xin[0:rows, s, c * 128:(c + 1) * 128],
                             identity=ident_f[0:rows, 0:rows])
                    copy(evac_eng(), xT[:, half * 4:half * 4 + 4, tcols(t)],
                         banks[bk][:].rearrange("p (j q) -> p j q", j=4)[:, :, 0:rows], r=[bkey(bk)], w=[("xT", min(t // 4, 4))])
            P.barrier()

        def qkv_proj(ph, wsrc, layer, kvp, kvs, QT, KT, Vt, qs_tok, ks_tok, vs_tok):
            stg = sb("stg%d" % layer, [128, 2, 512], F32, ph)
            bfs = sb("bfs%d" % layer, [128, 2, 512], BF16, ph)
            rt = sb("rt%d" % layer, [128, 4, 64], F32, ph)
            for cg in range(3):
                wv, wk = load_w(wsrc[:, cg * 512:(cg + 1) * 512])
                for t in range(17):
                    rows = trows(t)
                    bk = nextbank(0, 4)
                    s = t % 2
                    for kc in range(8):
                        P.op("pe", "matmul", r=[tkey(t), wk], w=[bkey(bk)], out=banks[bk][0:rows, :],
                             lhsT=hT[:, kc, tcols(t)], rhs=wv[:, kc, :], start=(kc == 0), stop=(kc == 7))
                    sk = ("stg", s)
                    P.op("act", "activation", r=[bkey(bk)], w=[sk], out=stg[0:rows, s, :], in_=banks[bk][0:rows, :], func=AF.Copy)
                    if layer == 0 and cg < 2:
                        X = stg[0:rows, s, :].rearrange("p (h d) -> p h d", h=8)
                        x1 = X[:, :, 0:8]
                        x2 = X[:, :, 8:16]
                        ca_ = ropec_t[0:rows, t, :]
                        sa_ = ropes_t[0:rows, t, :]
                        cb = bass.AP(ropec_t, ca_.offset, [list(ca_.ap[0]), [0, 8], [1, 8]])
                        sbb = bass.AP(ropes_t, sa_.offset, [list(sa_.ap[0]), [0, 8], [1, 8]])
                        tv = [rt[0:rows, i, :].rearrange("p (h d) -> p h d", h=8) for i in range(4)]
                        P.op("dve", "tensor_tensor", r=[sk, "rope"], w=["rt0"], out=tv[0], in0=x1, in1=cb, op=ALU.mult)
                        P.op("dve", "tensor_tensor", r=[sk, "rope"], w=["rt1"], out=tv[1], in0=x2, in1=sbb, op=ALU.mult)
                        P.op("dve", "tensor_tensor", r=[sk, "rope"], w=["rt2"], out=tv[2], in0=x2, in1=cb, op=ALU.mult)
                        P.op("dve", "tensor_tensor", r=[sk, "rope"], w=["rt3"], out=tv[3], in0=x1, in1=sbb, op=ALU.mult)
                        P.op("dve", "tensor_tensor", r=["rt0", "rt1"], w=[sk], out=x1, in0=tv[0], in1=tv[1], op=ALU.subtract)
                        P.op("dve", "tensor_tensor", r=["rt2", "rt3"], w=[sk], out=x2, in0=tv[2], in1=tv[3], op=ALU.add)
                    if cg >= 1:
                        dst = (kvp[t * 128:(t + 1) * 128, (cg - 1) * 512:cg * 512] if t < 16 else kvs[:, (cg - 1) * 512:cg * 512])
                        P.dma(dst, stg[0:rows, s, :], r=[sk])
                    if t == 16:
                        tok = (qs_tok, ks_tok, vs_tok)[cg]
                        P.op("dve", "tensor_copy", r=[sk], w=[tok.name], out=tok[:], in_=stg[0:rows, s, :])
                        continue
                    if cg == 2:
                        P.op("pool", "tensor_copy", r=[sk], w=[("Vt", t)], out=Vt[:, t, :], in_=stg[:, s, :])
                    else:
                        bkk = ("bfs", s)
                        P.op("pool", "tensor_copy", r=[sk], w=[bkk], out=bfs[:, s, :], in_=stg[:, s, :])
                        dstT = QT if cg == 0 else KT
                        bk2 = nextbank(4, 6)
                        for j in range(4):
                            P.op("pe", "transpose", r=[bkk, "ident_b"], w=[bkey(bk2)], out=banksb[bk2][:, j * 128:(j + 1) * 128],
                                 in_=bfs[:, s, j * 128:(j + 1) * 128], identity=ident_b[:])
                        copy("dve", dstT[:, :, tcols(t)], banksb[bk2][:, 0:512].rearrange("p (j q) -> p j q", j=4),
                             r=[bkey(bk2)], w=[("QT" if cg == 0 else "KT", t)])

        def attention(ph, layer, QT, KT, Vt):
            pexp = sb("pexp%d" % layer, [128, 2, T], BF16, ph)
            PTs = sb("PTs%d" % layer, [128, 2, 512], BF16, ph)
            atok = sb("atok%d" % layer, [128, 2, 512], BF16, ph)
            rsum = sb("rsum%d" % layer, [128, 2, 16], F32, ph)
            rtot = sb("rtot%d" % layer, [128, 2, 2], F32, ph)
            dtmp = sb("dtmp%d" % layer, [128, 2, 128], BF16, ph)
            if layer == 0:
                kms = sb("kms", [128, 4, 8], F32, ph)
                kmT = sb("kmT", [128, 4, 8], BF16, ph)
                gsb = sb("gsb", [128, 8, 8], F32, ph)
                m8 = sb("m8", [128, 8, 8], F32, ph)
                bias_t = sb("bias_t", [128, 2, 64], F32, ph)
                for c in range(4):
                    P.op("dve", "tensor_reduce", r=[("KT", t) for t in range(16)], w=["kms"], out=kms[:, c, :],
                         in_=KT[:, c, :].rearrange("p (n s) -> p n s", s=256), axis=AX.X, op=ALU.add)
                P.op("act", "activation", r=["kms"], w=["kmT"], out=kmT[:], in_=kms[:], func=AF.Copy, scale=1.0 / 256)
                KMb = sb("KMb", [128, 4, 64], BF16, ph)
                P.op("pool", "memset", w=["KMb"], ap=KMb[:], constant=0.0)
                for c in range(4):
                    P.op("dve", "tensor_copy", r=["kmT"], w=["KMb"], out=KMb[0:64, c, (2 * c) * 8:(2 * c) * 8 + 8], in_=kmT[0:64, c, :])
                    P.op("dve", "tensor_copy", r=["kmT"], w=["KMb"], out=KMb[64:128, c, (2 * c + 1) * 8:(2 * c + 1) * 8 + 8], in_=kmT[64:128, c, :])
            else:
                sprow = sb("sprow", [128, 2, 1 + T], F32, ph)
                csx = sb("csx", [128, 2, T], F32, ph)
                etmp_flat = sq_t.bitcast(F32)[:].rearrange("p a b -> p (a b)")
                negT = sb("negT", [128, 2], F32, ph)
                carry = sb("carry", [128, 2, 8], F32, ph)
                P.op("pool", "memset", w=[("sprow", 0), ("sprow", 1)], ap=sprow[:, :, 0:1], constant=0.0)
            PI = {"v": 0}
            for i in range(16):
                G = i // 2
                qk = [("QT", i)]
                W_ = 128 * (i + 1)
                if layer == 0:
                    bsl = i % 2
                    bkg = 6
                    for c in range(4):
                        P.op("pe", "matmul", r=qk + ["KMb"], w=[bkey(bkg)], out=banks[bkg][:, 0:64],
                             lhsT=QT[:, c, tcols(i)], rhs=KMb[:, c, :], start=(c == 0), stop=(c == 3))
                    if G >= 4:
                        P.op("act", "activation", r=[bkey(bkg)], w=["gsb"], out=gsb[:].rearrange("p h n -> p (h n)"),
                             in_=banks[bkg][:, 0:64], func=AF.Copy)
                        if G < 8:
                            P.op("pool", "memset", r=[], w=["gsb"], ap=gsb[:, :, G:8], constant=-1e30)
                        for h in range(8):
                            P.op("dve", "max", r=["gsb"], w=["m8"], out=m8[:, h, :], in_=gsb[:, h, :])
                        for h in range(8):
                            P.op("dve", "tensor_scalar", r=["gsb", "m8"], w=[("bias", bsl)], out=bias_t[:, bsl, h * 8:(h + 1) * 8],
                                 in0=gsb[:, h, :], scalar1=m8[:, h, 2:3], scalar2=NEG, op0=ALU.is_lt, op1=ALU.mult)
                    else:
                        P.op("pool", "memset", w=[("bias", bsl)], ap=bias_t[:, bsl, :], constant=0.0)
                HS = {}

                def headA(h):
                    c, po = h // 2, (h % 2) * 64
                    ps_ = PI["v"] % 2
                    PI["v"] += 1
                    pk = ("pexp", ps_)
                    kall = [("KT", t) for t in range(i + 1)]
                    nsl = 0
                    if layer == 0:
                        for kc in range(i + 1):
                            bk = nextbank(0, 4)
                            P.op("pe", "matmul", r=qk + kall, w=[bkey(bk)], out=banks[bk][:, 0:128],
                                 lhsT=QT[po:po + 64, c, tcols(i)], rhs=KT[po:po + 64, c, kc * 128:(kc + 1) * 128], start=True, stop=True)
                            if kc == i:
                                ds_ = ps_
                                P.op("act", "activation", r=[bkey(bk)], w=[("dtmp", ds_)], out=dtmp[:, ds_, :],
                                     in_=banks[bk][:, 0:128], func=AF.Exp, scale=0.125)
                                P.op("dve", "tensor_tensor", r=[("dtmp", ds_), "tri_b"], w=[pk], out=pexp[:, ps_, W_ - 128:W_],
                                     in0=dtmp[:, ds_, :], in1=tri_b[:], op=ALU.mult)
                            elif kc // 2 < G:
                                nb_ = kc // 2
                                P.op("act", "activation", r=[bkey(bk), ("bias", bsl)], w=[pk],
                                     out=pexp[:, ps_, kc * 128:(kc + 1) * 128], in_=banks[bk][:, 0:128], func=AF.Exp,
                                     scale=0.125, bias=bias_t[:, bsl, h * 8 + nb_:h * 8 + nb_ + 1])
                            else:
                                P.op("act", "activation", r=[bkey(bk)], w=[pk], out=pexp[:, ps_, kc * 128:(kc + 1) * 128],
                                     in_=banks[bk][:, 0:128], func=AF.Exp, scale=0.125)
                        P.op("dve", "reduce_sum", r=[pk], w=[("rtot", ps_)], out=rtot[:, ps_, 0:1],
                             in_=pexp[:, ps_, 0:W_], axis=AX.X)
                        P.op("dve", "reciprocal", r=[("rtot", ps_)], w=[("rtot", ps_)], out=rtot[:, ps_, 1:2], in_=rtot[:, ps_, 0:1])
                    else:
                        for s0 in range(0, W_, 512):
                            n = min(512, W_ - s0)
                            bk = nextbank(0, 4)
                            es_ = (s0 // 512) % 2
                            P.op("pe", "matmul", r=qk + kall, w=[bkey(bk)], out=banks[bk][:, 0:n],
                                 lhsT=QT[po:po + 64, c, tcols(i)], rhs=KT[po:po + 64, c, s0:s0 + n], start=True, stop=True)
                            P.op("act", "activation", r=[bkey(bk)], w=[("etmp", es_)], out=etmp_flat[:, es_ * 512:es_ * 512 + n], in_=banks[bk][:, 0:n],
                                 func=AF.Exp, scale=0.125)
                            P.op("act", "activation", r=[("etmp", es_)], w=[("sprow", ps_)], out=sprow[:, ps_, 1 + s0:1 + s0 + n],
                                 in_=etmp_flat[:, es_ * 512:es_ * 512 + n], func=AF.Ln, bias=1.0)
                            if s0 + n == W_:
                                P.op("pool", "tensor_tensor", r=[("sprow", ps_), "tris_f"], w=[("sprow", ps_)], out=sprow[:, ps_, 1 + W_ - 128:1 + W_],
                                     in0=sprow[:, ps_, 1 + W_ - 128:1 + W_], in1=tris_f[:], op=ALU.mult)
                            si_ = s0 // 512
                            init = 0.0 if s0 == 0 else carry[:, ps_, si_ - 1:si_]
                            P.op("dve", "tensor_tensor_scan", r=[("sprow", ps_), ("csx", ps_), ("carry", ps_)], w=[("csx", ps_)], out=csx[:, ps_, s0:s0 + n],
                                 data0=sprow[:, ps_, s0:s0 + n], data1=sprow[:, ps_, s0:s0 + n], initial=init, op0=ALU.add, op1=ALU.bypass)
                            P.op("dve", "tensor_copy", r=[("csx", ps_)], w=[("carry", ps_)], out=carry[:, ps_, si_:si_ + 1], in_=csx[:, ps_, s0 + n - 1:s0 + n])
                            if s0 + n == W_:
                                P.op("dve", "tensor_scalar", r=[("csx", ps_)], w=[("negT", ps_)], out=negT[:, ps_:ps_ + 1], in0=csx[:, ps_, W_ - 1:W_],
                                     scalar1=-1.0, scalar2=None, op0=ALU.mult)
                            P.op("dve", "scalar_tensor_tensor", r=[bkey(bk), ("csx", ps_)], w=[("csx", ps_)], out=csx[:, ps_, s0:s0 + n],
                                 in0=banks[bk][:, 0:n], scalar=0.125, in1=csx[:, ps_, s0:s0 + n], op0=ALU.mult, op1=ALU.add)
                        for s0 in range(0, W_, 512):
                            n = min(512, W_ - s0)
                            P.op("act", "activation", r=[("csx", ps_), ("negT", ps_)], w=[pk], out=pexp[:, ps_, s0:s0 + n], in_=csx[:, ps_, s0:s0 + n],
                                 func=AF.Exp, bias=negT[:, ps_:ps_ + 1])
                        P.op("pool", "tensor_tensor", r=[pk, "tris_b"], w=[pk], out=pexp[:, ps_, W_ - 128:W_],
                             in0=pexp[:, ps_, W_ - 128:W_], in1=tris_b[:], op=ALU.mult)
                    HS[h] = (ps_, pk)

                def headB(h):
                    c, po = h // 2, (h % 2) * 64
                    ps_, pk = HS[h]
                    bo = 6 + ps_ if layer == 1 else 7
                    for k0 in range(0, i + 1, 4):
                        nk_ = min(4, i + 1 - k0)
                        bk = nextbank(4, 6)
                        pts = (k0 // 4) % 2
                        for j in range(nk_):
                            P.op("pe", "transpose", r=[pk, "ident_b"], w=[bkey(bk)], out=banksb[bk][:, j * 128:(j + 1) * 128],
                                 in_=pexp[:, ps_, (k0 + j) * 128:(k0 + j + 1) * 128], identity=ident_b[:])
                        copy(evac_eng(), PTs[:, pts, 0:nk_ * 128], banksb[bk][:, 0:nk_ * 128], r=[bkey(bk)], w=[("PTs", pts)])
                        for j in range(nk_):
                            kc = k0 + j
                            P.op("pe", "matmul", r=[("PTs", pts), ("Vt", kc)], w=[bkey(bo)], out=banks[bo][:, h * 64:(h + 1) * 64],
                                 lhsT=PTs[:, pts, j * 128:(j + 1) * 128], rhs=Vt[:, kc, h * 64:(h + 1) * 64],
                                 start=(kc == 0), stop=(kc == i))
                    asl = i % 2
                    if layer == 0:
                        P.op("act", "activation", r=[bkey(bo), ("rtot", ps_)], w=[("atok", asl)], out=atok[:, asl, h * 64:(h + 1) * 64],
                             in_=banks[bo][:, h * 64:(h + 1) * 64], func=AF.Copy, scale=rtot[:, ps_, 1:2])
                    else:
                        P.op("act", "activation", r=[bkey(bo)], w=[("atok", asl)], out=atok[:, asl, h * 64:(h + 1) * 64],
                             in_=banks[bo][:, h * 64:(h + 1) * 64], func=AF.Copy)

                headA(0)
                for h in range(8):
                    if h + 1 < 8:
                        headA(h + 1)
                    headB(h)
                transpose_to_hT(atok[:, i % 2, :], ("atok", i % 2), 128, 0, i)

        def sample_attention(ph, layer, cache, qtok, kvs_dram, qscr):
            L = "s%d" % layer
            NV = 26
            NK = 6
            vbuf = sb("vbuf" + L, [128, NV, 512], F32, ph)
            kbuf = sb("kbuf" + L, [128, NK, 512], F32, ph)
            ids_i = sb("ids_i" + L, [128, 16], I32, ph)
            idf = sb("idf" + L, [128, 16], F32, ph)
            idx = sb("idx" + L, [128, 2, 2, 16], I32, ph)
            idf2 = sb("idf2" + L, [128, 2, 16], F32, ph)
            pcol = sb("pcol" + L, [128, 1], F32, ph)
            pci = sb("pci" + L, [128, 1], I32, ph)
            rows = cache.rearrange("n t (two c) -> (n t two) c", two=2)
            P.op("pool", "iota", w=["pci"], out=pci[:], pattern=[[0, 1]], base=0, channel_multiplier=1)
            P.op("pool", "tensor_copy", r=["pci"], w=["pcol"], out=pcol[:], in_=pci[:])
            qb = sb("qb" + L, [128, 2, 512], F32, ph)
            prod = sb("prod" + L, [128, 2, 512], F32, ph)
            S_all = sb("S_all" + L, [128, 2, 17, 8], F32, ph)
            Gs = sb("Gs" + L, [128, 16, 8], F32, ph)
            gate = sb("gate" + L, [128, 8, 8], F32, ph)
            m8s = sb("m8s" + L, [128, 8, 8], F32, ph)
            biasb = sb("biasb" + L, [128, 8, 8], F32, ph)
            arg = sb("arg" + L, [128, 17, 8], F32, ph)
            carry_ = sb("carry_" + L, [128, 16, 8], F32, ph)
            Zp = sb("Zp" + L, [128, 1, 17, 128], F32, ph)
            Opad = sb("Opad" + L, [128, 512], F32, ph)
            kself = sb("kself" + L, [128, 512], F32, ph)
            vself = sb("vself" + L, [128, 512], F32, ph)
            rden = sb("rden" + L, [128, 2], F32, ph)
            ones_f = sb("ones_f" + L, [128, 128], F32, ph)
            tri_f = sb("tri_f" + L, [128, 128], F32, ph)
            P.op("pool", "memset", w=["Zp"], ap=Zp[:], constant=0.0)
            P.op("pool", "memset", w=["Opad"], ap=Opad[:], constant=0.0)
            P.op("pool", "memset", w=["kself"], ap=kself[:], constant=0.0)
            P.op("pool", "memset", w=["vself"], ap=vself[:], constant=0.0)
            P.op("pool", "memset", w=["ones_f"], ap=ones_f[:], constant=1.0)
            P.op("pool", "memset", w=["tri_f"], ap=tri_f[:], constant=1.0)
            P.op("pool", "affine_select", r=["tri_f"], w=["tri_f"], out=tri_f[:], in_=tri_f[:], pattern=[[-1, 128]],
                 compare_op=ALU.is_ge, fill=0.0, base=0, channel_multiplier=1)
            P.op("pool", "memset", w=["carry_"], ap=carry_[:], constant=0.0)
            P.dma(qscr[:, :], qtok[:], r=[qtok.name], w=["qscr"])
            npg = 17 if layer == 0 else 16
            cnt = {"k": 0, "v": 0, "r": 0}
            VS = {}
            def scores(s):
                z = s % 2
                P.dma(qb[:, z, :], bass.AP(qscr.tensor, s * 512, [[0, 128], [1, 512]]), r=["qscr"], w=[("qb", z)])
                P.dma(ids_i[:], bass.AP(ptab.tensor, s * 16, [[0, 128], [1, 16]]), w=["ids_i"])
                if layer == 0:
                    P.dma(kself[0:1, :], kvs_dram[s:s + 1, 0:512], w=["kself"])
                    P.dma(vself[0:1, :], kvs_dram[s:s + 1, 512:1024], w=["vself"])
                P.op("pool", "tensor_copy", r=["ids_i"], w=["idf"], out=idf[:], in_=ids_i[:])
                P.op("pool", "tensor_scalar", r=["idf", "pcol"], w=["idf"], out=idf[:], in0=idf[:], scalar1=128.0, scalar2=pcol[:, 0:1],
                     op0=ALU.mult, op1=ALU.add)
                P.op("pool", "tensor_scalar", r=["idf"], w=["idf2"], out=idf2[:, 0, :], in0=idf[:], scalar1=2.0, scalar2=None, op0=ALU.mult)
                P.op("pool", "tensor_scalar", r=["idf"], w=["idf2"], out=idf2[:, 1, :], in0=idf[:], scalar1=2.0, scalar2=1.0, op0=ALU.mult, op1=ALU.add)
                P.op("pool", "tensor_copy", r=["idf2"], w=[("idx", z)], out=idx[:, z, :, :], in_=idf2[:])
                vsl = []
                VS[s] = vsl
                for j in range(npg):
                    if j < 16:
                        vs_ = cnt["v"] % NV
                        cnt["v"] += 1
                        ks_ = cnt["k"] % NK
                        cnt["k"] += 1
                        P.idma(kbuf[:, ks_, :], rows[:, :], idx[:, z, 0, j:j + 1], r=[("idx", z)], w=[("kb", ks_)])
                        P.idma(vbuf[:, vs_, :], rows[:, :], idx[:, z, 1, j:j + 1], r=[("idx", z)], w=[("vb", vs_)])
                        ksrc, kkey = kbuf[:, ks_, :], ("kb", ks_)
                        vsl.append((vbuf[:, vs_, :], ("vb", vs_)))
                    else:
                        ksrc, kkey = kself[:], "kself"
                        vsl.append((vself[:], "vself"))
                    pr = j % 2
                    P.op("dve", "tensor_tensor", r=[kkey, ("qb", z)], w=[("prod", pr)], out=prod[:, pr, :], in0=ksrc, in1=qb[:, z, :], op=ALU.mult)
                    P.op("dve", "tensor_reduce", r=[("prod", pr)], w=[("S_all", z)], out=S_all[:, z, j, :],
                         in_=prod[:, pr, :].rearrange("p (h d) -> p h d", h=8), axis=AX.X, op=ALU.add)

            def mid(s):
                z = s % 2
                sk = ("S_all", z)
                zk = ("Zp", 0)
                bn_, bd_ = 2 + (s % 2), 4
                vsl = VS[s]
                Sf = S_all[:, z, 0:16, :].rearrange("p a h -> p (a h)")
                sk = ("S_all", z)
                zk = ("Zp", 0)
                if layer == 0:
                    P.op("pe", "matmul", r=[sk, "ones_f"], w=[bkey(0)], out=banks[0][:, 0:128], lhsT=ones_f[:], rhs=Sf, start=True, stop=True)
                    P.op("act", "activation", r=[bkey(0)], w=["Gs"], out=Gs[:].rearrange("p a h -> p (a h)"), in_=banks[0][:, 0:128], func=AF.Copy)
                    Gv = Gs[:].rearrange("p (n i) h -> p n i h", i=2)
                    P.op("dve", "tensor_tensor", r=["Gs"], w=["gate"], out=gate[:], in0=Gv[:, :, 0, :], in1=Gv[:, :, 1, :], op=ALU.add)
                    for h in range(8):
                        P.op("dve", "max", r=["gate"], w=["m8s"], out=m8s[:, h, :], in_=gate[:, :, h])
                    for h in range(8):
                        P.op("dve", "tensor_scalar", r=["gate", "m8s"], w=["biasb"], out=biasb[:, :, h], in0=gate[:, :, h],
                             scalar1=m8s[:, h, 2:3], scalar2=NEG, op0=ALU.is_lt, op1=ALU.mult)
                    Sv = S_all[:, z, 0:16, :].rearrange("p (n i) h -> p n i h", i=2)
                    Av = arg[:, 0:16, :].rearrange("p (n i) h -> p n i h", i=2)
                    for i_ in range(2):
                        P.op("dve", "scalar_tensor_tensor", r=[sk, "biasb"], w=["arg"], out=Av[:, :, i_, :], in0=Sv[:, :, i_, :], scalar=0.125,
                             in1=biasb[:], op0=ALU.mult, op1=ALU.add)
                    P.op("dve", "tensor_scalar", r=[sk], w=["arg"], out=arg[:, 16, :], in0=S_all[:, z, 16, :], scalar1=0.125, scalar2=None, op0=ALU.mult)
                    P.op("act", "activation", r=["arg"], w=[zk], out=Zp[:, 0, :, 0:8], in_=arg[:], func=AF.Exp)
                    P.op("dve", "tensor_scalar", r=[zk, "ident_f"], w=[zk], out=Zp[:, 0, 16, 0:8], in0=Zp[:, 0, 16, 0:8], scalar1=ident_f[:, 0:1],
                         scalar2=None, op0=ALU.mult)
                else:
                    af = arg[:, 0:16, :].rearrange("p a h -> p (a h)")
                    gf = Gs[:].rearrange("p a h -> p (a h)")
                    P.op("act", "activation", r=[sk], w=["arg"], out=af, in_=Sf, func=AF.Exp, scale=0.125)
                    P.op("act", "activation", r=["arg"], w=["Gs"], out=gf, in_=af, func=AF.Ln, bias=1.0)
                    P.op("pe", "matmul", r=["Gs", "tri_f"], w=[bkey(0)], out=banks[0][:, 0:128], lhsT=tri_f[:], rhs=gf, start=True, stop=True)
                    P.op("pe", "matmul", r=["Gs", "ones_f"], w=[bkey(1)], out=banks[1][:, 0:128], lhsT=ones_f[:], rhs=gf, start=True, stop=True)
                    P.op("act", "activation", r=[bkey(1)], w=[("prod", 0)], out=prod[:, 0, 0:128], in_=banks[1][:, 0:128], func=AF.Copy)
                    Tv = prod[:, 0, 0:128].rearrange("p (a h) -> p a h", h=8)
                    for pgi in range(14, -1, -1):
                        P.op("dve", "tensor_tensor", r=[("prod", 0), "carry_"], w=["carry_"], out=carry_[:, pgi, :], in0=carry_[:, pgi + 1, :],
                             in1=Tv[:, pgi + 1, :], op=ALU.add)
                    P.op("dve", "scalar_tensor_tensor", r=[sk, bkey(0)], w=["arg"], out=af, in0=Sf, scalar=0.125, in1=banks[0][:, 0:128],
                         op0=ALU.mult, op1=ALU.subtract)
                    P.op("dve", "tensor_tensor", r=["arg", "carry_"], w=["arg"], out=af, in0=af, in1=carry_[:].rearrange("p a h -> p (a h)"), op=ALU.subtract)
                    P.op("act", "activation", r=["arg"], w=[zk], out=Zp[:, 0, 0:16, 0:8], in_=arg[:, 0:16, :], func=AF.Exp)
                bn_, bd_ = 2 + (s % 2), 4
                for j in range(npg):
                    vap, vk = vsl[j]
                    P.op("pe", "matmul", r=[zk, vk], w=[bkey(bn_)], out=banks[bn_][:, :], lhsT=Zp[:, 0, j, :], rhs=vap,
                         start=(j == 0), stop=(j == npg - 1))
                if layer == 0:
                    for j in range(npg):
                        P.op("pe", "matmul", r=[zk, "ones_f"], w=[bkey(bd_)], out=banks[bd_][:, 0:64], lhsT=Zp[:, 0, j, :], rhs=ones_f[:, 0:64],
                             start=(j == 0), stop=(j == npg - 1))

            def finish(s):
                z = s % 2
                sk = ("S_all", z)
                zk = ("Zp", 0)
                bn_, bd_ = 2 + (s % 2), 4
                vsl = VS[s]
                if layer == 0:
                    P.op("dve", "reciprocal", r=[bkey(bd_)], w=["rden"], out=rden[0:8, 0:1], in_=banks[bd_][0:8, 0:1])
                    P.op("dve", "tensor_scalar", r=[bkey(bn_), "rden"], w=["Opad"], out=Opad[0:8, :], in0=banks[bn_][0:8, :], scalar1=rden[0:8, 0:1],
                         scalar2=None, op0=ALU.mult)
                else:
                    P.op("act", "activation", r=[bkey(bn_)], w=["Opad"], out=Opad[0:8, :], in_=banks[bn_][0:8, :], func=AF.Copy)
                bt_ = 5
                for c in range(4):
                    P.op("pe", "transpose", r=["Opad", "ident_f"], w=[bkey(bt_)], out=banks[bt_][:, c * 128:(c + 1) * 128],
                         in_=Opad[:, c * 128:(c + 1) * 128], identity=ident_f[:])
                for c in range(4):
                    P.op("dve", "tensor_copy", r=[bkey(bt_)], w=[tkey(16)], out=hT[0:64, c, T + s:T + s + 1],
                         in_=banks[bt_][0:64, c * 128 + 2 * c:c * 128 + 2 * c + 1])
                    P.op("act", "activation", r=[bkey(bt_)], w=[tkey(16)], out=hT[64:128, c, T + s:T + s + 1],
                         in_=banks[bt_][64:128, c * 128 + 2 * c + 1:c * 128 + 2 * c + 2], func=AF.Copy)


            scores(0)
            for s in range(NS):
                mid(s)
                if s + 1 < NS:
                    scores(s + 1)
                finish(s)

        def ffn(l):
            with ExitStack() as ph:
                walloc(ph, 3)
                aT = sb("aT%d" % l, [128, 4, TT], BF16, ph)
                gbuf = sb("gbuf%d" % l, [128, 2 + T], F32, ph)
                gss = sb("gss%d" % l, [128, 3, NS], F32, ph)
                ctmp = sb("ctmp%d" % l, [128, 2, 512], F32, ph)
                gel = sb("gel%d" % l, [128, 2, 512], BF16, ph)
                cst = sb("cst%d" % l, [NS, 2, 512], F32, ph)
                cvs = sb("cvs%d" % l, [NS, 2, 512], F32, ph)
                cvp = sb("cvp%d" % l, [2, 512], F32, ph)
                P.op("pool", "memset", w=["gbuf"], ap=gbuf[:, 0:2], constant=0.0)
                cst_v = conv_st[l].rearrange("s (r f) -> s r f", r=2)
                cso_v = conv_s[l].rearrange("s (r f) -> s r f", r=2)
                ei = 0
                for fg in range(6):
                    f0 = fg * 512
                    ncols = min(512, DFF - f0)
                    nf = ncols // 128
                    wg, wgk = load_w(w_gate[l][:, f0:f0 + ncols], ncols=ncols)
                    wu, wuk = load_w(w_up[l][:, f0:f0 + ncols], ncols=ncols)
                    P.dma(cst[:, :, 0:ncols], cst_v[:, :, f0:f0 + ncols], w=["cst"])
                    P.op("pool", "tensor_copy", r=["cst"], w=["cvs"], out=cvs[:, 0, 0:ncols], in_=cst[:, 1, 0:ncols])
                    for j in range(nf):
                        fc = fg * 4 + j
                        cw = [convc_t[:, l, i_, fc:fc + 1] for i_ in range(4)]
                        bks = nextbank(4, 6)
                        for r_ in range(2):
                            P.op("pe", "transpose", r=["cst", "ident_f"], w=[bkey(bks)], out=banks[bks][:, r_ * NS:(r_ + 1) * NS],
                                 in_=cst[:, r_, j * 128:(j + 1) * 128], identity=ident_f[0:NS, 0:NS])
                        P.op("dve", "tensor_copy", r=[bkey(bks)], w=["gss"], out=gss[:, 0:2, :].rearrange("p r s -> p (r s)"),
                             in_=banks[bks][:, 0:2 * NS])
                        for g in range(5):
                            cs = gcolsl(g)
                            n = cs.stop - cs.start
                            ba = nextbank(0, 4)
                            for kc in range(8):
                                P.op("pe", "matmul", r=hkeys(g) + [wgk], w=[bkey(ba)], out=banks[ba][:, 0:n],
                                     lhsT=wg[:, kc, j * 128:(j + 1) * 128], rhs=hT[:, kc, cs], start=(kc == 0), stop=(kc == 7))
                            bb = nextbank(0, 4)
                            for kc in range(8):
                                P.op("pe", "matmul", r=hkeys(g) + [wuk], w=[bkey(bb)], out=banks[bb][:, 0:n],
                                     lhsT=wu[:, kc, j * 128:(j + 1) * 128], rhs=hT[:, kc, cs], start=(kc == 0), stop=(kc == 7))
                            e_ = ei % 2
                            ei += 1
                            ck, gk = ("ctmp", e_), ("gel", e_)
                            if g < 4:
                                o = 2 + 512 * g
                                P.op("act", "activation", r=[bkey(ba)], w=["gbuf"], out=gbuf[:, o:o + 512], in_=banks[ba][:, 0:512], func=AF.Copy)
                                srcs = [gbuf[:, o - 2:o + 510], gbuf[:, o - 1:o + 511], gbuf[:, o:o + 512]]
                                sk_ = "gbuf"
                            else:
                                P.op("act", "activation", r=[bkey(ba)], w=["gss"], out=gss[:, 2, :], in_=banks[ba][:, 0:NS], func=AF.Copy)
                                srcs = [gss[:, 0, :], gss[:, 1, :], gss[:, 2, :]]
                                sk_ = "gss"
                            ct = ctmp[:, e_, 0:n]
                            P.op("dve", "tensor_scalar", r=[sk_, "convc"], w=[ck], out=ct, in0=srcs[2], scalar1=cw[2], scalar2=cw[3],
                                 op0=ALU.mult, op1=ALU.add)
                            P.op("dve", "scalar_tensor_tensor", r=[sk_, "convc", ck], w=[ck], out=ct, in0=srcs[1], scalar=cw[1], in1=ct,
                                 op0=ALU.mult, op1=ALU.add)
                            P.op("dve", "scalar_tensor_tensor", r=[sk_, "convc", ck], w=[ck], out=ct, in0=srcs[0], scalar=cw[0], in1=ct,
                                 op0=ALU.mult, op1=ALU.add)
                            P.op("act", "activation", r=[ck], w=[gk], out=gel[:, e_, 0:n], in_=ct, func=AF.Gelu_apprx_tanh)
                            P.op("dve", "tensor_tensor", r=[gk, bkey(bb)], w=[("aT", g)], out=aT[:, j, cs], in0=gel[:, e_, 0:n],
                                 in1=banks[bb][:, 0:n], op=ALU.mult)
                        bko = nextbank(4, 6)
                        P.op("pe", "transpose", r=["gbuf", "ident_f"], w=[bkey(bko)], out=banks[bko][0:2, 0:128],
                             in_=gbuf[:, T:T + 2], identity=ident_f[:])
                        P.op("pe", "transpose", r=["gss", "ident_f"], w=[bkey(bko)], out=banks[bko][0:NS, 128:256],
                             in_=gss[:, 2, :], identity=ident_f[:])
                        P.op("dve", "tensor_copy", r=[bkey(bko)], w=["cvp"], out=cvp[:, j * 128:(j + 1) * 128], in_=banks[bko][0:2, 0:128])
                        P.op("dve", "tensor_copy", r=[bkey(bko)], w=["cvs"], out=cvs[:, 1, j * 128:(j + 1) * 128], in_=banks[bko][0:NS, 128:256])
                    P.dma(conv_p[l][:, f0:f0 + ncols], cvp[:, 0:ncols], r=["cvp"])
                    P.dma(cso_v[:, :, f0:f0 + ncols], cvs[:, :, 0:ncols], r=["cvs"])
                    wd, wdk = load_w(w_down[l][f0:f0 + ncols, :], nk=nf, ncols=1024)
                    for dc in range(8):
                        for g in range(5):
                            cs = gcolsl(g)
                            n = cs.stop - cs.start
                            bk = nextbank(0, 4)
                            for j in range(nf):
                                P.op("pe", "matmul", r=[("aT", g), wdk], w=[bkey(bk)], out=banks[bk][:, 0:n],
                                     lhsT=wd[:, j, dc * 128:(dc + 1) * 128], rhs=aT[:, j, cs], start=(j == 0), stop=(j == nf - 1))
                            P.op("dve", "tensor_tensor", r=[bkey(bk), ("xT", g)], w=[("xT", g)], out=xT[:, dc, cs],
                                 in0=xT[:, dc, cs], in1=banks[bk][:, 0:n], op=ALU.add)
                P.barrier()

        rmsnorm(0)
        if STAGE >= 1:
            with ExitStack() as ph0:
                qs_tok = sb("qs_tok0", [NS, 512], F32, ph0)
                ks_tok = sb("ks_tok0", [NS, 512], F32, ph0)
                vs_tok = sb("vs_tok0", [NS, 512], F32, ph0)
                phA = ExitStack()
                QT = sb("QT0", [128, 4, T], BF16, phA)
                KT = sb("KT0", [128, 4, T], BF16, phA)
                Vt = sb("Vt0", [128, 16, 512], BF16, phA)
                with ExitStack() as ph:
                    walloc(ph, 2)
                    with ExitStack() as phq:
                        qkv_proj(phq, w_in0, 0, kv0_p, kv0_s, QT, KT, Vt, qs_tok, ks_tok, vs_tok)
                        P.barrier()
                    if STAGE >= 2:
                        sgw_f = sb("sgw_f", [128, 4, 128], F32, ph)
                        sgw_b = sb("sgw_b", [128, 4, 128], BF16, ph)
                        gain_t = sb("gain_t", [128, 512], F32, ph)
                        bcol_t = sb("bcol_t", [128, 4], F32, ph)
                        ssw = sb("ssw", [NS, 512], F32, ph)
                        ssb = sb("ssb", [NS, 512], F32, ph)
                        gst = sb("gst", [128, 2, 512], F32, ph)
                        gbf = sb("gbf", [128, 2, 512], BF16, ph)
                        ubf = sb("ubf", [128, 2, 512], BF16, ph)
                        fbt = sb("fbt", [128, 2, 512], BF16, ph)
                        bst = sb("bst", [128, 2, 8], F32, ph)
                        P.dma(sgw_f[:], sgu_wT[:, :, :], w=["sgw_f"])
                        P.op("pool", "affine_select", r=["sgw_f"], w=["sgw_b"], out=sgw_b[:], in_=sgw_f[:], pattern=[[0, 4], [1, 128]],
                             compare_op=ALU.is_ge, fill=0.0, base=0, channel_multiplier=-1)
                        P.dma(gain_t[:], sgu_gain_bc[:, :], w=["gain"])
                        P.dma(bcol_t[:], sgu_bcol[:, :], w=["bcol"])
                        P.dma(ssw[:], sgu_s_w[:, :], w=["ssw"])
                        P.dma(ssb[:], sgu_s_b[:, :], w=["ssw"])
                        wv_, wvk = load_w(w_in0[:, 2048:2560])
                        wu_, wuk = load_w(w_in0[:, 1536:2048])
                        for t in range(17):
                            rows = trows(t)
                            s = t % 2
                            bk = nextbank(0, 4)
                            for kc in range(8):
                                P.op("pe", "matmul", r=[tkey(t), wvk], w=[bkey(bk)], out=banks[bk][0:rows, :],
                                     lhsT=hT[:, kc, tcols(t)], rhs=wv_[:, kc, :], start=(kc == 0), stop=(kc == 7))
                            bu_ = nextbank(0, 4)
                            for kc in range(8):
                                P.op("pe", "matmul", r=[tkey(t), wuk], w=[bkey(bu_)], out=banks[bu_][0:rows, :],
                                     lhsT=hT[:, kc, tcols(t)], rhs=wu_[:, kc, :], start=(kc == 0), stop=(kc == 7))
                            gk = ("gst", s)
                            P.op("act", "activation", r=[bkey(bk)], w=[gk], out=gst[0:rows, s, :], in_=banks[bk][0:rows, :], func=AF.Gelu_apprx_tanh)
                            P.op("act", "activation", r=[bkey(bu_)], w=[("ubf", s)], out=ubf[0:rows, s, :], in_=banks[bu_][0:rows, :], func=AF.Gelu_apprx_tanh)
                            P.op("dve", "bn_stats", r=[gk], w=[("bst", s)], out=bst[0:rows, s, 0:6], in_=gst[0:rows, s, :])
                            P.op("dve", "bn_aggr", r=[("bst", s)], w=[("bst", s)], out=bst[0:rows, s, 6:8], in_=bst[0:rows, s, 0:6])
                            P.op("act", "activation", r=[("bst", s), "eps"], w=[("bst", s)], out=bst[0:rows, s, 7:8], in_=bst[0:rows, s, 7:8],
                                 func=AF.Sqrt, bias=eps_t[0:rows, 0:1])
                            P.op("dve", "reciprocal", r=[("bst", s)], w=[("bst", s)], out=bst[0:rows, s, 7:8], in_=bst[0:rows, s, 7:8])
                            P.op("dve", "tensor_scalar", r=[gk, ("bst", s)], w=[gk], out=gst[0:rows, s, :], in0=gst[0:rows, s, :],
                                 scalar1=bst[0:rows, s, 6:7], scalar2=bst[0:rows, s, 7:8], op0=ALU.subtract, op1=ALU.mult)
                            P.op("dve", "tensor_tensor", r=[gk, "gain"], w=[gk], out=gst[0:rows, s, :], in0=gst[0:rows, s, :],
                                 in1=gain_t[0:rows, :], op=ALU.mult)
                            fk = ("fbt", s)
                            if t == 16:
                                P.dma(sgu_s[:, :], gst[0:rows, s, :], r=[gk])
                                P.op("dve", "tensor_tensor", r=[gk, "ssw"], w=[gk], out=gst[0:rows, s, :], in0=gst[0:rows, s, :], in1=ssw[:], op=ALU.mult)
                                P.op("dve", "tensor_tensor", r=[gk, "ssw"], w=[gk], out=gst[0:rows, s, :], in0=gst[0:rows, s, :], in1=ssb[:], op=ALU.add)
                                P.op("dve", "tensor_tensor", r=[gk, ("ubf", s)], w=[fk], out=fbt[0:rows, s, :], in0=gst[0:rows, s, :],
                                     in1=ubf[0:rows, s, :], op=ALU.mult)
                            else:
                                P.op("pool", "tensor_copy", r=[gk], w=[("gbf", s)], out=gbf[:, s, :], in_=gst[:, s, :])
                                bf_ = nextbank(4, 6)
                                for g4 in range(4):
                                    P.op("pe", "matmul", r=[("gbf", s), "sgw_b"], w=[bkey(bf_)], out=banks[bf_][:, g4 * 128:(g4 + 1) * 128],
                                         lhsT=sgw_b[:, g4, :], rhs=gbf[:, s, g4 * 128:(g4 + 1) * 128], start=True, stop=True)
                                for g4 in range(4):
                                    P.op("dve", "scalar_tensor_tensor", r=[bkey(bf_), "bcol", ("ubf", s)], w=[fk],
                                         out=fbt[:, s, g4 * 128:(g4 + 1) * 128], in0=banks[bf_][:, g4 * 128:(g4 + 1) * 128],
                                         scalar=bcol_t[:, g4:g4 + 1], in1=ubf[:, s, g4 * 128:(g4 + 1) * 128], op0=ALU.add, op1=ALU.mult)
                            transpose_to_hT(fbt[0:rows, s, :], fk, rows, 4, t)
                    P.barrier()
                if STAGE >= 3:
                    with ExitStack() as ph:
                        attention(ph, 0, QT, KT, Vt)
                        P.barrier()
                P.barrier()
                phA.close()
                if STAGE >= 3 and SAMPLE:
                    with ExitStack() as ph:
                        sample_attention(ph, 0, cache0, qs_tok, kv0_s, qscr0)
                        P.barrier()
            if STAGE >= 4:
                with ExitStack() as ph:
                    walloc(ph, 2)
                    out_proj(w_out0)
                    P.barrier()
        if STAGE >= 5:
            rmsnorm(1)
            ffn(0)
        if STAGE >= 6:
            rmsnorm(2)
            with ExitStack() as ph:
                walloc(ph, 2)
                uT = sb("uT", [128, 4, 15 + T], F32, ph)
                pa = sb("pa", [128, 15 + T], F32, ph)
                pb = sb("pb", [128, 15 + T], F32, ph)
                pT_ = sb("pT_", [128, T], BF16, ph)
                dst_ = sb("dst_", [128, TT], BF16, ph)
                plw_f = sb("plw_f", [128, 4, 128], F32, ph)
                plw_b = sb("plw_b", [128, 4, 128], BF16, ph)
                pscol = sb("pscol", [128, 4], F32, ph)
                pfix = sb("pfix", [128, 64], F32, ph)
                us_tok = sb("us_tok", [NS, 512], F32, ph)
                pst = sb("pst", [NS, 15, 128], F32, ph)
                pls = sb("pls", [NS, 128], F32, ph)
                plsb = sb("plsb", [128, NS], BF16, ph)
                ppo = sb("ppo", [15, 512], F32, ph)
                P.dma(plw_f[:], pool_w[:, :, :], w=["plw_f"])
                P.op("pool", "tensor_copy", r=["plw_f"], w=["plw_b"], out=plw_b[:], in_=plw_f[:])
                P.dma(pscol[:], pool_scol[:, :], w=["pscol"])
                P.dma(pfix[:], pool_fix[:, :], w=["pfix"])
                P.op("pool", "memset", w=["uT"], ap=uT[:, :, 0:15], constant=0.0)
                P.op("pool", "memset", w=["pa"], ap=pa[:, 0:15], constant=0.0)
                P.op("pool", "memset", w=["pb"], ap=pb[:, 0:15], constant=0.0)
                wv_, wvk = load_w(w_in1[:, 1536:2048])
                bk = nextbank(0, 4)
                for kc in range(8):
                    P.op("pe", "matmul", r=[tkey(16), wvk], w=[bkey(bk)], out=banks[bk][0:NS, :], lhsT=hT[:, kc, T:TT], rhs=wv_[:, kc, :],
                         start=(kc == 0), stop=(kc == 7))
                P.op("act", "activation", r=[bkey(bk)], w=["us_tok"], out=us_tok[:], in_=banks[bk][0:NS, :], func=AF.Copy)
                P.dma(pool_s.rearrange("s (r c) -> s r c", r=15)[:, 14, :], us_tok[:], r=["us_tok"])
                pst_v = pool_st.rearrange("s (r c) -> s r c", r=15)
                pso_v = pool_s.rearrange("s (r c) -> s r c", r=15)
                for j in range(4):
                    w_ = (2, 4, 8, 16)[j]
                    for g in range(4):
                        bk = nextbank(0, 4)
                        for kc in range(8):
                            P.op("pe", "matmul", r=hkeys(g) + [wvk], w=[bkey(bk)], out=banks[bk][:, :], lhsT=wv_[:, kc, j * 128:(j + 1) * 128],
                                 rhs=hT[:, kc, gcolsl(g)], start=(kc == 0), stop=(kc == 7))
                        copy(evac_eng(), uT[:, j, 15 + 512 * g:15 + 512 * g + 512], banks[bk][:, :], r=[bkey(bk)], w=["uT"])
                    bko = nextbank(4, 6)
                    P.op("pe", "transpose", r=["uT", "ident_f"], w=[bkey(bko)], out=banks[bko][0:15, 0:128], in_=uT[:, j, T:T + 15],
                         identity=ident_f[:])
                    P.op("dve", "tensor_copy", r=[bkey(bko)], w=["ppo"], out=ppo[:, j * 128:(j + 1) * 128], in_=banks[bko][0:15, 0:128])
                    cur, ck_ = uT[:, j, :], "uT"
                    bufs = [(pa, "pa"), (pb, "pb")]
                    for st_ in range(j + 1):
                        sh = 1 << st_
                        nx, nk2 = bufs[st_ % 2]
                        P.op("dve", "tensor_tensor", r=[ck_], w=[nk2], out=nx[:, 15:15 + T], in0=cur[:, 15:15 + T], in1=cur[:, 15 - sh:15 + T - sh], op=ALU.add)
                        cur, ck_ = nx, nk2
                    oth, ok_ = bufs[(j + 1) % 2]
                    P.op("dve", "scalar_tensor_tensor", r=[ck_, "uT"], w=[ok_], out=oth[:, 15:15 + T], in0=cur[:, 15:15 + T], scalar=1.0 / w_,
                         in1=uT[:, j, 15:15 + T], op0=ALU.mult, op1=ALU.subtract)
                    P.op("dve", "tensor_tensor", r=[ck_, "pfix"], w=[ck_], out=cur[:, 15:31], in0=cur[:, 15:31], in1=pfix[:, j * 16:(j + 1) * 16], op=ALU.mult)
                    P.op("dve", "tensor_tensor", r=[ck_, "uT"], w=[ok_], out=oth[:, 15:31], in0=cur[:, 15:31], in1=uT[:, j, 15:31], op=ALU.subtract)
                    P.op("act", "activation", r=[ok_], w=["pT_"], out=pT_[:], in_=oth[:, 15:15 + T], func=AF.Copy)
                    for g in range(4):
                        bk = nextbank(0, 4)
                        P.op("pe", "matmul", r=["pT_", "plw_b"], w=[bkey(bk)], out=banks[bk][:, :], lhsT=plw_b[:, j, :], rhs=pT_[:, gcolsl(g)],
                             start=True, stop=True)
                        P.op("act", "activation", r=[bkey(bk), "pscol"], w=["dst_"], out=dst_[:, gcolsl(g)], in_=banks[bk][:, :], func=AF.Copy,
                             scale=pscol[:, j:j + 1])
                    P.dma(pst[:], pst_v[:, :, j * 128:(j + 1) * 128], w=["pst"])
                    P.dma(pso_v[:, 0:14, j * 128:(j + 1) * 128], pst[:, 1:15, :], r=["pst"])
                    P.op("dve", "tensor_reduce", r=["pst"], w=["pls"], out=pls[:], in_=pst[:, 16 - w_:15, :].rearrange("p r c -> p c r"),
                         axis=AX.X, op=ALU.add)
                    P.op("dve", "tensor_tensor", r=["pls", "us_tok"], w=["pls"], out=pls[:], in0=pls[:], in1=us_tok[:, j * 128:(j + 1) * 128], op=ALU.add)
                    P.op("dve", "scalar_tensor_tensor", r=["pls", "us_tok"], w=["pls"], out=pls[:], in0=pls[:], scalar=1.0 / w_,
                         in1=us_tok[:, j * 128:(j + 1) * 128], op0=ALU.mult, op1=ALU.subtract)
                    bko = nextbank(4, 6)
                    P.op("pe", "transpose", r=["pls", "ident_f"], w=[bkey(bko)], out=banks[bko][:, 0:NS], in_=pls[:], identity=ident_f[0:NS, 0:NS])
                    P.op("act", "activation", r=[bkey(bko)], w=["plsb"], out=plsb[:], in_=banks[bko][:, 0:NS], func=AF.Copy)
                    bk = nextbank(0, 4)
                    P.op("pe", "matmul", r=["plsb", "plw_b"], w=[bkey(bk)], out=banks[bk][:, 0:NS], lhsT=plw_b[:, j, :], rhs=plsb[:], start=True, stop=True)
                    P.op("act", "activation", r=[bkey(bk), "pscol"], w=["dst_"], out=dst_[:, T:TT], in_=banks[bk][:, 0:NS], func=AF.Copy,
                         scale=pscol[:, j:j + 1])
                    P.dma(dscr[:, j, :], dst_[:], r=["dst_"], w=["dscr"])
                P.dma(pool_p[:, :], ppo[:], r=["ppo"])
                P.barrier()
            with ExitStack() as ph1:
                qs1 = sb("qs_tok1", [NS, 512], F32, ph1)
                ks1 = sb("ks_tok1", [NS, 512], F32, ph1)
                vs1 = sb("vs_tok1", [NS, 512], F32, ph1)
                phB = ExitStack()
                QT = sb("QT1", [128, 4, T], BF16, phB)
                KT = sb("KT1", [128, 4, T], BF16, phB)
                Vt = sb("Vt1", [128, 16, 512], BF16, phB)
                with ExitStack() as ph:
                    walloc(ph, 2)
                    qkv_proj(ph, w_in1, 1, kv1_p, kv1_s, QT, KT, Vt, qs1, ks1, vs1)
                    P.barrier()
                P.dma(hT[:, 4:8, :], dscr[:, :, :], r=["dscr"], w=[tkey(t) for t in range(17)])
                if STAGE >= 7:
                    with ExitStack() as ph:
                        attention(ph, 1, QT, KT, Vt)
                        P.barrier()
                P.barrier()
                phB.close()
                if STAGE >= 7 and SAMPLE:
                    with ExitStack() as ph:
                        sample_attention(ph, 1, cache1, qs1, kv1_s, qscr1)
                        P.barrier()
            if STAGE >= 8:
                with ExitStack() as ph:
                    walloc(ph, 2)
                    out_proj(w_out1)
                    P.barrier()
        if STAGE >= 9:
            rmsnorm(3)
            ffn(1)
        if STAGE >= 10:
            with ExitStack() as ph:
                yT = sb("yT", [128, 8, 512], F32, ph)
                ytok = sb("ytok", [128, 2, D], F32, ph)
                for g in range(5):
                    ri, n, cs = rmsnorm_stats(g)
                    for c in range(8):
                        P.op("dve", "scalar_tensor_tensor", r=[("xT", g), ("rs", ri), "gcol"], w=["yT"],
                             out=yT[:, c, 0:n], in0=xT[:, c, cs], scalar=gcol_t[:, 4 * 8 + c:4 * 8 + c + 1],
                             in1=rs_t[:, ri, 0:n], op0=ALU.mult, op1=ALU.mult)
                    for tt, t in enumerate(gtiles(g)):
                        rows = trows(t)
                        s = t % 2
                        for half in range(2):
                            bk = nextbank(0, 4)
                            for j in range(4):
                                c = half * 4 + j
                                P.op("pe", "transpose", r=["yT", "ident_f"], w=[bkey(bk)], out=banks[bk][0:rows, j * 128:(j + 1) * 128],
                                     in_=yT[:, c, tt * 128:tt * 128 + rows], identity=ident_f[:])
                            copy(evac_eng(), ytok[0:rows, s, half * 512:(half + 1) * 512], banks[bk][0:rows, :], r=[bkey(bk)], w=[("ytok", s)])
                        dst = y_p[t * 128:(t + 1) * 128, :] if t < 16 else y_s[:, :]
                        P.dma(dst, ytok[0:rows, s, :], r=[("ytok", s)])
                P.barrier()

        P.finish()
    print("instructions:", P.nins, {k: v for k, v in P.cnt.items()})
    return nc


def _prep_inputs(inputs):
    f32 = np.float32
    g = lambda k: np.asarray(inputs[k])
    common = {}
    common["cache0"] = np.ascontiguousarray(g("cache_l0_kv"), dtype=f32).reshape(-1, 128, 1024)
    common["cache1"] = np.ascontiguousarray(g("cache_l1_kv"), dtype=f32).reshape(-1, 128, 1024)
    common["w_in0"] = np.ascontiguousarray(g("l0_w_in"), dtype=f32)
    common["w_out0"] = np.ascontiguousarray(g("l0_w_out"), dtype=f32)
    common["w_in1"] = np.ascontiguousarray(g("l1_w_in"), dtype=f32)
    common["w_out1"] = np.ascontiguousarray(g("l1_w_out"), dtype=f32)
    common["w_gate"] = np.ascontiguousarray(g("ffn_w_gate"), dtype=f32)
    common["w_up"] = np.ascontiguousarray(g("ffn_w_up"), dtype=f32)
    common["w_down"] = np.ascontiguousarray(g("ffn_w_down"), dtype=f32)
    common["sgu_wT"] = np.ascontiguousarray(g("l0_sgu_w").transpose(2, 0, 1), dtype=f32)
    common["pool_w"] = np.ascontiguousarray(g("l1_pool_w").transpose(1, 0, 2), dtype=f32)
    norms = np.stack([g("l0_norm"), g("ffn_norm")[0], g("l1_norm"), g("ffn_norm")[1], g("final_norm")], 0)
    common["gcols"] = np.ascontiguousarray(norms.reshape(5, 8, 128).transpose(2, 0, 1).reshape(128, 40), dtype=f32)
    cw = g("ffn_conv_w")
    cb = g("ffn_conv_b")
    cc = np.concatenate([cw, cb[:, None, :]], axis=1)
    common["convc"] = np.ascontiguousarray(cc.reshape(2, 4, NFC, 128).transpose(3, 0, 1, 2).reshape(128, 2 * 4 * NFC), dtype=f32)
    common["sgu_gain_bc"] = np.ascontiguousarray(np.broadcast_to(g("l0_sgu_gain")[None, :], (128, 512)), dtype=f32)
    common["sgu_bcol"] = np.ascontiguousarray(g("l0_sgu_b").T, dtype=f32)
    common["sgu_s_w"] = np.ascontiguousarray(np.broadcast_to(np.repeat(g("l0_sgu_w")[:, 0, 0], 128)[None, :], (NS, 512)), dtype=f32)
    common["sgu_s_b"] = np.ascontiguousarray(np.broadcast_to(np.repeat(g("l0_sgu_b")[:, 0], 128)[None, :], (NS, 512)), dtype=f32)
    common["pool_scol"] = np.ascontiguousarray(g("l1_pool_scale").reshape(4, 128).T, dtype=f32)
    fix = np.ones((4, 16), f32)
    for gi, w in enumerate((2, 4, 8, 16)):
        for t in range(16):
            fix[gi, t] = 1.0 / min(t + 1, w)
    common["pool_fix"] = np.ascontiguousarray(np.broadcast_to(fix.reshape(1, 64), (128, 64)), dtype=f32)
    half = 8
    inv_freq = (np.float32(500000.0) ** (-(np.arange(half, dtype=f32) * np.float32(2.0) / np.float32(16)))).astype(f32)
    pos = np.concatenate([np.arange(T), np.full(128, T)]).astype(f32)
    ang = (pos[:, None] * inv_freq[None, :]).astype(f32)
    common["ropec"] = np.ascontiguousarray(np.cos(ang).astype(f32).reshape(17, 128, 8).transpose(1, 0, 2))
    common["ropes"] = np.ascontiguousarray(np.sin(ang).astype(f32).reshape(17, 128, 8).transpose(1, 0, 2))
    maps = []
    xp = g("x_prompt")
    xs = g("x_sample")
    pt = g("page_table")
    ps_ = g("state_l1_pool")
    cs_ = g("state_ffn_conv")
    for c in range(8):
        m = dict(common)
        sl = slice(NS * c, NS * (c + 1))
        m["x_p"] = np.ascontiguousarray(xp[c], dtype=f32)
        m["x_s"] = np.ascontiguousarray(xs[sl, 0, :], dtype=f32)
        m["ptab"] = np.ascontiguousarray(pt[sl], dtype=np.int32)
        m["pool_st"] = np.ascontiguousarray(ps_[sl].reshape(NS, 15 * 512), dtype=f32)
        m["conv_st"] = np.ascontiguousarray(cs_[:, sl].reshape(2, NS, 2 * DFF), dtype=f32)
        maps.append(m)
    return maps


_NC = None


def kernel(**inputs):
    global _NC
    maps = _prep_inputs(inputs)
    if _NC is None:
        _NC = build(maps[0]["cache0"].shape[0])
    res = run_bass_kernel_spmd(_NC, maps, core_ids=list(range(8)))
    R = res.results
    f32 = np.float32
    cat = lambda k: np.stack([np.asarray(r[k], dtype=f32) for r in R], 0)
    y_p = cat("y_p")
    y_s = cat("y_s").reshape(128, 1, D)
    kv0_p = cat("kv0_p").reshape(8, T, 2, 8, 64)
    kv0_s = cat("kv0_s").reshape(128, 1, 2, 8, 64)
    sgu_s = cat("sgu_s").reshape(128, 1, 512)
    kv1_p = cat("kv1_p").reshape(8, T, 2, 8, 64)
    kv1_s = cat("kv1_s").reshape(128, 1, 2, 8, 64)
    pool_p = cat("pool_p")
    pool_s = cat("pool_s").reshape(128, 15, 512)
    conv_p = cat("conv_p").transpose(1, 0, 2, 3)
    conv_s = cat("conv_s").reshape(8, 2, NS, 2, DFF).transpose(1, 0, 2, 3, 4).reshape(2, 128, 2, DFF)
    return (y_p, y_s, kv0_p, kv0_s, sgu_s, kv1_p, kv1_s, pool_p, pool_s,
            np.ascontiguousarray(conv_p), np.ascontiguousarray(conv_s))
```

```python
import numpy as np
from contextlib import ExitStack
import concourse.bass as bass
import concourse.mybir as mybir
from concourse.bass_utils import run_bass_kernel_spmd

F32 = mybir.dt.float32
BF16 = mybir.dt.bfloat16
I32 = mybir.dt.int32
AF = mybir.ActivationFunctionType
ALU = mybir.AluOpType
AX = mybir.AxisListType

T = 2048
NS = 16
TT = T + NS
D = 1024
DFF = 2816
NFC = 22
EPS = 1e-6
NPHYS = 2560
NEG = -30000.0
STAGE = 99
SAMPLE = True


class Prog:
    def __init__(self, nc, es, ndma=20):
        self.nc = nc
        self.E = {"pe": nc.tensor, "act": nc.scalar, "dve": nc.vector, "pool": nc.gpsimd, "sp": nc.sync}
        self.sem = {k: es.enter_context(nc.semaphore("s_" + k)) for k in self.E}
        self.dsem = [es.enter_context(nc.semaphore("d%d" % i)) for i in range(ndma)]
        self.cnt = {k: 0 for k in self.E}
        self.seen = {k: {} for k in self.E}
        self.dval = [0] * ndma
        self.drr = 0
        self.lastw = {}
        self.readers = {}
        self.nins = 0
        self.log = []
        self._cur = None

    def _waits(self, eng, reads, writes, extra=()):
        need = {}

        def add(ev):
            k = ev[:2]
            if ev[2] > need.get(k, 0):
                need[k] = ev[2]

        for k in reads:
            if k in self.lastw:
                add(self.lastw[k])
        for k in writes:
            if k in self.lastw:
                add(self.lastw[k])
            for ev in self.readers.get(k, ()):
                add(ev)
        for ev in extra:
            add(ev)
        e = self.E[eng]
        for k, v in need.items():
            if k[0] == "e" and k[1] == eng and eng == "pe":
                continue
            if self.seen[eng].get(k, 0) >= v:
                continue
            self.seen[eng][k] = v
            s = self.sem[k[1]] if k[0] == "e" else self.dsem[k[1]]
            e.wait_ge(s, v)
            self.log.append((eng, 'wait', k, v))

    def _record(self, ev, reads, writes):
        for k in reads:
            self.readers.setdefault(k, []).append(ev)
        for k in writes:
            self.lastw[k] = ev
            self.readers[k] = []

    def op(self, eng, name, r=(), w=(), inc=True, **kw):
        self._waits(eng, r, w)
        ins = getattr(self.E[eng], name)(**kw)
        if inc:
            self.cnt[eng] += 1
            ins.then_inc(self.sem[eng], 1)
            self.log.append((eng, 'inc', ('e', eng), 1, name))
            ev = ("e", eng, self.cnt[eng])
        else:
            ev = ("e", eng, self.cnt[eng] + 1)
        self._record(ev, r, w)
        self.nins += 1
        return ins

    def dma(self, out, in_, r=(), w=(), eng="sp", **kw):
        i = self.drr
        self.drr = (self.drr + 1) % len(self.dsem)
        extra = [("d", i, self.dval[i])] if self.dval[i] else []
        self._waits(eng, r, w, extra)
        self.dval[i] += 16
        self.E[eng].dma_start(out=out, in_=in_, **kw).then_inc(self.dsem[i], 16)
        self.log.append((eng, 'inc', ('d', i), 16, 'dma'))
        self._record(("d", i, self.dval[i]), r, w)
        self.nins += 1

    def idma(self, out, in_, idx, r=(), w=()):
        eng = "pool"
        i = self.drr
        self.drr = (self.drr + 1) % len(self.dsem)
        extra = [("d", i, self.dval[i])] if self.dval[i] else []
        self._waits(eng, r, w, extra)
        self.dval[i] += 16
        self.E[eng].indirect_dma_start(out=out, out_offset=None, in_=in_,
                                       in_offset=bass.IndirectOffsetOnAxis(ap=idx, axis=0)).then_inc(self.dsem[i], 16)
        self.log.append((eng, 'inc', ('d', i), 16, 'idma'))
        self._record(("d", i, self.dval[i]), r, w)
        self.nins += 1

    def barrier(self):
        for eng, e in self.E.items():
            for o in self.E:
                if o != eng and self.cnt[o] > self.seen[eng].get(("e", o), 0):
                    self.seen[eng][("e", o)] = self.cnt[o]
                    e.wait_ge(self.sem[o], self.cnt[o])
                    self.log.append((eng, 'wait', ('e', o), self.cnt[o]))
            for i, v in enumerate(self.dval):
                if v > self.seen[eng].get(("d", i), 0):
                    self.seen[eng][("d", i)] = v
                    e.wait_ge(self.dsem[i], v)
                    self.log.append((eng, 'wait', ('d', i), v))
        self.lastw = {}
        self.readers = {}

    def finish(self):
        e = self.E["sp"]
        for i, v in enumerate(self.dval):
            if v:
                e.wait_ge(self.dsem[i], v)
        for o in self.E:
            if o != "sp" and self.cnt[o]:
                e.wait_ge(self.sem[o], self.cnt[o])


def build(NPHYS=NPHYS):
    nc = bass.Bass("TRN2", target_bir_lowering=False)

    def din(name, shape, dt=F32):
        return nc.dram_tensor(name, list(shape), dt, kind="ExternalInput").ap()

    def dout(name, shape):
        return nc.dram_tensor(name, list(shape), F32, kind="ExternalOutput").ap()

    x_p = din("x_p", [T, D])
    x_s = din("x_s", [NS, D])
    cache0 = din("cache0", [NPHYS, 128, 1024])
    cache1 = din("cache1", [NPHYS, 128, 1024])
    ptab = din("ptab", [NS, 16], I32)
    pool_st = din("pool_st", [NS, 15 * 512])
    conv_st = din("conv_st", [2, NS, 2 * DFF])
    w_in0 = din("w_in0", [D, 2560])
    w_out0 = din("w_out0", [D, D])
    w_in1 = din("w_in1", [D, 2048])
    w_out1 = din("w_out1", [D, D])
    w_gate = din("w_gate", [2, D, DFF])
    w_up = din("w_up", [2, D, DFF])
    w_down = din("w_down", [2, DFF, D])
    sgu_wT = din("sgu_wT", [128, 4, 128])
    pool_w = din("pool_w", [128, 4, 128])
    gcols = din("gcols", [128, 40])
    convc = din("convc", [128, 2 * 4 * NFC])
    sgu_gain_bc = din("sgu_gain_bc", [128, 512])
    sgu_bcol = din("sgu_bcol", [128, 4])
    sgu_s_w = din("sgu_s_w", [NS, 512])
    sgu_s_b = din("sgu_s_b", [NS, 512])
    pool_scol = din("pool_scol", [128, 4])
    pool_fix = din("pool_fix", [128, 4 * 16])
    ropec = din("ropec", [128, 17, 8])
    ropes = din("ropes", [128, 17, 8])

    y_p = dout("y_p", [T, D])
    y_s = dout("y_s", [NS, D])
    kv0_p = dout("kv0_p", [T, 1024])
    kv0_s = dout("kv0_s", [NS, 1024])
    sgu_s = dout("sgu_s", [NS, 512])
    kv1_p = dout("kv1_p", [T, 1024])
    kv1_s = dout("kv1_s", [NS, 1024])
    pool_p = dout("pool_p", [15, 512])
    pool_s = dout("pool_s", [NS, 15 * 512])
    conv_p = dout("conv_p", [2, 2, DFF])
    conv_s = dout("conv_s", [2, NS, 2 * DFF])
    dscr = nc.dram_tensor("dscr", [128, 4, TT], BF16, kind="Internal").ap()
    qscr0 = nc.dram_tensor("qscr0", [NS, 512], F32, kind="Internal").ap()
    qscr1 = nc.dram_tensor("qscr1", [NS, 512], F32, kind="Internal").ap()

    es = ExitStack()
    with es:
        P = Prog(nc, es)

        def sb(name, shape, dt=F32, stack=es):
            return stack.enter_context(nc.sbuf_tensor(name, list(shape), dt))

        banks = [es.enter_context(nc.psum_tensor("ps%d" % i, [128, 512], F32)) for i in range(8)]
        banksb = [b.bitcast(BF16) for b in banks]

        def bkey(i):
            return ("ps", i)

        xT = sb("xT", [128, 8, TT])
        hT = sb("hT", [128, 8, TT], BF16)
        ident_b = sb("ident_b", [128, 128], BF16)
        ident_f = sb("ident_f", [128, 128])
        ones_b = sb("ones_b", [128, 128], BF16)
        tri_b = sb("tri_b", [128, 128], BF16)
        tris_b = sb("tris_b", [128, 128], BF16)
        tris_f = sb("tris_f", [128, 128])
        gcol_t = sb("gcol_t", [128, 40])
        convc_t = sb("convc_t", [128, 2, 4, NFC])
        ropec_t = sb("ropec_t", [128, 17, 8])
        ropes_t = sb("ropes_t", [128, 17, 8])
        rs_t = sb("rs_t", [128, 2, 512])
        sq_t = sb("sq_t", [128, 4, 512], BF16)
        eps_t = sb("eps_t", [128, 1])

        def tkey(t):
            return ("hT", t)

        def gtiles(g):
            return list(range(4 * g, 4 * g + 4)) if g < 4 else [16]

        def gcolsl(g):
            return slice(512 * g, 512 * g + 512) if g < 4 else slice(T, TT)

        def tcols(t):
            return slice(128 * t, 128 * t + 128) if t < 16 else slice(T, TT)

        def trows(t):
            return 128 if t < 16 else NS

        def hkeys(g):
            return [tkey(t) for t in gtiles(g)]

        for tl in (ident_b, ident_f):
            P.op("pool", "memset", w=[tl.name], ap=tl[:], constant=1.0)
            P.op("pool", "affine_select", r=[tl.name], w=[tl.name], out=tl[:], in_=tl[:], pattern=[[-1, 128]],
                 compare_op=ALU.is_equal, fill=0.0, base=0, channel_multiplier=1)
        P.op("pool", "memset", w=["ones_b"], ap=ones_b[:], constant=1.0)
        P.op("pool", "memset", w=["tri_b"], ap=tri_b[:], constant=1.0)
        P.op("pool", "affine_select", r=["tri_b"], w=["tri_b"], out=tri_b[:], in_=tri_b[:], pattern=[[-1, 128]],
             compare_op=ALU.is_ge, fill=0.0, base=0, channel_multiplier=1)
        for tl in (tris_b, tris_f):
            P.op("pool", "memset", w=[tl.name], ap=tl[:], constant=1.0)
            P.op("pool", "affine_select", r=[tl.name], w=[tl.name], out=tl[:], in_=tl[:], pattern=[[-1, 128]],
                 compare_op=ALU.is_gt, fill=0.0, base=0, channel_multiplier=1)
        P.op("pool", "memset", w=["eps"], ap=eps_t[:], constant=EPS)
        P.dma(gcol_t[:], gcols[:, :], w=["gcol"])
        P.dma(convc_t[:].rearrange("p a b c -> p (a b c)"), convc[:, :], w=["convc"])
        P.dma(ropec_t[:], ropec[:, :, :], w=["rope"])
        P.dma(ropes_t[:], ropes[:, :, :], w=["rope"])

        rr = {"wst": 0, "wbf": 0, "sq": 0, "ev": 0, "bk": 0}
        W = {}

        def evac_eng():
            rr["ev"] ^= 1
            return "act" if rr["ev"] else "dve"

        def copy(eng, out, in_, r, w):
            if eng == "act":
                P.op("act", "activation", r=r, w=w, out=out, in_=in_, func=AF.Copy)
            else:
                P.op(eng, "tensor_copy", r=r, w=w, out=out, in_=in_)

        def nextbank(lo, hi):
            b = lo + rr["bk"] % (hi - lo)
            rr["bk"] += 1
            return b

        def walloc(stack, nb):
            W["wst"] = sb("wst%d" % P.nins, [128, 2, 512], F32, stack)
            W["wbf"] = sb("wbf%d" % P.nins, [128, nb, 8 * 512], BF16, stack)
            W["nb"] = nb
            rr["wbf"] = 0

        def load_w(src, nk=8, ncols=512):
            b = rr["wbf"]
            rr["wbf"] = (b + 1) % W["nb"]
            key = ("wbf", b)
            view = W["wbf"][:, b, 0:nk * ncols].rearrange("p (k c) -> p k c", k=nk)
            for k in range(nk):
                for c0 in range(0, ncols, 512):
                    n = min(512, ncols - c0)
                    s = rr["wst"]
                    rr["wst"] ^= 1
                    P.dma(W["wst"][:, s, 0:n], src[k * 128:(k + 1) * 128, c0:c0 + n], w=[("wst", s)])
                    P.op("pool", "tensor_copy", r=[("wst", s)], w=[key], out=view[:, k, c0:c0 + n], in_=W["wst"][:, s, 0:n])
            return view, key

        def rmsnorm_stats(g):
            cs = gcolsl(g)
            n = cs.stop - cs.start
            bk = 6 + (g % 2)
            for c in range(8):
                s = rr["sq"]
                rr["sq"] = (s + 1) % 4
                P.op("act", "activation", r=[("xT", g)], w=[("sq", s)], out=sq_t[:, s, 0:n], in_=xT[:, c, cs], func=AF.Square)
                P.op("pe", "matmul", r=[("sq", s), "ones_b"], w=[bkey(bk)], out=banks[bk][:, 0:n],
                     lhsT=ones_b[:], rhs=sq_t[:, s, 0:n], start=(c == 0), stop=(c == 7))
            ri = g % 2
            P.op("act", "activation", r=[bkey(bk), "eps"], w=[("rs", ri)], out=rs_t[:, ri, 0:n], in_=banks[bk][:, 0:n],
                 func=AF.Sqrt, scale=1.0 / D, bias=eps_t[:, 0:1])
            P.op("dve", "reciprocal", r=[("rs", ri)], w=[("rs", ri)], out=rs_t[:, ri, 0:n], in_=rs_t[:, ri, 0:n])
            return ri, n, cs

        def rmsnorm(nidx):
            for g in range(5):
                ri, n, cs = rmsnorm_stats(g)
                for c in range(8):
                    P.op("dve", "scalar_tensor_tensor", r=[("xT", g), ("rs", ri), "gcol"], w=hkeys(g),
                         out=hT[:, c, cs], in0=xT[:, c, cs], scalar=gcol_t[:, nidx * 8 + c:nidx * 8 + c + 1],
                         in1=rs_t[:, ri, 0:n], op0=ALU.mult, op1=ALU.mult)

        def transpose_to_hT(src_tile, src_key, rows, c0, t, eng=None):
            bk = nextbank(4, 6)
            for j in range(4):
                P.op("pe", "transpose", r=[src_key, "ident_b"], w=[bkey(bk)], out=banksb[bk][:, j * 128:j * 128 + rows],
                     in_=src_tile[:, j * 128:(j + 1) * 128], identity=ident_b[0:rows, 0:rows])
            copy(eng or evac_eng(), hT[:, c0:c0 + 4, tcols(t)],
                 banksb[bk][:, 0:512].rearrange("p (j q) -> p j q", j=4)[:, :, 0:rows], r=[bkey(bk)], w=[tkey(t)])

        def out_proj(wsrc):
            for dcg in range(2):
                wv, wk = load_w(wsrc[:, dcg * 512:(dcg + 1) * 512])
                for j in range(4):
                    dc = dcg * 4 + j
                    for g in range(5):
                        cs = gcolsl(g)
                        n = cs.stop - cs.start
                        bk = nextbank(0, 4)
                        for kc in range(8):
                            P.op("pe", "matmul", r=hkeys(g) + [wk], w=[bkey(bk)], out=banks[bk][:, 0:n],
                                 lhsT=wv[:, kc, j * 128:(j + 1) * 128], rhs=hT[:, kc, cs], start=(kc == 0), stop=(kc == 7), inc=(kc == 7))
                        P.op("dve", "tensor_tensor", r=[bkey(bk), ("xT", g)], w=[("xT", g)], out=xT[:, dc, cs],
                             in0=xT[:, dc, cs], in1=banks[bk][:, 0:n], op=ALU.add)

        with ExitStack() as ph:
            xin = sb("xin", [128, 2, D], stack=ph)
            for t in range(17):
                rows = trows(t)
                s = t % 2
                src = x_p[t * 128:(t + 1) * 128, :] if t < 16 else x_s[:, :]
                P.dma(xin[0:rows, s, :], src, w=[("xin", s)])
                for half in range(2):
                    bk = (2 * t + half) % 4
                    for j in range(4):
                        c = half * 4 + j
                        P.op("pe", "transpose", r=[("xin", s), "ident_f"], w=[bkey(bk)],
                             out=banks[bk][:, j * 128:j * 128 + rows], in_=xin[0:rows, s, c * 128:(c + 1) * 128],
                             identity=ident_f[0:rows, 0:rows])
                    copy(evac_eng(), xT[:, half * 4:half * 4 + 4, tcols(t)],
                         banks[bk][:].rearrange("p (j q) -> p j q", j=4)[:, :, 0:rows], r=[bkey(bk)], w=[("xT", min(t // 4, 4))])
            P.barrier()

        def qkv_proj(ph, wsrc, layer, kvp, kvs, QT, KT, Vt, qs_tok, ks_tok, vs_tok):
            stg = sb("stg%d" % layer, [128, 2, 512], F32, ph)
            bfs = sb("bfs%d" % layer, [128, 2, 512], BF16, ph)
            rt = sb("rt%d" % layer, [128, 4, 64], F32, ph)
            for cg in range(3):
                wv, wk = load_w(wsrc[:, cg * 512:(cg + 1) * 512])
                for t in range(17):
                    rows = trows(t)
                    bk = nextbank(0, 4)
                    s = t % 2
                    for kc in range(8):
                        P.op("pe", "matmul", r=[tkey(t), wk], w=[bkey(bk)], out=banks[bk][0:rows, :],
                             lhsT=hT[:, kc, tcols(t)], rhs=wv[:, kc, :], start=(kc == 0), stop=(kc == 7), inc=(kc == 7))
                    sk = ("stg", s)
                    P.op("act", "activation", r=[bkey(bk)], w=[sk], out=stg[0:rows, s, :], in_=banks[bk][0:rows, :], func=AF.Copy)
                    if layer == 0 and cg < 2:
                        X = stg[0:rows, s, :].rearrange("p (h d) -> p h d", h=8)
                        x1 = X[:, :, 0:8]
                        x2 = X[:, :, 8:16]
                        ca_ = ropec_t[0:rows, t, :]
                        sa_ = ropes_t[0:rows, t, :]
                        cb = bass.AP(ropec_t, ca_.offset, [list(ca_.ap[0]), [0, 8], [1, 8]])
                        sbb = bass.AP(ropes_t, sa_.offset, [list(sa_.ap[0]), [0, 8], [1, 8]])
                        tv = [rt[0:rows, i, :].rearrange("p (h d) -> p h d", h=8) for i in range(4)]
                        P.op("dve", "tensor_tensor", r=[sk, "rope"], w=["rt0"], out=tv[0], in0=x1, in1=cb, op=ALU.mult)
                        P.op("dve", "tensor_tensor", r=[sk, "rope"], w=["rt1"], out=tv[1], in0=x2, in1=sbb, op=ALU.mult)
                        P.op("dve", "tensor_tensor", r=[sk, "rope"], w=["rt2"], out=tv[2], in0=x2, in1=cb, op=ALU.mult)
                        P.op("dve", "tensor_tensor", r=[sk, "rope"], w=["rt3"], out=tv[3], in0=x1, in1=sbb, op=ALU.mult)
                        P.op("dve", "tensor_tensor", r=["rt0", "rt1"], w=[sk], out=x1, in0=tv[0], in1=tv[1], op=ALU.subtract)
                        P.op("dve", "tensor_tensor", r=["rt2", "rt3"], w=[sk], out=x2, in0=tv[2], in1=tv[3], op=ALU.add)
                    if cg >= 1:
                        dst = (kvp[t * 128:(t + 1) * 128, (cg - 1) * 512:cg * 512] if t < 16 else kvs[:, (cg - 1) * 512:cg * 512])
                        P.dma(dst, stg[0:rows, s, :], r=[sk])
                    if t == 16:
                        tok = (qs_tok, ks_tok, vs_tok)[cg]
                        P.op("dve", "tensor_copy", r=[sk], w=[tok.name], out=tok[:], in_=stg[0:rows, s, :])
                        continue
                    if cg == 2:
                        P.op("pool", "tensor_copy", r=[sk], w=[("Vt", t)], out=Vt[:, t, :], in_=stg[:, s, :])
                    else:
                        bkk = ("bfs", s)
                        P.op("pool", "tensor_copy", r=[sk], w=[bkk], out=bfs[:, s, :], in_=stg[:, s, :])
                        dstT = QT if cg == 0 else KT
                        bk2 = nextbank(4, 6)
                        for j in range(4):
                            P.op("pe", "transpose", r=[bkk, "ident_b"], w=[bkey(bk2)], out=banksb[bk2][:, j * 128:(j + 1) * 128],
                                 in_=bfs[:, s, j * 128:(j + 1) * 128], identity=ident_b[:])
                        copy("dve", dstT[:, :, tcols(t)], banksb[bk2][:, 0:512].rearrange("p (j q) -> p j q", j=4),
                             r=[bkey(bk2)], w=[("QT" if cg == 0 else "KT", t)])

        def attention(ph, layer, QT, KT, Vt):
            pexp = sb("pexp%d" % layer, [128, 2, T], BF16, ph)
            PTs = sb("PTs%d" % layer, [128, 2, 512], BF16, ph)
            atok = sb("atok%d" % layer, [128, 2, 512], BF16, ph)
            rsum = sb("rsum%d" % layer, [128, 2, 16], F32, ph)
            rtot = sb("rtot%d" % layer, [128, 2, 2], F32, ph)
            dtmp = sb("dtmp%d" % layer, [128, 2, 128], BF16, ph)
            if layer == 0:
                kms = sb("kms", [128, 4, 8], F32, ph)
                kmT = sb("kmT", [128, 4, 8], BF16, ph)
                gsb = sb("gsb", [128, 8, 8], F32, ph)
                m8 = sb("m8", [128, 8, 8], F32, ph)
                bias_t = sb("bias_t", [128, 2, 64], F32, ph)
                for c in range(4):
                    P.op("dve", "tensor_reduce", r=[("KT", t) for t in range(16)], w=["kms"], out=kms[:, c, :],
                         in_=KT[:, c, :].rearrange("p (n s) -> p n s", s=256), axis=AX.X, op=ALU.add)
                P.op("act", "activation", r=["kms"], w=["kmT"], out=kmT[:], in_=kms[:], func=AF.Copy, scale=1.0 / 256)
                KMb = sb("KMb", [128, 4, 64], BF16, ph)
                P.op("pool", "memset", w=["KMb"], ap=KMb[:], constant=0.0)
                for c in range(4):
                    P.op("dve", "tensor_copy", r=["kmT"], w=["KMb"], out=KMb[0:64, c, (2 * c) * 8:(2 * c) * 8 + 8], in_=kmT[0:64, c, :])
                    P.op("dve", "tensor_copy", r=["kmT"], w=["KMb"], out=KMb[64:128, c, (2 * c + 1) * 8:(2 * c + 1) * 8 + 8], in_=kmT[64:128, c, :])
            else:
                sprow = sb("sprow", [128, 2, 1 + T], F32, ph)
                csx = sb("csx", [128, 2, T], F32, ph)
                etmp_flat = sq_t.bitcast(F32)[:].rearrange("p a b -> p (a b)")
                negT = sb("negT", [128, 2], F32, ph)
                carry = sb("carry", [128, 2, 8], F32, ph)
                P.op("pool", "memset", w=[("sprow", 0), ("sprow", 1)], ap=sprow[:, :, 0:1], constant=0.0)
            PI = {"v": 0}
            for i in range(16):
                G = i // 2
                qk = [("QT", i)]
                W_ = 128 * (i + 1)
                if layer == 0:
                    bsl = i % 2
                    bkg = 6
                    for c in range(4):
                        P.op("pe", "matmul", r=qk + ["KMb"], w=[bkey(bkg)], out=banks[bkg][:, 0:64],
                             lhsT=QT[:, c, tcols(i)], rhs=KMb[:, c, :], start=(c == 0), stop=(c == 3))
                    if G >= 4:
                        P.op("act", "activation", r=[bkey(bkg)], w=["gsb"], out=gsb[:].rearrange("p h n -> p (h n)"),
                             in_=banks[bkg][:, 0:64], func=AF.Copy)
                        if G < 8:
                            P.op("pool", "memset", r=[], w=["gsb"], ap=gsb[:, :, G:8], constant=-1e30)
                        for h in range(8):
                            P.op("dve", "max", r=["gsb"], w=["m8"], out=m8[:, h, :], in_=gsb[:, h, :])
                        for h in range(8):
                            P.op("dve", "tensor_scalar", r=["gsb", "m8"], w=[("bias", bsl)], out=bias_t[:, bsl, h * 8:(h + 1) * 8],
                                 in0=gsb[:, h, :], scalar1=m8[:, h, 2:3], scalar2=NEG, op0=ALU.is_lt, op1=ALU.mult)
                    else:
                        P.op("pool", "memset", w=[("bias", bsl)], ap=bias_t[:, bsl, :], constant=0.0)
                HS = {}

                def headA(h):
                    c, po = h // 2, (h % 2) * 64
                    ps_ = PI["v"] % 2
                    PI["v"] += 1
                    pk = ("pexp", ps_)
                    kall = [("KT", t) for t in range(i + 1)]
                    nsl = 0
                    if layer == 0:
                        for kc in range(i + 1):
                            bk = nextbank(0, 4)
                            P.op("pe", "matmul", r=qk + kall, w=[bkey(bk)], out=banks[bk][:, 0:128],
                                 lhsT=QT[po:po + 64, c, tcols(i)], rhs=KT[po:po + 64, c, kc * 128:(kc + 1) * 128], start=True, stop=True)
                            if kc == i:
                                ds_ = ps_
                                P.op("act", "activation", r=[bkey(bk)], w=[("dtmp", ds_)], out=dtmp[:, ds_, :],
                                     in_=banks[bk][:, 0:128], func=AF.Exp, scale=0.125)
                                P.op("dve", "tensor_tensor", r=[("dtmp", ds_), "tri_b"], w=[pk], out=pexp[:, ps_, W_ - 128:W_],
                                     in0=dtmp[:, ds_, :], in1=tri_b[:], op=ALU.mult)
                            elif kc // 2 < G:
                                nb_ = kc // 2
                                P.op("act", "activation", r=[bkey(bk), ("bias", bsl)], w=[pk],
                                     out=pexp[:, ps_, kc * 128:(kc + 1) * 128], in_=banks[bk][:, 0:128], func=AF.Exp,
                                     scale=0.125, bias=bias_t[:, bsl, h * 8 + nb_:h * 8 + nb_ + 1])
                            else:
                                P.op("act", "activation", r=[bkey(bk)], w=[pk], out=pexp[:, ps_, kc * 128:(kc + 1) * 128],
                                     in_=banks[bk][:, 0:128], func=AF.Exp, scale=0.125)
                        P.op("dve", "reduce_sum", r=[pk], w=[("rtot", ps_)], out=rtot[:, ps_, 0:1],
                             in_=pexp[:, ps_, 0:W_], axis=AX.X)
                        P.op("dve", "reciprocal", r=[("rtot", ps_)], w=[("rtot", ps_)], out=rtot[:, ps_, 1:2], in_=rtot[:, ps_, 0:1])
                    else:
                        for s0 in range(0, W_, 512):
                            n = min(512, W_ - s0)
                            bk = nextbank(0, 4)
                            es_ = (s0 // 512) % 2
                            P.op("pe", "matmul", r=qk + kall, w=[bkey(bk)], out=banks[bk][:, 0:n],
                                 lhsT=QT[po:po + 64, c, tcols(i)], rhs=KT[po:po + 64, c, s0:s0 + n], start=True, stop=True)
                            P.op("act", "activation", r=[bkey(bk)], w=[("etmp", es_)], out=etmp_flat[:, es_ * 512:es_ * 512 + n], in_=banks[bk][:, 0:n],
                                 func=AF.Exp, scale=0.125)
                            P.op("act", "activation", r=[("etmp", es_)], w=[("sprow", ps_)], out=sprow[:, ps_, 1 + s0:1 + s0 + n],
                                 in_=etmp_flat[:, es_ * 512:es_ * 512 + n], func=AF.Ln, bias=1.0)
                            if s0 + n == W_:
                                P.op("pool", "tensor_tensor", r=[("sprow", ps_), "tris_f"], w=[("sprow", ps_)], out=sprow[:, ps_, 1 + W_ - 128:1 + W_],
                                     in0=sprow[:, ps_, 1 + W_ - 128:1 + W_], in1=tris_f[:], op=ALU.mult)
                            si_ = s0 // 512
                            init = 0.0 if s0 == 0 else carry[:, ps_, si_ - 1:si_]
                            P.op("dve", "tensor_tensor_scan", r=[("sprow", ps_), ("csx", ps_), ("carry", ps_)], w=[("csx", ps_)], out=csx[:, ps_, s0:s0 + n],
                                 data0=sprow[:, ps_, s0:s0 + n], data1=sprow[:, ps_, s0:s0 + n], initial=init, op0=ALU.add, op1=ALU.bypass)
                            P.op("dve", "tensor_copy", r=[("csx", ps_)], w=[("carry", ps_)], out=carry[:, ps_, si_:si_ + 1], in_=csx[:, ps_, s0 + n - 1:s0 + n])
                            if s0 + n == W_:
                                P.op("dve", "tensor_scalar", r=[("csx", ps_)], w=[("negT", ps_)], out=negT[:, ps_:ps_ + 1], in0=csx[:, ps_, W_ - 1:W_],
                                     scalar1=-1.0, scalar2=None, op0=ALU.mult)
                            P.op("dve", "scalar_tensor_tensor", r=[bkey(bk), ("csx", ps_)], w=[("csx", ps_)], out=csx[:, ps_, s0:s0 + n],
                                 in0=banks[bk][:, 0:n], scalar=0.125, in1=csx[:, ps_, s0:s0 + n], op0=ALU.mult, op1=ALU.add)
                        for s0 in range(0, W_, 512):
                            n = min(512, W_ - s0)
                            P.op("act", "activation", r=[("csx", ps_), ("negT", ps_)], w=[pk], out=pexp[:, ps_, s0:s0 + n], in_=csx[:, ps_, s0:s0 + n],
                                 func=AF.Exp, bias=negT[:, ps_:ps_ + 1])
                        P.op("pool", "tensor_tensor", r=[pk, "tris_b"], w=[pk], out=pexp[:, ps_, W_ - 128:W_],
                             in0=pexp[:, ps_, W_ - 128:W_], in1=tris_b[:], op=ALU.mult)
                    HS[h] = (ps_, pk)

                def headB(h):
                    c, po = h // 2, (h % 2) * 64
                    ps_, pk = HS[h]
                    bo = 6 + ps_ if layer == 1 else 7
                    for k0 in range(0, i + 1, 4):
                        nk_ = min(4, i + 1 - k0)
                        bk = nextbank(4, 6)
                        pts = (k0 // 4) % 2
                        for j in range(nk_):
                            P.op("pe", "transpose", r=[pk, "ident_b"], w=[bkey(bk)], out=banksb[bk][:, j * 128:(j + 1) * 128],
                                 in_=pexp[:, ps_, (k0 + j) * 128:(k0 + j + 1) * 128], identity=ident_b[:])
                        copy(evac_eng(), PTs[:, pts, 0:nk_ * 128], banksb[bk][:, 0:nk_ * 128], r=[bkey(bk)], w=[("PTs", pts)])
                        for j in range(nk_):
                            kc = k0 + j
                            P.op("pe", "matmul", r=[("PTs", pts), ("Vt", kc)], w=[bkey(bo)], out=banks[bo][:, h * 64:(h + 1) * 64],
                                 lhsT=PTs[:, pts, j * 128:(j + 1) * 128], rhs=Vt[:, kc, h * 64:(h + 1) * 64],
                                 start=(kc == 0), stop=(kc == i))
                    asl = i % 2
                    if layer == 0:
                        P.op("act", "activation", r=[bkey(bo), ("rtot", ps_)], w=[("atok", asl)], out=atok[:, asl, h * 64:(h + 1) * 64],
                             in_=banks[bo][:, h * 64:(h + 1) * 64], func=AF.Copy, scale=rtot[:, ps_, 1:2])
                    else:
                        P.op("act", "activation", r=[bkey(bo)], w=[("atok", asl)], out=atok[:, asl, h * 64:(h + 1) * 64],
                             in_=banks[bo][:, h * 64:(h + 1) * 64], func=AF.Copy)

                headA(0)
                for h in range(8):
                    if h + 1 < 8:
                        headA(h + 1)
                    headB(h)
                transpose_to_hT(atok[:, i % 2, :], ("atok", i % 2), 128, 0, i)

        def sample_attention(ph, layer, cache, qtok, kvs_dram, qscr):
            L = "s%d" % layer
            NV = 26
            NK = 6
            vbuf = sb("vbuf" + L, [128, NV, 512], F32, ph)
            kbuf = sb("kbuf" + L, [128, NK, 512], F32, ph)
            ids_i = sb("ids_i" + L, [128, 16], I32, ph)
            idf = sb("idf" + L, [128, 16], F32, ph)
            idx = sb("idx" + L, [128, 2, 2, 16], I32, ph)
            idf2 = sb("idf2" + L, [128, 2, 16], F32, ph)
            pcol = sb("pcol" + L, [128, 1], F32, ph)
            pci = sb("pci" + L, [128, 1], I32, ph)
            rows = cache.rearrange("n t (two c) -> (n t two) c", two=2)
            P.op("pool", "iota", w=["pci"], out=pci[:], pattern=[[0, 1]], base=0, channel_multiplier=1)
            P.op("pool", "tensor_copy", r=["pci"], w=["pcol"], out=pcol[:], in_=pci[:])
            qb = sb("qb" + L, [128, 2, 512], F32, ph)
            prod = sb("prod" + L, [128, 2, 512], F32, ph)
            S_all = sb("S_all" + L, [128, 2, 17, 8], F32, ph)
            Gs = sb("Gs" + L, [128, 16, 8], F32, ph)
            gate = sb("gate" + L, [128, 8, 8], F32, ph)
            m8s = sb("m8s" + L, [128, 8, 8], F32, ph)
            biasb = sb("biasb" + L, [128, 8, 8], F32, ph)
            arg = sb("arg" + L, [128, 17, 8], F32, ph)
            carry_ = sb("carry_" + L, [128, 16, 8], F32, ph)
            Zp = sb("Zp" + L, [128, 1, 17, 128], F32, ph)
            Opad = sb("Opad" + L, [128, 512], F32, ph)
            kself = sb("kself" + L, [128, 512], F32, ph)
            vself = sb("vself" + L, [128, 512], F32, ph)
            rden = sb("rden" + L, [128, 2], F32, ph)
            ones_f = sb("ones_f" + L, [128, 128], F32, ph)
            tri_f = sb("tri_f" + L, [128, 128], F32, ph)
            P.op("pool", "memset", w=["Zp"], ap=Zp[:], constant=0.0)
            P.op("pool", "memset", w=["Opad"], ap=Opad[:], constant=0.0)
            P.op("pool", "memset", w=["kself"], ap=kself[:], constant=0.0)
            P.op("pool", "memset", w=["vself"], ap=vself[:], constant=0.0)
            P.op("pool", "memset", w=["ones_f"], ap=ones_f[:], constant=1.0)
            P.op("pool", "memset", w=["tri_f"], ap=tri_f[:], constant=1.0)
            P.op("pool", "affine_select", r=["tri_f"], w=["tri_f"], out=tri_f[:], in_=tri_f[:], pattern=[[-1, 128]],
                 compare_op=ALU.is_ge, fill=0.0, base=0, channel_multiplier=1)
            P.op("pool", "memset", w=["carry_"], ap=carry_[:], constant=0.0)
            P.dma(qscr[:, :], qtok[:], r=[qtok.name], w=["qscr"])
            npg = 17 if layer == 0 else 16
            cnt = {"k": 0, "v": 0, "r": 0}
            VS = {}
            def scores(s):
                z = s % 2
                P.dma(qb[:, z, :], bass.AP(qscr.tensor, s * 512, [[0, 128], [1, 512]]), r=["qscr"], w=[("qb", z)])
                P.dma(ids_i[:], bass.AP(ptab.tensor, s * 16, [[0, 128], [1, 16]]), w=["ids_i"])
                if layer == 0:
                    P.dma(kself[0:1, :], kvs_dram[s:s + 1, 0:512], w=["kself"])
                    P.dma(vself[0:1, :], kvs_dram[s:s + 1, 512:1024], w=["vself"])
                P.op("pool", "tensor_copy", r=["ids_i"], w=["idf"], out=idf[:], in_=ids_i[:])
                P.op("pool", "tensor_scalar", r=["idf", "pcol"], w=["idf"], out=idf[:], in0=idf[:], scalar1=128.0, scalar2=pcol[:, 0:1],
                     op0=ALU.mult, op1=ALU.add)
                P.op("pool", "tensor_scalar", r=["idf"], w=["idf2"], out=idf2[:, 0, :], in0=idf[:], scalar1=2.0, scalar2=None, op0=ALU.mult)
                P.op("pool", "tensor_scalar", r=["idf"], w=["idf2"], out=idf2[:, 1, :], in0=idf[:], scalar1=2.0, scalar2=1.0, op0=ALU.mult, op1=ALU.add)
                P.op("pool", "tensor_copy", r=["idf2"], w=[("idx", z)], out=idx[:, z, :, :], in_=idf2[:])
                vsl = []
                VS[s] = vsl
                for j in range(npg):
                    if j < 16:
                        vs_ = cnt["v"] % NV
                        cnt["v"] += 1
                        ks_ = cnt["k"] % NK
                        cnt["k"] += 1
                        P.idma(kbuf[:, ks_, :], rows[:, :], idx[:, z, 0, j:j + 1], r=[("idx", z)], w=[("kb", ks_)])
                        P.idma(vbuf[:, vs_, :], rows[:, :], idx[:, z, 1, j:j + 1], r=[("idx", z)], w=[("vb", vs_)])
                        ksrc, kkey = kbuf[:, ks_, :], ("kb", ks_)
                        vsl.append((vbuf[:, vs_, :], ("vb", vs_)))
                    else:
                        ksrc, kkey = kself[:], "kself"
                        vsl.append((vself[:], "vself"))
                    pr = j % 2
                    P.op("dve", "tensor_tensor", r=[kkey, ("qb", z)], w=[("prod", pr)], out=prod[:, pr, :], in0=ksrc, in1=qb[:, z, :], op=ALU.mult)
                    P.op("dve", "tensor_reduce", r=[("prod", pr)], w=[("S_all", z)], out=S_all[:, z, j, :],
                         in_=prod[:, pr, :].rearrange("p (h d) -> p h d", h=8), axis=AX.X, op=ALU.add)

            def mid(s):
                z = s % 2
                sk = ("S_all", z)
                zk = ("Zp", 0)
                bn_, bd_ = 2 + (s % 2), 4
                vsl = VS[s]
                Sf = S_all[:, z, 0:16, :].rearrange("p a h -> p (a h)")
                sk = ("S_all", z)
                zk = ("Zp", 0)
                if layer == 0:
                    P.op("pe", "matmul", r=[sk, "ones_f"], w=[bkey(0)], out=banks[0][:, 0:128], lhsT=ones_f[:], rhs=Sf, start=True, stop=True)
                    P.op("act", "activation", r=[bkey(0)], w=["Gs"], out=Gs[:].rearrange("p a h -> p (a h)"), in_=banks[0][:, 0:128], func=AF.Copy)
                    Gv = Gs[:].rearrange("p (n i) h -> p n i h", i=2)
                    P.op("dve", "tensor_tensor", r=["Gs"], w=["gate"], out=gate[:], in0=Gv[:, :, 0, :], in1=Gv[:, :, 1, :], op=ALU.add)
                    for h in range(8):
                        P.op("dve", "max", r=["gate"], w=["m8s"], out=m8s[:, h, :], in_=gate[:, :, h])
                    for h in range(8):
                        P.op("dve", "tensor_scalar", r=["gate", "m8s"], w=["biasb"], out=biasb[:, :, h], in0=gate[:, :, h],
                             scalar1=m8s[:, h, 2:3], scalar2=NEG, op0=ALU.is_lt, op1=ALU.mult)
                    Sv = S_all[:, z, 0:16, :].rearrange("p (n i) h -> p n i h", i=2)
                    Av = arg[:, 0:16, :].rearrange("p (n i) h -> p n i h", i=2)
                    for i_ in range(2):
                        P.op("dve", "scalar_tensor_tensor", r=[sk, "biasb"], w=["arg"], out=Av[:, :, i_, :], in0=Sv[:, :, i_, :], scalar=0.125,
                             in1=biasb[:], op0=ALU.mult, op1=ALU.add)
                    P.op("dve", "tensor_scalar", r=[sk], w=["arg"], out=arg[:, 16, :], in0=S_all[:, z, 16, :], scalar1=0.125, scalar2=None, op0=ALU.mult)
                    P.op("act", "activation", r=["arg"], w=[zk], out=Zp[:, 0, :, 0:8], in_=arg[:], func=AF.Exp)
                    P.op("dve", "tensor_scalar", r=[zk, "ident_f"], w=[zk], out=Zp[:, 0, 16, 0:8], in0=Zp[:, 0, 16, 0:8], scalar1=ident_f[:, 0:1],
                         scalar2=None, op0=ALU.mult)
                else:
                    af = arg[:, 0:16, :].rearrange("p a h -> p (a h)")
                    gf = Gs[:].rearrange("p a h -> p (a h)")
                    P.op("act", "activation", r=[sk], w=["arg"], out=af, in_=Sf, func=AF.Exp, scale=0.125)
                    P.op("act", "activation", r=["arg"], w=["Gs"], out=gf, in_=af, func=AF.Ln, bias=1.0)
                    P.op("pe", "matmul", r=["Gs", "tri_f"], w=[bkey(0)], out=banks[0][:, 0:128], lhsT=tri_f[:], rhs=gf, start=True, stop=True)
                    P.op("pe", "matmul", r=["Gs", "ones_f"], w=[bkey(1)], out=banks[1][:, 0:128], lhsT=ones_f[:], rhs=gf, start=True, stop=True)
                    P.op("act", "activation", r=[bkey(1)], w=[("prod", 0)], out=prod[:, 0, 0:128], in_=banks[1][:, 0:128], func=AF.Copy)
                    Tv = prod[:, 0, 0:128].rearrange("p (a h) -> p a h", h=8)
                    for pgi in range(14, -1, -1):
                        P.op("dve", "tensor_tensor", r=[("prod", 0), "carry_"], w=["carry_"], out=carry_[:, pgi, :], in0=carry_[:, pgi + 1, :],
                             in1=Tv[:, pgi + 1, :], op=ALU.add)
                    P.op("dve", "scalar_tensor_tensor", r=[sk, bkey(0)], w=["arg"], out=af, in0=Sf, scalar=0.125, in1=banks[0][:, 0:128],
                         op0=ALU.mult, op1=ALU.subtract)
                    P.op("dve", "tensor_tensor", r=["arg", "carry_"], w=["arg"], out=af, in0=af, in1=carry_[:].rearrange("p a h -> p (a h)"), op=ALU.subtract)
                    P.op("act", "activation", r=["arg"], w=[zk], out=Zp[:, 0, 0:16, 0:8], in_=arg[:, 0:16, :], func=AF.Exp)
                bn_, bd_ = 2 + (s % 2), 4
                for j in range(npg):
                    vap, vk = vsl[j]
                    P.op("pe", "matmul", r=[zk, vk], w=[bkey(bn_)], out=banks[bn_][:, :], lhsT=Zp[:, 0, j, :], rhs=vap,
                         start=(j == 0), stop=(j == npg - 1))
                if layer == 0:
                    for j in range(npg):
                        P.op("pe", "matmul", r=[zk, "ones_f"], w=[bkey(bd_)], out=banks[bd_][:, 0:64], lhsT=Zp[:, 0, j, :], rhs=ones_f[:, 0:64],
                             start=(j == 0), stop=(j == npg - 1))

            def finish(s):
                z = s % 2
                sk = ("S_all", z)
                zk = ("Zp", 0)
                bn_, bd_ = 2 + (s % 2), 4
                vsl = VS[s]
                if layer == 0:
                    P.op("dve", "reciprocal", r=[bkey(bd_)], w=["rden"], out=rden[0:8, 0:1], in_=banks[bd_][0:8, 0:1])
                    P.op("dve", "tensor_scalar", r=[bkey(bn_), "rden"], w=["Opad"], out=Opad[0:8, :], in0=banks[bn_][0:8, :], scalar1=rden[0:8, 0:1],
                         scalar2=None, op0=ALU.mult)
                else:
                    P.op("act", "activation", r=[bkey(bn_)], w=["Opad"], out=Opad[0:8, :], in_=banks[bn_][0:8, :], func=AF.Copy)
                bt_ = 5
                for c in range(4):
                    P.op("pe", "transpose", r=["Opad", "ident_f"], w=[bkey(bt_)], out=banks[bt_][:, c * 128:(c + 1) * 128],
                         in_=Opad[:, c * 128:(c + 1) * 128], identity=ident_f[:])
                for c in range(4):
                    P.op("dve", "tensor_copy", r=[bkey(bt_)], w=[tkey(16)], out=hT[0:64, c, T + s:T + s + 1],
                         in_=banks[bt_][0:64, c * 128 + 2 * c:c * 128 + 2 * c + 1])
                    P.op("act", "activation", r=[bkey(bt_)], w=[tkey(16)], out=hT[64:128, c, T + s:T + s + 1],
                         in_=banks[bt_][64:128, c * 128 + 2 * c + 1:c * 128 + 2 * c + 2], func=AF.Copy)


            scores(0)
            for s in range(NS):
                mid(s)
                if s + 1 < NS:
                    scores(s + 1)
                finish(s)

        def ffn(l):
            with ExitStack() as ph:
                walloc(ph, 3)
                aT = sb("aT%d" % l, [128, 4, TT], BF16, ph)
                gbuf = sb("gbuf%d" % l, [128, 2 + T], F32, ph)
                gss = sb("gss%d" % l, [128, 3, NS], F32, ph)
                ctmp = sb("ctmp%d" % l, [128, 2, 512], F32, ph)
                gel = sb("gel%d" % l, [128, 2, 512], BF16, ph)
                cst = sb("cst%d" % l, [NS, 2, 512], F32, ph)
                cvs = sb("cvs%d" % l, [NS, 2, 512], F32, ph)
                cvp = sb("cvp%d" % l, [2, 512], F32, ph)
                P.op("pool", "memset", w=["gbuf"], ap=gbuf[:, 0:2], constant=0.0)
                cst_v = conv_st[l].rearrange("s (r f) -> s r f", r=2)
                cso_v = conv_s[l].rearrange("s (r f) -> s r f", r=2)
                ei = 0
                for fg in range(6):
                    f0 = fg * 512
                    ncols = min(512, DFF - f0)
                    nf = ncols // 128
                    wg, wgk = load_w(w_gate[l][:, f0:f0 + ncols], ncols=ncols)
                    wu, wuk = load_w(w_up[l][:, f0:f0 + ncols], ncols=ncols)
                    P.dma(cst[:, :, 0:ncols], cst_v[:, :, f0:f0 + ncols], w=["cst"])
                    P.op("pool", "tensor_copy", r=["cst"], w=["cvs"], out=cvs[:, 0, 0:ncols], in_=cst[:, 1, 0:ncols])
                    for j in range(nf):
                        fc = fg * 4 + j
                        cw = [convc_t[:, l, i_, fc:fc + 1] for i_ in range(4)]
                        bks = nextbank(4, 6)
                        for r_ in range(2):
                            P.op("pe", "transpose", r=["cst", "ident_f"], w=[bkey(bks)], out=banks[bks][:, r_ * NS:(r_ + 1) * NS],
                                 in_=cst[:, r_, j * 128:(j + 1) * 128], identity=ident_f[0:NS, 0:NS])
                        P.op("dve", "tensor_copy", r=[bkey(bks)], w=["gss"], out=gss[:, 0:2, :].rearrange("p r s -> p (r s)"),
                             in_=banks[bks][:, 0:2 * NS])
                        for g in range(5):
                            cs = gcolsl(g)
                            n = cs.stop - cs.start
                            ba = nextbank(0, 4)
                            for kc in range(8):
                                P.op("pe", "matmul", r=hkeys(g) + [wgk], w=[bkey(ba)], out=banks[ba][:, 0:n],
                                     lhsT=wg[:, kc, j * 128:(j + 1) * 128], rhs=hT[:, kc, cs], start=(kc == 0), stop=(kc == 7), inc=(kc == 7))
                            bb = nextbank(0, 4)
                            for kc in range(8):
                                P.op("pe", "matmul", r=hkeys(g) + [wuk], w=[bkey(bb)], out=banks[bb][:, 0:n],
                                     lhsT=wu[:, kc, j * 128:(j + 1) * 128], rhs=hT[:, kc, cs], start=(kc == 0), stop=(kc == 7), inc=(kc == 7))
                            e_ = ei % 2
                            ei += 1
                            ck, gk = ("ctmp", e_), ("gel", e_)
                            if g < 4:
                                o = 2 + 512 * g
                                P.op("act", "activation", r=[bkey(ba)], w=["gbuf"], out=gbuf[:, o:o + 512], in_=banks[ba][:, 0:512], func=AF.Copy)
                                srcs = [gbuf[:, o - 2:o + 510], gbuf[:, o - 1:o + 511], gbuf[:, o:o + 512]]
                                sk_ = "gbuf"
                            else:
                                P.op("act", "activation", r=[bkey(ba)], w=["gss"], out=gss[:, 2, :], in_=banks[ba][:, 0:NS], func=AF.Copy)
                                srcs = [gss[:, 0, :], gss[:, 1, :], gss[:, 2, :]]
                                sk_ = "gss"
                            ct = ctmp[:, e_, 0:n]
                            P.op("dve", "tensor_scalar", r=[sk_, "convc"], w=[ck], out=ct, in0=srcs[2], scalar1=cw[2], scalar2=cw[3],
                                 op0=ALU.mult, op1=ALU.add)
                            P.op("dve", "scalar_tensor_tensor", r=[sk_, "convc", ck], w=[ck], out=ct, in0=srcs[1], scalar=cw[1], in1=ct,
                                 op0=ALU.mult, op1=ALU.add)
                            P.op("dve", "scalar_tensor_tensor", r=[sk_, "convc", ck], w=[ck], out=ct, in0=srcs[0], scalar=cw[0], in1=ct,
                                 op0=ALU.mult, op1=ALU.add)
                            P.op("act", "activation", r=[ck], w=[gk], out=gel[:, e_, 0:n], in_=ct, func=AF.Gelu_apprx_tanh)
                            P.op("dve", "tensor_tensor", r=[gk, bkey(bb)], w=[("aT", g)], out=aT[:, j, cs], in0=gel[:, e_, 0:n],
                                 in1=banks[bb][:, 0:n], op=ALU.mult)
                        bko = nextbank(4, 6)
                        P.op("pe", "transpose", r=["gbuf", "ident_f"], w=[bkey(bko)], out=banks[bko][0:2, 0:128],
                             in_=gbuf[:, T:T + 2], identity=ident_f[:])
                        P.op("pe", "transpose", r=["gss", "ident_f"], w=[bkey(bko)], out=banks[bko][0:NS, 128:256],
                             in_=gss[:, 2, :], identity=ident_f[:])
                        P.op("dve", "tensor_copy", r=[bkey(bko)], w=["cvp"], out=cvp[:, j * 128:(j + 1) * 128], in_=banks[bko][0:2, 0:128])
                        P.op("dve", "tensor_copy", r=[bkey(bko)], w=["cvs"], out=cvs[:, 1, j * 128:(j + 1) * 128], in_=banks[bko][0:NS, 128:256])
                    P.dma(conv_p[l][:, f0:f0 + ncols], cvp[:, 0:ncols], r=["cvp"])
                    P.dma(cso_v[:, :, f0:f0 + ncols], cvs[:, :, 0:ncols], r=["cvs"])
                    wd, wdk = load_w(w_down[l][f0:f0 + ncols, :], nk=nf, ncols=1024)
                    for dc in range(8):
                        for g in range(5):
                            cs = gcolsl(g)
                            n = cs.stop - cs.start
                            bk = nextbank(0, 4)
                            for j in range(nf):
                                P.op("pe", "matmul", r=[("aT", g), wdk], w=[bkey(bk)], out=banks[bk][:, 0:n],
                                     lhsT=wd[:, j, dc * 128:(dc + 1) * 128], rhs=aT[:, j, cs], start=(j == 0), stop=(j == nf - 1), inc=(j == nf - 1))
                            P.op("dve", "tensor_tensor", r=[bkey(bk), ("xT", g)], w=[("xT", g)], out=xT[:, dc, cs],
                                 in0=xT[:, dc, cs], in1=banks[bk][:, 0:n], op=ALU.add)
                P.barrier()

        rmsnorm(0)
        if STAGE >= 1:
            with ExitStack() as ph0:
                qs_tok = sb("qs_tok0", [NS, 512], F32, ph0)
                ks_tok = sb("ks_tok0", [NS, 512], F32, ph0)
                vs_tok = sb("vs_tok0", [NS, 512], F32, ph0)
                phA = ExitStack()
                QT = sb("QT0", [128, 4, T], BF16, phA)
                KT = sb("KT0", [128, 4, T], BF16, phA)
                Vt = sb("Vt0", [128, 16, 512], BF16, phA)
                with ExitStack() as ph:
                    walloc(ph, 2)
                    with ExitStack() as phq:
                        qkv_proj(phq, w_in0, 0, kv0_p, kv0_s, QT, KT, Vt, qs_tok, ks_tok, vs_tok)
                        P.barrier()
                    if STAGE >= 2:
                        sgw_f = sb("sgw_f", [128, 4, 128], F32, ph)
                        sgw_b = sb("sgw_b", [128, 4, 128], BF16, ph)
                        gain_t = sb("gain_t", [128, 512], F32, ph)
                        bcol_t = sb("bcol_t", [128, 4], F32, ph)
                        ssw = sb("ssw", [NS, 512], F32, ph)
                        ssb = sb("ssb", [NS, 512], F32, ph)
                        gst = sb("gst", [128, 2, 512], F32, ph)
                        gbf = sb("gbf", [128, 2, 512], BF16, ph)
                        ubf = sb("ubf", [128, 2, 512], BF16, ph)
                        fbt = sb("fbt", [128, 2, 512], BF16, ph)
                        bst = sb("bst", [128, 2, 8], F32, ph)
                        P.dma(sgw_f[:], sgu_wT[:, :, :], w=["sgw_f"])
                        P.op("pool", "affine_select", r=["sgw_f"], w=["sgw_b"], out=sgw_b[:], in_=sgw_f[:], pattern=[[0, 4], [1, 128]],
                             compare_op=ALU.is_ge, fill=0.0, base=0, channel_multiplier=-1)
                        P.dma(gain_t[:], sgu_gain_bc[:, :], w=["gain"])
                        P.dma(bcol_t[:], sgu_bcol[:, :], w=["bcol"])
                        P.dma(ssw[:], sgu_s_w[:, :], w=["ssw"])
                        P.dma(ssb[:], sgu_s_b[:, :], w=["ssw"])
                        wv_, wvk = load_w(w_in0[:, 2048:2560])
                        wu_, wuk = load_w(w_in0[:, 1536:2048])
                        for t in range(17):
                            rows = trows(t)
                            s = t % 2
                            bk = nextbank(0, 4)
                            for kc in range(8):
                                P.op("pe", "matmul", r=[tkey(t), wvk], w=[bkey(bk)], out=banks[bk][0:rows, :],
                                     lhsT=hT[:, kc, tcols(t)], rhs=wv_[:, kc, :], start=(kc == 0), stop=(kc == 7), inc=(kc == 7))
                            bu_ = nextbank(0, 4)
                            for kc in range(8):
                                P.op("pe", "matmul", r=[tkey(t), wuk], w=[bkey(bu_)], out=banks[bu_][0:rows, :],
                                     lhsT=hT[:, kc, tcols(t)], rhs=wu_[:, kc, :], start=(kc == 0), stop=(kc == 7), inc=(kc == 7))
                            gk = ("gst", s)
                            P.op("act", "activation", r=[bkey(bk)], w=[gk], out=gst[0:rows, s, :], in_=banks[bk][0:rows, :], func=AF.Gelu_apprx_tanh)
                            P.op("act", "activation", r=[bkey(bu_)], w=[("ubf", s)], out=ubf[0:rows, s, :], in_=banks[bu_][0:rows, :], func=AF.Gelu_apprx_tanh)
                            P.op("dve", "bn_stats", r=[gk], w=[("bst", s)], out=bst[0:rows, s, 0:6], in_=gst[0:rows, s, :])
                            P.op("dve", "bn_aggr", r=[("bst", s)], w=[("bst", s)], out=bst[0:rows, s, 6:8], in_=bst[0:rows, s, 0:6])
                            P.op("act", "activation", r=[("bst", s), "eps"], w=[("bst", s)], out=bst[0:rows, s, 7:8], in_=bst[0:rows, s, 7:8],
                                 func=AF.Sqrt, bias=eps_t[0:rows, 0:1])
                            P.op("dve", "reciprocal", r=[("bst", s)], w=[("bst", s)], out=bst[0:rows, s, 7:8], in_=bst[0:rows, s, 7:8])
                            P.op("dve", "tensor_scalar", r=[gk, ("bst", s)], w=[gk], out=gst[0:rows, s, :], in0=gst[0:rows, s, :],
                                 scalar1=bst[0:rows, s, 6:7], scalar2=bst[0:rows, s, 7:8], op0=ALU.subtract, op1=ALU.mult)
                            P.op("dve", "tensor_tensor", r=[gk, "gain"], w=[gk], out=gst[0:rows, s, :], in0=gst[0:rows, s, :],
                                 in1=gain_t[0:rows, :], op=ALU.mult)
                            fk = ("fbt", s)
                            if t == 16:
                                P.dma(sgu_s[:, :], gst[0:rows, s, :], r=[gk])
                                P.op("dve", "tensor_tensor", r=[gk, "ssw"], w=[gk], out=gst[0:rows, s, :], in0=gst[0:rows, s, :], in1=ssw[:], op=ALU.mult)
                                P.op("dve", "tensor_tensor", r=[gk, "ssw"], w=[gk], out=gst[0:rows, s, :], in0=gst[0:rows, s, :], in1=ssb[:], op=ALU.add)
                                P.op("dve", "tensor_tensor", r=[gk, ("ubf", s)], w=[fk], out=fbt[0:rows, s, :], in0=gst[0:rows, s, :],
                                     in1=ubf[0:rows, s, :], op=ALU.mult)
                            else:
                                P.op("pool", "tensor_copy", r=[gk], w=[("gbf", s)], out=gbf[:, s, :], in_=gst[:, s, :])
                                bf_ = nextbank(4, 6)
                                for g4 in range(4):
                                    P.op("pe", "matmul", r=[("gbf", s), "sgw_b"], w=[bkey(bf_)], out=banks[bf_][:, g4 * 128:(g4 + 1) * 128],
                                         lhsT=sgw_b[:, g4, :], rhs=gbf[:, s, g4 * 128:(g4 + 1) * 128], start=True, stop=True)
                                for g4 in range(4):
                                    P.op("dve", "scalar_tensor_tensor", r=[bkey(bf_), "bcol", ("ubf", s)], w=[fk],
                                         out=fbt[:, s, g4 * 128:(g4 + 1) * 128], in0=banks[bf_][:, g4 * 128:(g4 + 1) * 128],
                                         scalar=bcol_t[:, g4:g4 + 1], in1=ubf[:, s, g4 * 128:(g4 + 1) * 128], op0=ALU.add, op1=ALU.mult)
                            transpose_to_hT(fbt[0:rows, s, :], fk, rows, 4, t)
                    P.barrier()
                if STAGE >= 3:
                    with ExitStack() as ph:
                        attention(ph, 0, QT, KT, Vt)
                        P.barrier()
                P.barrier()
                phA.close()
                if STAGE >= 3 and SAMPLE:
                    with ExitStack() as ph:
                        sample_attention(ph, 0, cache0, qs_tok, kv0_s, qscr0)
                        P.barrier()
            if STAGE >= 4:
                with ExitStack() as ph:
                    walloc(ph, 2)
                    out_proj(w_out0)
                    P.barrier()
        if STAGE >= 5:
            rmsnorm(1)
            ffn(0)
        if STAGE >= 6:
            rmsnorm(2)
            with ExitStack() as ph:
                walloc(ph, 2)
                uT = sb("uT", [128, 4, 15 + T], F32, ph)
                pa = sb("pa", [128, 15 + T], F32, ph)
                pb = sb("pb", [128, 15 + T], F32, ph)
                pT_ = sb("pT_", [128, T], BF16, ph)
                dst_ = sb("dst_", [128, TT], BF16, ph)
                plw_f = sb("plw_f", [128, 4, 128], F32, ph)
                plw_b = sb("plw_b", [128, 4, 128], BF16, ph)
                pscol = sb("pscol", [128, 4], F32, ph)
                pfix = sb("pfix", [128, 64], F32, ph)
                us_tok = sb("us_tok", [NS, 512], F32, ph)
                pst = sb("pst", [NS, 15, 128], F32, ph)
                pls = sb("pls", [NS, 128], F32, ph)
                plsb = sb("plsb", [128, NS], BF16, ph)
                ppo = sb("ppo", [15, 512], F32, ph)
                P.dma(plw_f[:], pool_w[:, :, :], w=["plw_f"])
                P.op("pool", "tensor_copy", r=["plw_f"], w=["plw_b"], out=plw_b[:], in_=plw_f[:])
                P.dma(pscol[:], pool_scol[:, :], w=["pscol"])
                P.dma(pfix[:], pool_fix[:, :], w=["pfix"])
                P.op("pool", "memset", w=["uT"], ap=uT[:, :, 0:15], constant=0.0)
                P.op("pool", "memset", w=["pa"], ap=pa[:, 0:15], constant=0.0)
                P.op("pool", "memset", w=["pb"], ap=pb[:, 0:15], constant=0.0)
                wv_, wvk = load_w(w_in1[:, 1536:2048])
                bk = nextbank(0, 4)
                for kc in range(8):
                    P.op("pe", "matmul", r=[tkey(16), wvk], w=[bkey(bk)], out=banks[bk][0:NS, :], lhsT=hT[:, kc, T:TT], rhs=wv_[:, kc, :],
                         start=(kc == 0), stop=(kc == 7), inc=(kc == 7))
                P.op("act", "activation", r=[bkey(bk)], w=["us_tok"], out=us_tok[:], in_=banks[bk][0:NS, :], func=AF.Copy)
                P.dma(pool_s.rearrange("s (r c) -> s r c", r=15)[:, 14, :], us_tok[:], r=["us_tok"])
                pst_v = pool_st.rearrange("s (r c) -> s r c", r=15)
                pso_v = pool_s.rearrange("s (r c) -> s r c", r=15)
                for j in range(4):
                    w_ = (2, 4, 8, 16)[j]
                    for g in range(4):
                        bk = nextbank(0, 4)
                        for kc in range(8):
                            P.op("pe", "matmul", r=hkeys(g) + [wvk], w=[bkey(bk)], out=banks[bk][:, :], lhsT=wv_[:, kc, j * 128:(j + 1) * 128],
                                 rhs=hT[:, kc, gcolsl(g)], start=(kc == 0), stop=(kc == 7), inc=(kc == 7))
                        copy(evac_eng(), uT[:, j, 15 + 512 * g:15 + 512 * g + 512], banks[bk][:, :], r=[bkey(bk)], w=["uT"])
                    bko = nextbank(4, 6)
                    P.op("pe", "transpose", r=["uT", "ident_f"], w=[bkey(bko)], out=banks[bko][0:15, 0:128], in_=uT[:, j, T:T + 15],
                         identity=ident_f[:])
                    P.op("dve", "tensor_copy", r=[bkey(bko)], w=["ppo"], out=ppo[:, j * 128:(j + 1) * 128], in_=banks[bko][0:15, 0:128])
                    cur, ck_ = uT[:, j, :], "uT"
                    bufs = [(pa, "pa"), (pb, "pb")]
                    for st_ in range(j + 1):
                        sh = 1 << st_
                        nx, nk2 = bufs[st_ % 2]
                        P.op("dve", "tensor_tensor", r=[ck_], w=[nk2], out=nx[:, 15:15 + T], in0=cur[:, 15:15 + T], in1=cur[:, 15 - sh:15 + T - sh], op=ALU.add)
                        cur, ck_ = nx, nk2
                    oth, ok_ = bufs[(j + 1) % 2]
                    P.op("dve", "scalar_tensor_tensor", r=[ck_, "uT"], w=[ok_], out=oth[:, 15:15 + T], in0=cur[:, 15:15 + T], scalar=1.0 / w_,
                         in1=uT[:, j, 15:15 + T], op0=ALU.mult, op1=ALU.subtract)
                    P.op("dve", "tensor_tensor", r=[ck_, "pfix"], w=[ck_], out=cur[:, 15:31], in0=cur[:, 15:31], in1=pfix[:, j * 16:(j + 1) * 16], op=ALU.mult)
                    P.op("dve", "tensor_tensor", r=[ck_, "uT"], w=[ok_], out=oth[:, 15:31], in0=cur[:, 15:31], in1=uT[:, j, 15:31], op=ALU.subtract)
                    P.op("act", "activation", r=[ok_], w=["pT_"], out=pT_[:], in_=oth[:, 15:15 + T], func=AF.Copy)
                    for g in range(4):
                        bk = nextbank(0, 4)
                        P.op("pe", "matmul", r=["pT_", "plw_b"], w=[bkey(bk)], out=banks[bk][:, :], lhsT=plw_b[:, j, :], rhs=pT_[:, gcolsl(g)],
                             start=True, stop=True)
                        P.op("act", "activation", r=[bkey(bk), "pscol"], w=["dst_"], out=dst_[:, gcolsl(g)], in_=banks[bk][:, :], func=AF.Copy,
                             scale=pscol[:, j:j + 1])
                    P.dma(pst[:], pst_v[:, :, j * 128:(j + 1) * 128], w=["pst"])
                    P.dma(pso_v[:, 0:14, j * 128:(j + 1) * 128], pst[:, 1:15, :], r=["pst"])
                    P.op("dve", "tensor_reduce", r=["pst"], w=["pls"], out=pls[:], in_=pst[:, 16 - w_:15, :].rearrange("p r c -> p c r"),
                         axis=AX.X, op=ALU.add)
                    P.op("dve", "tensor_tensor", r=["pls", "us_tok"], w=["pls"], out=pls[:], in0=pls[:], in1=us_tok[:, j * 128:(j + 1) * 128], op=ALU.add)
                    P.op("dve", "scalar_tensor_tensor", r=["pls", "us_tok"], w=["pls"], out=pls[:], in0=pls[:], scalar=1.0 / w_,
                         in1=us_tok[:, j * 128:(j + 1) * 128], op0=ALU.mult, op1=ALU.subtract)
                    bko = nextbank(4, 6)
                    P.op("pe", "transpose", r=["pls", "ident_f"], w=[bkey(bko)], out=banks[bko][:, 0:NS], in_=pls[:], identity=ident_f[0:NS, 0:NS])
                    P.op("act", "activation", r=[bkey(bko)], w=["plsb"], out=plsb[:], in_=banks[bko][:, 0:NS], func=AF.Copy)
                    bk = nextbank(0, 4)
                    P.op("pe", "matmul", r=["plsb", "plw_b"], w=[bkey(bk)], out=banks[bk][:, 0:NS], lhsT=plw_b[:, j, :], rhs=plsb[:], start=True, stop=True)
                    P.op("act", "activation", r=[bkey(bk), "pscol"], w=["dst_"], out=dst_[:, T:TT], in_=banks[bk][:, 0:NS], func=AF.Copy,
                         scale=pscol[:, j:j + 1])
                    P.dma(dscr[:, j, :], dst_[:], r=["dst_"], w=["dscr"])
                P.dma(pool_p[:, :], ppo[:], r=["ppo"])
                P.barrier()
            with ExitStack() as ph1:
                qs1 = sb("qs_tok1", [NS, 512], F32, ph1)
                ks1 = sb("ks_tok1", [NS, 512], F32, ph1)
                vs1 = sb("vs_tok1", [NS, 512], F32, ph1)
                phB = ExitStack()
                QT = sb("QT1", [128, 4, T], BF16, phB)
                KT = sb("KT1", [128, 4, T], BF16, phB)
                Vt = sb("Vt1", [128, 16, 512], BF16, phB)
                with ExitStack() as ph:
                    walloc(ph, 2)
                    qkv_proj(ph, w_in1, 1, kv1_p, kv1_s, QT, KT, Vt, qs1, ks1, vs1)
                    P.barrier()
                P.dma(hT[:, 4:8, :], dscr[:, :, :], r=["dscr"], w=[tkey(t) for t in range(17)])
                if STAGE >= 7:
                    with ExitStack() as ph:
                        attention(ph, 1, QT, KT, Vt)
                        P.barrier()
                P.barrier()
                phB.close()
                if STAGE >= 7 and SAMPLE:
                    with ExitStack() as ph:
                        sample_attention(ph, 1, cache1, qs1, kv1_s, qscr1)
                        P.barrier()
            if STAGE >= 8:
                with ExitStack() as ph:
                    walloc(ph, 2)
                    out_proj(w_out1)
                    P.barrier()
        if STAGE >= 9:
            rmsnorm(3)
            ffn(1)
        if STAGE >= 10:
            with ExitStack() as ph:
                yT = sb("yT", [128, 8, 512], F32, ph)
                ytok = sb("ytok", [128, 2, D], F32, ph)
                for g in range(5):
                    ri, n, cs = rmsnorm_stats(g)
                    for c in range(8):
                        P.op("dve", "scalar_tensor_tensor", r=[("xT", g), ("rs", ri), "gcol"], w=["yT"],
                             out=yT[:, c, 0:n], in0=xT[:, c, cs], scalar=gcol_t[:, 4 * 8 + c:4 * 8 + c + 1],
                             in1=rs_t[:, ri, 0:n], op0=ALU.mult, op1=ALU.mult)
                    for tt, t in enumerate(gtiles(g)):
                        rows = trows(t)
                        s = t % 2
                        for half in range(2):
                            bk = nextbank(0, 4)
                            for j in range(4):
                                c = half * 4 + j
                                P.op("pe", "transpose", r=["yT", "ident_f"], w=[bkey(bk)], out=banks[bk][0:rows, j * 128:(j + 1) * 128],
                                     in_=yT[:, c, tt * 128:tt * 128 + rows], identity=ident_f[:])
                            copy(evac_eng(), ytok[0:rows, s, half * 512:(half + 1) * 512], banks[bk][0:rows, :], r=[bkey(bk)], w=[("ytok", s)])
                        dst = y_p[t * 128:(t + 1) * 128, :] if t < 16 else y_s[:, :]
                        P.dma(dst, ytok[0:rows, s, :], r=[("ytok", s)])
                P.barrier()

        P.finish()
    print("instructions:", P.nins, {k: v for k, v in P.cnt.items()})
    return nc


def _prep_inputs(inputs):
    f32 = np.float32
    g = lambda k: np.asarray(inputs[k])
    common = {}
    common["cache0"] = np.ascontiguousarray(g("cache_l0_kv"), dtype=f32).reshape(-1, 128, 1024)
    common["cache1"] = np.ascontiguousarray(g("cache_l1_kv"), dtype=f32).reshape(-1, 128, 1024)
    common["w_in0"] = np.ascontiguousarray(g("l0_w_in"), dtype=f32)
    common["w_out0"] = np.ascontiguousarray(g("l0_w_out"), dtype=f32)
    common["w_in1"] = np.ascontiguousarray(g("l1_w_in"), dtype=f32)
    common["w_out1"] = np.ascontiguousarray(g("l1_w_out"), dtype=f32)
    common["w_gate"] = np.ascontiguousarray(g("ffn_w_gate"), dtype=f32)
    common["w_up"] = np.ascontiguousarray(g("ffn_w_up"), dtype=f32)
    common["w_down"] = np.ascontiguousarray(g("ffn_w_down"), dtype=f32)
    common["sgu_wT"] = np.ascontiguousarray(g("l0_sgu_w").transpose(2, 0, 1), dtype=f32)
    common["pool_w"] = np.ascontiguousarray(g("l1_pool_w").transpose(1, 0, 2), dtype=f32)
    norms = np.stack([g("l0_norm"), g("ffn_norm")[0], g("l1_norm"), g("ffn_norm")[1], g("final_norm")], 0)
    common["gcols"] = np.ascontiguousarray(norms.reshape(5, 8, 128).transpose(2, 0, 1).reshape(128, 40), dtype=f32)
    cw = g("ffn_conv_w")
    cb = g("ffn_conv_b")
    cc = np.concatenate([cw, cb[:, None, :]], axis=1)
    common["convc"] = np.ascontiguousarray(cc.reshape(2, 4, NFC, 128).transpose(3, 0, 1, 2).reshape(128, 2 * 4 * NFC), dtype=f32)
    common["sgu_gain_bc"] = np.ascontiguousarray(np.broadcast_to(g("l0_sgu_gain")[None, :], (128, 512)), dtype=f32)
    common["sgu_bcol"] = np.ascontiguousarray(g("l0_sgu_b").T, dtype=f32)
    common["sgu_s_w"] = np.ascontiguousarray(np.broadcast_to(np.repeat(g("l0_sgu_w")[:, 0, 0], 128)[None, :], (NS, 512)), dtype=f32)
    common["sgu_s_b"] = np.ascontiguousarray(np.broadcast_to(np.repeat(g("l0_sgu_b")[:, 0], 128)[None, :], (NS, 512)), dtype=f32)
    common["pool_scol"] = np.ascontiguousarray(g("l1_pool_scale").reshape(4, 128).T, dtype=f32)
    fix = np.ones((4, 16), f32)
    for gi, w in enumerate((2, 4, 8, 16)):
        for t in range(16):
            fix[gi, t] = 1.0 / min(t + 1, w)
    common["pool_fix"] = np.ascontiguousarray(np.broadcast_to(fix.reshape(1, 64), (128, 64)), dtype=f32)
    half = 8
    inv_freq = (np.float32(500000.0) ** (-(np.arange(half, dtype=f32) * np.float32(2.0) / np.float32(16)))).astype(f32)
    pos = np.concatenate([np.arange(T), np.full(128, T)]).astype(f32)
    ang = (pos[:, None] * inv_freq[None, :]).astype(f32)
    common["ropec"] = np.ascontiguousarray(np.cos(ang).astype(f32).reshape(17, 128, 8).transpose(1, 0, 2))
    common["ropes"] = np.ascontiguousarray(np.sin(ang).astype(f32).reshape(17, 128, 8).transpose(1, 0, 2))
    maps = []
    xp = g("x_prompt")
    xs = g("x_sample")
    pt = g("page_table")
    ps_ = g("state_l1_pool")
    cs_ = g("state_ffn_conv")
    for c in range(8):
        m = dict(common)
        sl = slice(NS * c, NS * (c + 1))
        m["x_p"] = np.ascontiguousarray(xp[c], dtype=f32)
        m["x_s"] = np.ascontiguousarray(xs[sl, 0, :], dtype=f32)
        m["ptab"] = np.ascontiguousarray(pt[sl], dtype=np.int32)
        m["pool_st"] = np.ascontiguousarray(ps_[sl].reshape(NS, 15 * 512), dtype=f32)
        m["conv_st"] = np.ascontiguousarray(cs_[:, sl].reshape(2, NS, 2 * DFF), dtype=f32)
        maps.append(m)
    return maps


_NC = None


def kernel(**inputs):
    global _NC
    maps = _prep_inputs(inputs)
    if _NC is None:
        _NC = build(maps[0]["cache0"].shape[0])
    res = run_bass_kernel_spmd(_NC, maps, core_ids=list(range(8)))
    R = res.results
    f32 = np.float32
    cat = lambda k: np.stack([np.asarray(r[k], dtype=f32) for r in R], 0)
    y_p = cat("y_p")
    y_s = cat("y_s").reshape(128, 1, D)
    kv0_p = cat("kv0_p").reshape(8, T, 2, 8, 64)
    kv0_s = cat("kv0_s").reshape(128, 1, 2, 8, 64)
    sgu_s = cat("sgu_s").reshape(128, 1, 512)
    kv1_p = cat("kv1_p").reshape(8, T, 2, 8, 64)
    kv1_s = cat("kv1_s").reshape(128, 1, 2, 8, 64)
    pool_p = cat("pool_p")
    pool_s = cat("pool_s").reshape(128, 15, 512)
    conv_p = cat("conv_p").transpose(1, 0, 2, 3)
    conv_s = cat("conv_s").reshape(8, 2, NS, 2, DFF).transpose(1, 0, 2, 3, 4).reshape(2, 128, 2, DFF)
    return (y_p, y_s, kv0_p, kv0_s, sgu_s, kv1_p, kv1_s, pool_p, pool_s,
            np.ascontiguousarray(conv_p), np.ascontiguousarray(conv_s))
```
